# Optimizing a Trainium2 kernel written in Bass

```python
import math
import jax, jax.numpy as jnp
from jax import lax
import numpy as np

D_MODEL = 2048
BATCH = 2
SEQ = 4096
DEPTH = 2
DEC_BATCH = 8
DEC_SEQ = 4
PAST_LEN = 16384
PAGE_SIZE = 128

HEAD_DIM = 128
ATTN_WIDTH = D_MODEL // 2
ATTN_HEADS = ATTN_WIDTH // HEAD_DIM
CONV_CH = D_MODEL - ATTN_WIDTH
CONV_K = 3
CONV_HIST = CONV_K - 1
IN_WIDTH = 3 * ATTN_WIDTH + 3 * CONV_CH
MOBA_BLOCK = 256
MOBA_TOPK = 3
QUERY_CHUNK = 16
ROPE_THETA = 10000.0
POOL_WINDOWS = (2, 4, 8, 16)
POOL_GROUP = D_MODEL // len(POOL_WINDOWS)
POOL_HIST = max(POOL_WINDOWS) - 1
D_FF = -(-(8 * D_MODEL) // (3 * 256)) * 256
N_EVEN = (DEPTH + 1) // 2
N_ODD = DEPTH // 2
RMS_EPS = 1e-6
NEG_INF = -1e30

kernel_name = 'moba_shortconv_pool_hybrid_step'


def rmsnorm(x, g):
    xf = x.astype(jnp.float32)
    y = xf * lax.rsqrt(jnp.mean(xf * xf, axis=-1, keepdims=True) + RMS_EPS)
    return (y * g.astype(jnp.float32)).astype(x.dtype)


def rope(x, pos):
    half = HEAD_DIM // 2
    inv = ROPE_THETA ** (-jnp.arange(half, dtype=jnp.float32) / half)
    ang = pos.astype(jnp.float32)[:, None] * inv[None, :]
    cos = jnp.cos(ang)[None, :, None, :]
    sin = jnp.sin(ang)[None, :, None, :]
    xf = x.astype(jnp.float32)
    x1, x2 = xf[..., :half], xf[..., half:]
    return jnp.concatenate([x1 * cos - x2 * sin, x2 * cos + x1 * sin], axis=-1).astype(x.dtype)


def moba_attention(q, k, v, q_pos):
    B, Tq, H, Dh = q.shape
    L = k.shape[1]
    nb = -(-L // MOBA_BLOCK)
    padw = ((0, 0), (0, nb * MOBA_BLOCK - L), (0, 0), (0, 0))
    kb = jnp.pad(k, padw).reshape(B, nb, MOBA_BLOCK, H, Dh)
    vb = jnp.pad(v, padw).reshape(B, nb, MOBA_BLOCK, H, Dh)
    k_mean = jnp.mean(kb.astype(jnp.float32), axis=2)
    n_sel = min(MOBA_TOPK, nb)
    n_slots = n_sel + 1
    scale = HEAD_DIM ** -0.5
    b_idx = jnp.arange(B)[:, None, None, None]
    h_idx = jnp.arange(H)[None, :, None, None]
    blk = jnp.arange(nb)
    off = jnp.arange(MOBA_BLOCK)

    def attend_chunk(args):
        qc, pc = args
        qn = pc.shape[0]
        own = pc // MOBA_BLOCK
        gate = jnp.einsum('bqhd,bnhd->bhqn', qc.astype(jnp.float32), k_mean)
        gate = jnp.where(blk[None, :] < own[:, None], gate, NEG_INF)
        _, top = lax.top_k(gate, n_sel)
        own_b = jnp.broadcast_to(own[None, None, :, None], (B, H, qn, 1))
        idx = jnp.concatenate([top.astype(jnp.int32), own_b.astype(jnp.int32)], axis=-1)
        slot_ok = jnp.concatenate([jnp.arange(n_sel)[None, :] < own[:, None],
                                   jnp.ones((qn, 1), dtype=bool)], axis=-1)
        kg = kb[b_idx, idx, :, h_idx, :]
        vg = vb[b_idx, idx, :, h_idx, :]
        key_pos = idx[..., None] * MOBA_BLOCK + off
        mask = slot_ok[None, None, :, :, None] & (key_pos <= pc[None, None, :, None, None])
        logits = jnp.einsum('bqhd,bhqsjd->bhqsj', qc, kg).astype(jnp.float32) * scale
        logits = jnp.where(mask, logits, NEG_INF)
        p = jax.nn.softmax(logits.reshape(B, H, qn, n_slots * MOBA_BLOCK), axis=-1)
        p = p.reshape(B, H, qn, n_slots, MOBA_BLOCK).astype(vg.dtype)
        return jnp.einsum('bhqsj,bhqsjd->bqhd', p, vg)

    qc_size = math.gcd(Tq, QUERY_CHUNK)
    nc = Tq // qc_size
    q_chunks = q.reshape(B, nc, qc_size, H, Dh).transpose(1, 0, 2, 3, 4)
    p_chunks = q_pos.reshape(nc, qc_size)
    out = lax.map(attend_chunk, (q_chunks, p_chunks))
    return out.transpose(1, 0, 2, 3, 4).reshape(B, Tq, H * Dh)


def even_mixer(h, pos, k_past, v_past, conv_hist, w_in, conv_w, w_o):
    B, T, _ = h.shape
    proj = h @ w_in
    cuts = [ATTN_WIDTH, 2 * ATTN_WIDTH, 3 * ATTN_WIDTH,
            3 * ATTN_WIDTH + CONV_CH, 3 * ATTN_WIDTH + 2 * CONV_CH]
    q, k, v, gate_b, gate_c, x_conv = jnp.split(proj, cuts, axis=-1)
    q = rope(q.reshape(B, T, ATTN_HEADS, HEAD_DIM), pos)
    k = rope(k.reshape(B, T, ATTN_HEADS, HEAD_DIM), pos)
    v = v.reshape(B, T, ATTN_HEADS, HEAD_DIM)
    k_all = jnp.concatenate([k_past.astype(k.dtype), k], axis=1)
    v_all = jnp.concatenate([v_past.astype(v.dtype), v], axis=1)
    attn = moba_attention(q, k_all, v_all, pos)
    u = gate_c * x_conv
    ext = jnp.concatenate([conv_hist.astype(u.dtype), u], axis=1)
    conv = ext[:, 0:T] * conv_w[0]
    for j in range(1, CONV_K):
        conv = conv + ext[:, j:j + T] * conv_w[j]
    y_conv = gate_b * conv
    out = jnp.concatenate([attn, y_conv], axis=-1) @ w_o
    return out, k, v, ext[:, -CONV_HIST:]


def odd_mixer(h, pos, pool_hist, w_pool, pool_scale):
    B, T, D = h.shape
    ext = jnp.concatenate([pool_hist.astype(h.dtype), h], axis=1)
    cs = jnp.cumsum(ext.astype(jnp.float32), axis=1)
    cs = jnp.concatenate([jnp.zeros((B, 1, D), jnp.float32), cs], axis=1)
    end = cs[:, POOL_HIST + 1:]
    hf = h.astype(jnp.float32)
    outs = []
    for g, w in enumerate(POOL_WINDOWS):
        sl = slice(g * POOL_GROUP, (g + 1) * POOL_GROUP)
        start = cs[:, POOL_HIST + 1 - w:POOL_HIST + 1 - w + T, sl]
        cnt = jnp.minimum(pos + 1, w).astype(jnp.float32)[None, :, None]
        d = (end[..., sl] - start) / cnt - hf[..., sl]
        outs.append(d.astype(h.dtype) @ w_pool[g])
    out = jnp.concatenate(outs, axis=-1) * pool_scale
    return out, ext[:, -POOL_HIST:]


def swiglu(h, w_gate, w_up, w_down):
    return (jax.nn.silu(h @ w_gate) * (h @ w_up)) @ w_down


def trunk(x, pos, past_kv, conv_hist, pool_hist, norm_mix, norm_ffn, norm_final,
          w_in, conv_w, w_o, w_pool, pool_scale, w_gate, w_up, w_down):
    new_k, new_v, new_conv, new_pool = [], [], [], []
    for layer in range(DEPTH):
        h = rmsnorm(x, norm_mix[layer])
        if layer % 2 == 0:
            e = layer // 2
            k_past, v_past = past_kv(e)
            mix, k, v, conv_state = even_mixer(h, pos, k_past, v_past, conv_hist[e],
                                               w_in[e], conv_w[e], w_o[e])
            new_k.append(k)
            new_v.append(v)
            new_conv.append(conv_state)
        else:
            o = layer // 2
            mix, pool_state = odd_mixer(h, pos, pool_hist[o], w_pool[o], pool_scale[o])
            new_pool.append(pool_state)
        x = x + mix
        x = x + swiglu(rmsnorm(x, norm_ffn[layer]), w_gate[layer], w_up[layer], w_down[layer])
    y = rmsnorm(x, norm_final)
    return y, jnp.stack(new_k), jnp.stack(new_v), jnp.stack(new_conv), jnp.stack(new_pool)


def _normal(key, shape, scale):
    return jax.random.normal(key, shape, jnp.float32) * scale


def setup_inputs(seed: int = 0) -> dict:
    key = jax.random.key(seed)
    ks = jax.random.split(key, 18)
    n_pages = PAST_LEN // PAGE_SIZE
    n_pool = (5 * DEC_BATCH * n_pages + 3) // 4
    perm = jax.random.permutation(ks[4], n_pool)
    page_table = perm[:DEC_BATCH * n_pages].reshape(DEC_BATCH, n_pages).astype(jnp.int32)
    return {
        'x_prompt': _normal(ks[0], (BATCH, SEQ, D_MODEL), 1.0),
        'x_sample': _normal(ks[1], (DEC_BATCH, DEC_SEQ, D_MODEL), 1.0),
        'cache_k': _normal(ks[2], (N_EVEN, n_pool, PAGE_SIZE, ATTN_HEADS, HEAD_DIM), 1.0),
        'cache_v': _normal(ks[3], (N_EVEN, n_pool, PAGE_SIZE, ATTN_HEADS, HEAD_DIM), 1.0),
        'page_table': page_table,
        'state_conv': _normal(ks[5], (N_EVEN, DEC_BATCH, CONV_HIST, CONV_CH), 1.0),
        'state_pool': _normal(ks[6], (N_ODD, DEC_BATCH, POOL_HIST, D_MODEL), 1.0),
        'norm_mix': 1.0 + _normal(ks[7], (DEPTH, D_MODEL), 0.1),
        'norm_ffn': 1.0 + _normal(ks[8], (DEPTH, D_MODEL), 0.1),
        'norm_final': 1.0 + _normal(ks[9], (D_MODEL,), 0.1),
        'w_in': _normal(ks[10], (N_EVEN, D_MODEL, IN_WIDTH), D_MODEL ** -0.5),
        'conv_w': _normal(ks[11], (N_EVEN, CONV_K, CONV_CH), CONV_K ** -0.5),
        'w_o': _normal(ks[12], (N_EVEN, ATTN_WIDTH + CONV_CH, D_MODEL), (ATTN_WIDTH + CONV_CH) ** -0.5),
        'w_pool': _normal(ks[13], (N_ODD, len(POOL_WINDOWS), POOL_GROUP, POOL_GROUP), POOL_GROUP ** -0.5),
        'pool_scale': 1.0 + _normal(ks[14], (N_ODD, D_MODEL), 0.1),
        'w_gate': _normal(ks[15], (DEPTH, D_MODEL, D_FF), D_MODEL ** -0.5),
        'w_up': _normal(ks[16], (DEPTH, D_MODEL, D_FF), D_MODEL ** -0.5),
        'w_down': _normal(ks[17], (DEPTH, D_FF, D_MODEL), D_FF ** -0.5),
    }


def reference(x_prompt, x_sample, cache_k, cache_v, page_table, state_conv, state_pool,
              norm_mix, norm_ffn, norm_final, w_in, conv_w, w_o, w_pool, pool_scale,
              w_gate, w_up, w_down):
    n_batch, n_seq, d = x_prompt.shape
    n_dec, n_new, _ = x_sample.shape
    past_len = page_table.shape[1] * cache_k.shape[2]

    empty = jnp.zeros((n_batch, 0, ATTN_HEADS, HEAD_DIM), x_prompt.dtype)

    def prompt_past(e):
        return empty, empty

    def sample_past(e):
        k_past = cache_k[e][page_table].reshape(n_dec, past_len, ATTN_HEADS, HEAD_DIM)
        v_past = cache_v[e][page_table].reshape(n_dec, past_len, ATTN_HEADS, HEAD_DIM)
        return k_past, v_past

    pos_prompt = jnp.arange(n_seq, dtype=jnp.int32)
    pos_sample = past_len + jnp.arange(n_new, dtype=jnp.int32)
    conv_zero = jnp.zeros((N_EVEN, n_batch, CONV_HIST, CONV_CH), x_prompt.dtype)
    pool_zero = jnp.zeros((N_ODD, n_batch, POOL_HIST, d), x_prompt.dtype)

    y_prompt, k_prompt, v_prompt, conv_prompt, pool_prompt = trunk(
        x_prompt, pos_prompt, prompt_past, conv_zero, pool_zero, norm_mix, norm_ffn, norm_final,
        w_in, conv_w, w_o, w_pool, pool_scale, w_gate, w_up, w_down)
    y_sample, k_sample, v_sample, conv_sample, pool_sample = trunk(
        x_sample, pos_sample, sample_past, state_conv, state_pool, norm_mix, norm_ffn, norm_final,
        w_in, conv_w, w_o, w_pool, pool_scale, w_gate, w_up, w_down)
    return (y_prompt, y_sample, k_prompt, v_prompt, k_sample, v_sample,
            conv_prompt, conv_sample, pool_prompt, pool_sample)
```

```python
import os
import numpy as np
import ml_dtypes
import concourse.bass as bass
import concourse.mybir as mybir
from concourse.bass_utils import run_bass_kernel_spmd

F32 = mybir.dt.float32
BF16 = mybir.dt.bfloat16
I32 = mybir.dt.int32
U32 = mybir.dt.uint32
AF = mybir.ActivationFunctionType
ALU = mybir.AluOpType
AX = mybir.AxisListType

D = 2048
NCH = 16
DFF = 5632
NFT = 44
NPAST = 3072
NOWN = 1024
NS = 4
NH = 17
NPH = 15
NC_ = NS + NH
NT = NOWN + NC_
NTP = 1048
SCALE = 128 ** -0.5
NEGB = -30000.0
TOKT = [(0, 512), (512, 512), (1024, NC_)]
STAGE = int(os.environ.get("MK_STAGE", "5"))


class T:
    __slots__ = ("name", "w", "rd", "excl")

    def __init__(self, name="", excl=False):
        self.name = name
        self.w = None
        self.rd = []
        self.excl = excl


def _prune(rd):
    best = {}
    for tok in rd:
        if tok[0] in ("dma", "sw"):
            k = (tok[0], tok[1])
            if k not in best or best[k][2] < tok[2]:
                best[k] = tok
        else:
            k = tok[0].name
            if k not in best or best[k][1] < tok[1]:
                best[k] = tok
    return list(best.values())


class Eng:
    def __init__(self, ctx, name, e, sem):
        self.ctx, self.name, self.e, self.sem = ctx, name, e, sem
        self.count = 0
        self.seen = {}
        self.seen_dma = {}
        self.seen_sw = {}
        self.is_pe = name == "pe"

    def need(self, tok, war=False):
        if tok is None:
            return
        if tok[0] == "sw":
            _, s, g = tok
            if self.seen_sw.get(s, 0) >= g:
                return
            self.e.wait_ge(self.ctx.sw_sems[s], 16 * g)
            self.seen_sw[s] = g
            return
        if tok[0] == "dma":
            _, s, v = tok
            if self.seen_dma.get(s, 0) >= v:
                return
            self.e.wait_ge(self.ctx.dma_sems[s], v)
            self.seen_dma[s] = v
            return
        src, o = tok
        if src is self:
            if self.is_pe:
                return
            if self.seen.get(self.name, 0) >= o:
                return
            self.e.wait_ge(self.sem, o)
            self.seen[self.name] = o
            return
        if self.seen.get(src.name, 0) >= o:
            return
        self.e.wait_ge(src.sem, o)
        self.seen[src.name] = o

    def deps(self, reads, writes):
        for t in reads:
            self.need(t.w)
            if t.excl:
                for r in t.rd:
                    if r[0] is not self:
                        self.need(r)
        for t in writes:
            self.need(t.w)
            for r in t.rd:
                self.need(r, war=True)

    def commit(self, tok, reads, writes):
        for t in reads:
            t.rd.append(tok)
            if len(t.rd) > 16:
                t.rd = _prune(t.rd)
        for t in writes:
            t.w = tok
            t.rd = []

    def op(self, fn, reads=(), writes=()):
        self.deps(reads, writes)
        ins = fn()
        self.count += 1
        ins.then_inc(self.sem, 1)
        tok = (self, self.count)
        self.commit(tok, reads, writes)
        return tok

    def group(self, fns, reads=(), writes=()):
        self.deps(reads, writes)
        ins = None
        for fn in fns:
            ins = fn()
        self.count += 1
        ins.then_inc(self.sem, 1)
        tok = (self, self.count)
        self.commit(tok, reads, writes)
        return tok


class Ctx:
    def __init__(self, nc, n_dma_sems=40):
        self.nc = nc
        self.pe = Eng(self, "pe", nc.tensor, nc.alloc_semaphore("s_pe"))
        self.dve = Eng(self, "dve", nc.vector, nc.alloc_semaphore("s_dve"))
        self.act = Eng(self, "act", nc.scalar, nc.alloc_semaphore("s_act"))
        self.pool = Eng(self, "pool", nc.gpsimd, nc.alloc_semaphore("s_pool"))
        self.sp = Eng(self, "sp", nc.sync, nc.alloc_semaphore("s_sp"))
        self.dma_sems = [nc.alloc_semaphore(f"s_dma{i}") for i in range(n_dma_sems)]
        self.dma_val = [0] * n_dma_sems
        self.dma_next = 0
        self.sw_sems = [nc.alloc_semaphore(f"s_sw{i}") for i in range(10)]
        self.sw_gen = [0] * 10

    def swdma(self, i, out, in_, reads=(), writes=(), indirect=None, element_offset=0, **kw):
        q = self.pool
        q.deps(reads, writes)
        if self.sw_gen[i] > 0:
            q.need(("sw", i, self.sw_gen[i]))
        if indirect is not None:
            ins = q.e.indirect_dma_start(out=out, out_offset=None, in_=in_,
                                         in_offset=bass.IndirectOffsetOnAxis(ap=indirect, axis=0), element_offset=element_offset)
        else:
            ins = q.e.dma_start(out=out, in_=in_, **kw)
        self.sw_gen[i] += 1
        ins.then_inc(self.sw_sems[i], 16)
        tok = ("sw", i, self.sw_gen[i])
        q.commit(tok, reads, writes)
        return tok

    def dma(self, q, out, in_, reads=(), writes=(), **kw):
        q.deps(reads, writes)
        s = self.dma_next
        self.dma_next = (self.dma_next + 1) % len(self.dma_sems)
        if self.dma_val[s] > 0:
            q.need(("dma", s, self.dma_val[s]))
        ins = q.e.dma_start(out=out, in_=in_, **kw)
        self.dma_val[s] += 16
        ins.then_inc(self.dma_sems[s], 16)
        tok = ("dma", s, self.dma_val[s])
        q.commit(tok, reads, writes)
        return tok

    def barrier(self):
        engs = (self.pe, self.dve, self.act, self.pool, self.sp)
        for q in engs:
            for s, g in enumerate(self.sw_gen):
                if g > 0:
                    q.need(("sw", s, g))
            for s, v in enumerate(self.dma_val):
                if v > 0:
                    q.need(("dma", s, v))
            for e in engs:
                if e is not q and e.count > 0:
                    q.need((e, e.count))

    def finish(self):
        q = self.sp
        for s, v in enumerate(self.dma_val):
            if v > 0:
                q.need(("dma", s, v))
        for e in (self.pe, self.dve, self.act, self.pool):
            if e.count > 0:
                q.need((e, e.count))


def build_program():
    nc = bass.Bass("TRN2", target_bir_lowering=False)
    c = Ctx(nc)
    pe, dve, act, pool, sp = c.pe, c.dve, c.act, c.pool, c.sp
    V, S, PEn = nc.vector, nc.scalar, nc.tensor

    def din(name, shape, dt=F32):
        return nc.dram_tensor(name, list(shape), dt, kind="ExternalInput")

    def dout(name, shape, dt=F32):
        return nc.dram_tensor(name, list(shape), dt, kind="ExternalOutput")

    x_own = din("x_own", [NOWN, D]); x_c = din("x_c", [NC_, D]); x_past = din("x_past", [NPAST, D])
    if STAGE >= 5:
        cache_k = din("cache_k", [1280, 128, 8, 128]); cache_v = din("cache_v", [1280, 128, 8, 128])
    ptab = din("ptab", [1, 128], I32)
    st_conv = din("st_conv", [2, 1024]); st_pool = din("st_pool", [15, D])
    g_all = din("g_all", [128, 5, 16])
    w_in = din("w_in", [D, 6144]); conv_w = din("conv_w", [128, 8, 3]); w_o = din("w_o", [D, D])
    w_pool = din("w_pool", [4, 512, 512]); pscale = din("pscale", [128, 16])
    w_gate = din("w_gate", [2, D, DFF]); w_up = din("w_up", [2, D, DFF]); w_down = din("w_down", [2, DFF, D])
    cs_own = din("cs_own", [128, 2, NTP]); cs_past = din("cs_past", [128, 2, NPAST])
    c_ident = din("c_ident", [128, 128]); c_prot = din("c_prot", [128, 128], BF16)
    c_eall = din("c_eall", [16, 16, 128], BF16); c_cm = din("c_cm", [128, 4, 512], BF16)
    c_hb = din("c_hb", [128, 24, NH], BF16); c_addc = din("c_addc", [128, 8, 16]); c_addh = din("c_addh", [NH, 16])
    c_hv = din("c_hv", [128, NH]); c_invc = din("c_invc", [128, 4, 16])
    c_smask = din("c_smask", [4, 4]); c_esel = din("c_esel", [4, 4, 128])
    c_pio = din("c_pio", [128, 2]); c_iota64 = din("c_iota64", [128, 64])
    y_own = dout("y_own", [NOWN, D]); y_smp = dout("y_smp", [NS, D])
    k_own = dout("k_own", [NOWN, 8, 128]); v_own = dout("v_own", [NOWN, 8, 128])
    k_smp = dout("k_smp", [NS, 8, 128]); v_smp = dout("v_smp", [NS, 8, 128])
    conv_p = dout("conv_p", [2, 1024]); conv_s = dout("conv_s", [2, 1024])
    pool_p = dout("pool_p", [15, D]); pool_s = dout("pool_s", [15, D])
    kt_scr = nc.dram_tensor("kt_scr", [8, 128, 4096], BF16)
    v_scr = nc.dram_tensor("v_scr", [8, 32, 128, 128], BF16)

    sbuf_used = [0]

    def sb(name, shape, dt):
        return nc.alloc_sbuf_tensor(name, list(shape), dt)

    R1 = sb("R1", [128, NCH * NTP], F32)
    R2 = sb("R2", [128, NCH * NTP // 2], F32)
    R3 = sb("R3", [128, 11264], F32)
    xT = R1[:, :].rearrange("p (a b) -> p a b", a=NCH); t_xT = [T(f"xT{i}") for i in range(NCH)]
    hT = R2[:, :].bitcast(BF16).rearrange("p (a b) -> p a b", a=NCH); t_hT = T("hT")
    xT_scr = nc.dram_tensor("xT_scr", [128, NCH * NTP], F32)
    ident = sb("ident", [128, 128], F32); identb = sb("identb", [128, 128], BF16)
    onesb = sb("onesb", [128, 128], BF16); prot = sb("prot", [128, 128], BF16)
    eall = sb("eall", [16, 16, 128], BF16); cm = sb("cm", [128, 4, 512], BF16); hb = sb("hb", [128, 24, NH], BF16)
    addc = sb("addc", [128, 8, 16], F32); addh = sb("addh", [NH, 16], F32)
    hv = sb("hv", [128, NH], F32); invc = sb("invc", [128, 4, 16], F32)
    gall = sb("gall", [128, 5, 16], F32); convw = sb("convw", [128, 8, 3], F32); psc = sb("psc", [128, 16], F32)
    csown = sb("csown", [128, 2, NTP], F32)
    t_const = T("const")
    for dst, src in [(ident, c_ident), (prot, c_prot), (eall, c_eall), (cm, c_cm), (hb, c_hb), (addc, c_addc),
                     (addh, c_addh), (hv, c_hv), (invc, c_invc), (gall, g_all), (convw, conv_w), (psc, pscale),
                     (csown, cs_own)]:
        c.dma(sp, dst.ap(), src.ap(), writes=[t_const])
    dve.op(lambda: V.tensor_copy(identb[:, :], ident[:, :]), reads=[t_const], writes=[t_const])
    dve.op(lambda: V.memset(onesb[:, :], 1.0), writes=[t_const])
    c.barrier()

    PS = [nc.alloc_psum_tensor(f"ps{i}", [128, 512], F32) for i in range(8)]
    t_PS = [T(f"ps{i}", excl=True) for i in range(8)]

    class Arena:
        def __init__(self, base, nbytes):
            self.base = base
            self.nbytes = nbytes
            self.off = 0

        def reset(self):
            self.off = 0

        def take(self, shape, dt):
            esz = 4 if dt in (F32, I32, U32) else 2
            n = int(np.prod(shape[1:]))
            nbytes = (n * esz + 31) // 32 * 32
            assert self.off + nbytes <= self.nbytes, ("arena overflow", self.off, nbytes, self.nbytes)
            a = self.base[:, self.off // 4:(self.off + nbytes) // 4]
            self.off += nbytes
            if esz == 2:
                a = a.bitcast(BF16)[:, 0:n]
            elif dt != F32:
                a = a.bitcast(dt)[:, 0:n]
            else:
                a = a[:, 0:n]
            if len(shape) == 3:
                a = a.rearrange("p (a b) -> p a b", a=shape[1])
            if shape[0] < 128:
                a = a[0:shape[0]]
            return a

    A1 = Arena(R1, NCH * NTP * 4)
    A2 = Arena(R2, NCH * NTP * 2)
    A3 = Arena(R3, 11264 * 4)

    def take(ar, name, shape, dt):
        return ar.take(shape, dt)

    WB = [sb(f"wb{i}", [128, 16, 128], BF16) for i in range(4)]
    t_WB = [T(f"wb{i}") for i in range(4)]
    wb_next = [0]

    def wload(src_ap, nchunk):
        i = wb_next[0]
        wb_next[0] = (i + 1) % 4
        c.swdma(i, WB[i][:, 0:nchunk, :], src_ap.rearrange("(c p) n -> p c n", p=128), writes=[t_WB[i]])
        return WB[i], t_WB[i]

    evac_flip = [0]

    def evac(out_ap, in_ap, reads, writes):
        evac_flip[0] ^= 1
        if evac_flip[0]:
            return act.op(lambda: S.copy(out_ap, in_ap), reads=reads, writes=writes)
        return dve.op(lambda: V.tensor_copy(out_ap, in_ap), reads=reads, writes=writes)

    def load_transpose(src_rows_ap, nrows, dstT, t_dst, col0, stage, t_stage, psb):
        c.dma(sp, stage[0:nrows, :], src_rows_ap, writes=[t_stage])
        for q4 in range(4):
            b = psb[q4 % len(psb)]
            pe.group([(lambda k=k: PEn.transpose(PS[b][:, k * 128:k * 128 + nrows],
                                                  stage[0:nrows, (q4 * 4 + k) * 128:(q4 * 4 + k + 1) * 128],
                                                  ident[0:nrows, 0:nrows])) for k in range(4)],
                     reads=[t_stage, t_const], writes=[t_PS[b]])
            wr = t_dst[q4 * 4:q4 * 4 + 4] if isinstance(t_dst, list) else [t_dst]
            evac(dstT[:, q4 * 4:q4 * 4 + 4, col0:col0 + nrows],
                 PS[b][:, :].rearrange("p (k n) -> p k n", k=4)[:, :, 0:nrows], [t_PS[b]], wr)

    def rmsnorm_fm(srcT, t_src, cols, gidx, out_fn, tmp_sq, t_sq, rstd, t_rstd, psb):
        c0, n = cols
        rd = t_src if isinstance(t_src, list) else [t_src]
        fns = []
        for ch in range(NCH):
            k = ch % 2
            act.op(lambda ch=ch, k=k: S.activation(tmp_sq[k][:, 0:n], srcT[:, ch, c0:c0 + n], AF.Square),
                   reads=[rd[ch] if len(rd) > 1 else rd[0]], writes=[t_sq[k]])
            pe.deps([t_sq[k], t_const], [t_PS[psb]] if ch == 0 else [])
            ins = PEn.matmul(PS[psb][:, 0:n], onesb[:, :], tmp_sq[k][:, 0:n], start=(ch == 0), stop=(ch == NCH - 1))
            pe.count += 1
            ins.then_inc(pe.sem, 1)
            tok = (pe, pe.count)
            pe.commit(tok, [t_sq[k]], [t_PS[psb]] if ch == NCH - 1 else [])
            if ch != NCH - 1:
                t_PS[psb].w = tok
        act.op(lambda: S.activation(rstd[:, 0:n], PS[psb][:, 0:n], AF.Sqrt, bias=eps_ap[:, 0:1], scale=1.0 / D),
               reads=[t_PS[psb], t_const], writes=[t_rstd])
        dve.op(lambda: V.reciprocal(rstd[:, 0:n], rstd[:, 0:n]), reads=[t_rstd], writes=[t_rstd])
        for ch in range(NCH):
            out_fn(ch, rstd[:, 0:n])

    eps_ap = sb("eps", [128, 1], F32)
    dve.op(lambda: V.memset(eps_ap[:, :], 1e-6), writes=[t_const])

    def proj(wt, t_w, nchunk, rhs_fn, t_rhs, banks, tiles=TOKT):
        for (c0, n), b in zip(tiles, banks):
            pe.group([(lambda ch=ch, c0=c0, n=n, b=b: PEn.matmul(PS[b][:, 0:n], wt[:, ch, :], rhs_fn(ch, c0, n),
                                                              start=(ch == 0), stop=(ch == nchunk - 1)))
                      for ch in range(nchunk)], reads=[t_w] + list(t_rhs), writes=[t_PS[b]])

    def rope(psb, n, cos_ap, sin_ap, out_ap, t_out, tmp, t_tmp, rotb, t_cs=None):
        t_cs = t_cs or t_const
        act.op(lambda: S.copy(tmp["qb"][:, 0:n], PS[psb][:, 0:n]), reads=[t_PS[psb]], writes=[t_tmp["qb"]])
        pe.group([lambda: PEn.matmul(PS[rotb][:, 0:n], prot[:, :], tmp["qb"][:, 0:n], start=True, stop=True)],
                 reads=[t_tmp["qb"], t_const], writes=[t_PS[rotb]])
        dve.op(lambda: V.tensor_tensor(tmp["t1"][:, 0:n], PS[psb][:, 0:n], cos_ap, ALU.mult),
               reads=[t_PS[psb], t_cs], writes=[t_tmp["t1"]])
        dve.op(lambda: V.tensor_tensor(tmp["t2"][:, 0:n], PS[rotb][:, 0:n], sin_ap, ALU.mult),
               reads=[t_PS[rotb], t_cs], writes=[t_tmp["t2"]])
        dve.op(lambda: V.tensor_tensor(out_ap, tmp["t1"][:, 0:n], tmp["t2"][:, 0:n], ALU.add),
               reads=[t_tmp["t1"], t_tmp["t2"]], writes=[t_out])

    ksum = sb("ksum", [128, 8, 16], F32); t_ksum = T("ksum")
    sq2 = [sb(f"sq{i}", [128, 512], BF16) for i in range(2)]; t_sq2 = [T("sq0"), T("sq1")]
    rstd = sb("rstd", [128, 512], F32); t_rstd = T("rstd")
    rt = {"qb": sb("r_qb", [128, 512], BF16), "t1": sb("r_t1", [128, 512], F32), "t2": sb("r_t2", [128, 512], F32)}
    t_rt = {k: T(k) for k in rt}

    if STAGE >= 2:
        A1.reset(); A2.reset(); A3.reset()
        hpT = A1.take([128, NCH, 1536], BF16); t_hpT = T("hpT")
        xpT = A2.take([128, NCH, 512], F32); t_xpT = T("xpT")
        stage2 = [A3.take([128, D], F32) for i in range(2)]; t_stage2 = [T("st0"), T("st1")]
        cspast = A3.take([128, 2, 512], F32); t_csp = T("csp")
        kf = A3.take([128, 512], F32); t_kf = T("kf")
        kst = [A3.take([128, 512], BF16) for i in range(2)]; t_kst = [T("kst0"), T("kst1")]
        vst = [A3.take([128, 4, 128], BF16) for i in range(2)]; t_vst = [T("vst0"), T("vst1")]
        vtb = A3.take([128, 512], BF16); t_vtb = T("vtb")
        cnt = 0
        for grp in range(2):
            for tl in range(3):
                tok0 = grp * 1536 + tl * 512
                for sub in range(4):
                    k = cnt % 2; cnt += 1
                    load_transpose(x_past[tok0 + sub * 128: tok0 + (sub + 1) * 128, :], 128, xpT, t_xpT, sub * 128,
                                   stage2[k], t_stage2[k], [4, 5, 6, 7])

                def out_fn(ch, rs, tl=tl):
                    dve.op(lambda: V.scalar_tensor_tensor(hpT[:, ch, tl * 512:(tl + 1) * 512], xpT[:, ch, :],
                                                          gall[:, 0, ch:ch + 1], rs, ALU.mult, ALU.mult),
                           reads=[t_xpT, t_rstd, t_const], writes=[t_hpT])
                rmsnorm_fm(xpT, t_xpT, (0, 512), 0, out_fn, sq2, t_sq2, rstd, t_rstd, 3)
            for h in range(8):
                wt, t_w = wload(w_in[:, 1024 + h * 128: 1024 + (h + 1) * 128], NCH)
                for tl in range(3):
                    tok0 = grp * 1536 + tl * 512
                    b = tl % 3
                    proj(wt, t_w, NCH, lambda ch, c0, n: hpT[:, ch, c0:c0 + n], [t_hpT], [b], tiles=[(tl * 512, 512)])
                    c.dma(sp, cspast[:, :, :], cs_past[:, :, tok0:tok0 + 512], writes=[t_csp])
                    rope(b, 512, cspast[:, 0, :], cspast[:, 1, :], kf[:, :], t_kf, rt, t_rt, 3, t_cs=t_csp)
                    t_const_rd = t_csp
                    kk = cnt % 2; cnt += 1
                    act.op(lambda kk=kk: S.copy(kst[kk][:, :], kf[:, :]), reads=[t_kf, t_csp], writes=[t_kst[kk]])
                    c.dma(sp, kt_scr[h, :, tok0:tok0 + 512], kst[kk][:, :], reads=[t_kst[kk]])
                    sb0 = tok0 // 256
                    dve.op(lambda sb0=sb0, h=h: V.tensor_reduce(ksum[:, h, sb0:sb0 + 2],
                                                                kf[:, :].rearrange("p (a b) -> p a b", a=2), AX.X, ALU.add),
                           reads=[t_kf], writes=[t_ksum])
            for h in range(8):
                wt, t_w = wload(w_in[:, 2048 + h * 128: 2048 + (h + 1) * 128], NCH)
                for tl in range(3):
                    tok0 = grp * 1536 + tl * 512
                    b = tl % 3
                    proj(wt, t_w, NCH, lambda ch, c0, n: hpT[:, ch, c0:c0 + n], [t_hpT], [b], tiles=[(tl * 512, 512)])
                    act.op(lambda b=b: S.copy(vtb[:, :], PS[b][:, :]), reads=[t_PS[b]], writes=[t_vtb])
                    psb = 4 + (cnt % 4)
                    pb16 = PS[psb][:, :].bitcast(BF16)
                    pe.group([(lambda s4=s4: PEn.transpose(pb16[:, s4 * 128:(s4 + 1) * 128], vtb[:, s4 * 128:(s4 + 1) * 128],
                                                            identb[:, :])) for s4 in range(4)],
                             reads=[t_vtb, t_const], writes=[t_PS[psb]])
                    kk = cnt % 2; cnt += 1
                    dve.op(lambda kk=kk, pb16=pb16: V.tensor_copy(vst[kk][:, :, :].rearrange("p a b -> p (a b)"), pb16[:, 0:512]),
                           reads=[t_PS[psb]], writes=[t_vst[kk]])
                    c.dma(sp, v_scr[h, tok0 // 128: tok0 // 128 + 4, :, :].rearrange("s p d -> p s d"), vst[kk][:, :, :],
                          reads=[t_vst[kk]])
        c.barrier()

    A3.reset()
    stage2 = [A3.take([128, D], F32) for i in range(2)]; t_stage2 = [T("st0"), T("st1")]
    dve.op(lambda: V.memset(xT[:, :, NT:NTP], 0.0), writes=t_xT)
    dve.op(lambda: V.memset(hT[:, :, NT:NTP], 0.0), writes=[t_hT])
    for sub in range(8):
        load_transpose(x_own[sub * 128:(sub + 1) * 128, :], 128, xT, t_xT, sub * 128, stage2[sub % 2], t_stage2[sub % 2],
                       [4, 5, 6, 7])
    load_transpose(x_c[:, :], NC_, xT, t_xT, NOWN, stage2[0], t_stage2[0], [4, 5, 6, 7])

    def norm_to_hT(gidx):
        for (c0, n) in TOKT:
            def out_fn(ch, rs, c0=c0, n=n):
                dve.op(lambda: V.scalar_tensor_tensor(hT[:, ch, c0:c0 + n], xT[:, ch, c0:c0 + n],
                                                      gall[:, gidx, ch:ch + 1], rs, ALU.mult, ALU.mult),
                       reads=[t_xT[ch], t_rstd, t_const], writes=[t_hT])
            rmsnorm_fm(xT, t_xT, (c0, n), gidx, out_fn, sq2, t_sq2, rstd, t_rstd, 3)

    norm_to_hT(0)
    c.dma(sp, xT_scr.ap(), R1[:, :], reads=t_xT)
    c.barrier()
    A1.reset(); A3.reset()
    QT = A1.take([128, 8, NTP], BF16); t_QT = T("QT")
    YC = A3.take([128, 8, NTP], BF16); t_YC = T("YC")
    kf = A3.take([128, 512], F32); t_kf = T("kf")
    kst = [A3.take([128, 512], BF16) for i in range(2)]; t_kst = [T("kst0"), T("kst1")]
    vst = [A3.take([128, 4, 128], BF16) for i in range(2)]; t_vst = [T("vst0"), T("vst1")]

    hrhs = lambda ch, c0, n: hT[:, ch, c0:c0 + n]
    QsT = sb("QsT", [128, 8, NS], F32); KsT = sb("KsT", [128, 8, NS], F32); t_sm = T("smp")
    Vs = sb("Vs", [NS, 8, 128], F32); Ks = sb("Ks", [NS, 8, 128], F32); Qs = sb("Qs", [NS, 8, 128], F32)
    ost = [A3.take([128, 4, 128], F32) for i in range(2)]; t_ost = [T("ost0"), T("ost1")]
    vf = A3.take([128, 512], F32); t_vf = T("vf")
    ulast = sb("ulast", [128, 8, 2], F32); uslast = sb("uslast", [128, 8, 2], F32); t_ulast = T("ulast")
    stcT = sb("stcT", [128, 8, 2], F32)
    for t_ in range(2):
        c.dma(sp, stcT[:, :, t_], st_conv[t_, :].rearrange("(c p) -> p c", p=128), writes=[t_const],
              allow_slow_non_contiguous=True)
    gcs = A3.take([128, NTP], F32); t_gcs = T("gcs")
    uext = A3.take([128, NOWN + 2], F32); usx = sb("usx", [128, NS + 2], F32); t_u = T("u")
    cva = A3.take([128, NOWN], F32); cvs = sb("cvs", [128, NS], F32); t_cv = T("cv")
    uh = sb("uh", [128, NH], F32); cvh = sb("cvh", [128, NH], F32)
    cnt = 0
    bank_sets = [[0, 1, 2], [4, 5, 6]]
    bsi = 0
    for h in range(8):
        for which in range(3):
            wt, t_w = wload(w_in[:, which * 1024 + h * 128: which * 1024 + (h + 1) * 128], NCH)
            banks = bank_sets[bsi]; bsi ^= 1
            tb = 3 if banks[0] == 0 else 7
            proj(wt, t_w, NCH, hrhs, [t_hT], banks)
            for (c0, n), b in zip(TOKT, banks):
                cosap, sinap = csown[:, 0, c0:c0 + n], csown[:, 1, c0:c0 + n]
                if which == 0:
                    rope(b, n, cosap, sinap, QT[:, h, c0:c0 + n], t_QT, rt, t_rt, tb)
                    if n == NC_:
                        dve.op(lambda h=h: V.tensor_tensor(QsT[:, h, :], rt["t1"][:, 0:NS], rt["t2"][:, 0:NS], ALU.add),
                               reads=[t_rt["t1"], t_rt["t2"]], writes=[t_sm])
                        pe.group([lambda h=h: PEn.transpose(PS[tb][0:NS, 0:128], QsT[:, h, :], ident[:, :])],
                                 reads=[t_sm, t_const], writes=[t_PS[tb]])
                        dve.op(lambda h=h: V.tensor_copy(Qs[:, h, :], PS[tb][0:NS, 0:128]), reads=[t_PS[tb]], writes=[t_sm])
                elif which == 1:
                    rope(b, n, cosap, sinap, kf[:, 0:n], t_kf, rt, t_rt, tb)
                    if n == 512:
                        kk = cnt % 2; cnt += 1
                        act.op(lambda kk=kk: S.copy(kst[kk][:, :], kf[:, :]), reads=[t_kf], writes=[t_kst[kk]])
                        c.dma(sp, kt_scr[h, :, NPAST + c0:NPAST + c0 + 512], kst[kk][:, :], reads=[t_kst[kk]])
                        sb0 = 12 + c0 // 256
                        dve.op(lambda sb0=sb0, h=h: V.tensor_reduce(ksum[:, h, sb0:sb0 + 2],
                                                                    kf[:, :].rearrange("p (a b) -> p a b", a=2), AX.X, ALU.add),
                               reads=[t_kf], writes=[t_ksum])
                        pe.group([(lambda s4=s4: PEn.transpose(PS[tb][:, s4 * 128:(s4 + 1) * 128], kf[:, s4 * 128:(s4 + 1) * 128],
                                                                ident[:, :])) for s4 in range(4)],
                                 reads=[t_kf, t_const], writes=[t_PS[tb]])
                        kk = cnt % 2; cnt += 1
                        evac(ost[kk][:, :, :].rearrange("p a b -> p (a b)"), PS[tb][:, :], [t_PS[tb]], [t_ost[kk]])
                        c.dma(sp, k_own[c0:c0 + 512, h, :].rearrange("(s p) d -> p s d", p=128), ost[kk][:, :, :], reads=[t_ost[kk]])
                    else:
                        dve.op(lambda h=h: V.tensor_copy(KsT[:, h, :], kf[:, 0:NS]), reads=[t_kf], writes=[t_sm])
                        pe.group([lambda h=h: PEn.transpose(PS[tb][0:NS, 0:128], KsT[:, h, :], ident[:, :])],
                                 reads=[t_sm, t_const], writes=[t_PS[tb]])
                        dve.op(lambda h=h: V.tensor_copy(Ks[:, h, :], PS[tb][0:NS, 0:128]), reads=[t_PS[tb]], writes=[t_sm])
                else:
                    evac(vf[:, 0:n], PS[b][:, 0:n], [t_PS[b]], [t_vf])
                    if n == 512:
                        pe.group([(lambda s4=s4: PEn.transpose(PS[tb][:, s4 * 128:(s4 + 1) * 128], vf[:, s4 * 128:(s4 + 1) * 128],
                                                                ident[:, :])) for s4 in range(4)],
                                 reads=[t_vf, t_const], writes=[t_PS[tb]])
                        kk = cnt % 2; cnt += 1
                        evac(ost[kk][:, :, :].rearrange("p a b -> p (a b)"), PS[tb][:, :], [t_PS[tb]], [t_ost[kk]])
                        c.dma(sp, v_own[c0:c0 + 512, h, :].rearrange("(s p) d -> p s d", p=128), ost[kk][:, :, :], reads=[t_ost[kk]])
                        dve.op(lambda kk=kk: V.tensor_copy(vst[kk][:, :, :].rearrange("p a b -> p (a b)"), PS[tb][:, :]),
                               reads=[t_PS[tb]], writes=[t_vst[kk]])
                        st = (NPAST + c0) // 128
                        c.dma(sp, v_scr[h, st:st + 4, :, :].rearrange("s p d -> p s d"), vst[kk][:, :, :], reads=[t_vst[kk]])
                    else:
                        pe.group([lambda: PEn.transpose(PS[tb][0:NS, 0:128], vf[:, 0:NS], ident[:, :])],
                                 reads=[t_vf, t_const], writes=[t_PS[tb]])
                        dve.op(lambda h=h: V.tensor_copy(Vs[:, h, :], PS[tb][0:NS, 0:128]), reads=[t_PS[tb]], writes=[t_sm])
    c.dma(sp, k_smp[:, :, :], Ks[:, :, :], reads=[t_sm])
    c.dma(sp, v_smp[:, :, :], Vs[:, :, :], reads=[t_sm])

    for cc in range(8):
        wts = [wload(w_in[:, 3072 + which * 1024 + cc * 128: 3072 + which * 1024 + (cc + 1) * 128], NCH) for which in (1, 2, 0)]
        banks = bank_sets[bsi]; bsi ^= 1
        proj(wts[0][0], wts[0][1], NCH, hrhs, [t_hT], banks)
        for (c0, n), b in zip(TOKT, banks):
            evac(gcs[:, c0:c0 + n], PS[b][:, 0:n], [t_PS[b]], [t_gcs])
        banks = bank_sets[bsi]; bsi ^= 1
        proj(wts[1][0], wts[1][1], NCH, hrhs, [t_hT], banks)
        for (c0, n), b in zip(TOKT, banks):
            if n == 512:
                dve.op(lambda c0=c0, b=b: V.tensor_tensor(uext[:, 2 + c0:2 + c0 + 512], PS[b][:, 0:512], gcs[:, c0:c0 + 512], ALU.mult),
                       reads=[t_PS[b], t_gcs], writes=[t_u])
            else:
                dve.op(lambda b=b: V.tensor_tensor(usx[:, 2:2 + NS], PS[b][:, 0:NS], gcs[:, NOWN:NOWN + NS], ALU.mult),
                       reads=[t_PS[b], t_gcs], writes=[t_u])
                dve.op(lambda b=b: V.tensor_tensor(uext[:, 0:2], PS[b][:, NC_ - 2:NC_], gcs[:, NT - 2:NT], ALU.mult),
                       reads=[t_PS[b], t_gcs], writes=[t_u])
                dve.op(lambda b=b: V.tensor_tensor(uh[:, 0:NH], PS[b][:, NS:NC_], gcs[:, NOWN + NS:NT], ALU.mult),
                       reads=[t_PS[b], t_gcs], writes=[t_u])
                dve.op(lambda cc=cc: V.tensor_copy(usx[:, 0:2], stcT[:, cc, :]), reads=[t_const], writes=[t_u])
        dve.op(lambda cc=cc: V.tensor_copy(ulast[:, cc, :], uext[:, NOWN:NOWN + 2]), reads=[t_u], writes=[t_ulast])
        dve.op(lambda cc=cc: V.tensor_copy(uslast[:, cc, :], usx[:, NS:NS + 2]), reads=[t_u], writes=[t_ulast])
        for (ux, cv, n) in ((uext, cva, NOWN), (usx, cvs, NS), (uh, cvh, NH - 2)):
            dve.op(lambda ux=ux, cv=cv, n=n, cc=cc: V.tensor_scalar(cv[:, 0:n], ux[:, 0:n], convw[:, cc, 0:1], None, ALU.mult),
                   reads=[t_u, t_const], writes=[t_cv])
            for j in (1, 2):
                dve.op(lambda ux=ux, cv=cv, n=n, cc=cc, j=j: V.scalar_tensor_tensor(cv[:, 0:n], ux[:, j:j + n], convw[:, cc, j:j + 1],
                                                                                 cv[:, 0:n], ALU.mult, ALU.add),
                       reads=[t_u, t_cv, t_const], writes=[t_cv])
        banks = bank_sets[bsi]; bsi ^= 1
        proj(wts[2][0], wts[2][1], NCH, hrhs, [t_hT], banks)
        for (c0, n), b in zip(TOKT, banks):
            if n == 512:
                dve.op(lambda c0=c0, b=b, cc=cc: V.tensor_tensor(YC[:, cc, c0:c0 + 512], PS[b][:, 0:512], cva[:, c0:c0 + 512], ALU.mult),
                       reads=[t_PS[b], t_cv], writes=[t_YC])
            else:
                dve.op(lambda b=b, cc=cc: V.tensor_tensor(YC[:, cc, NOWN:NOWN + NS], PS[b][:, 0:NS], cvs[:, 0:NS], ALU.mult),
                       reads=[t_PS[b], t_cv], writes=[t_YC])
                dve.op(lambda b=b, cc=cc: V.tensor_tensor(YC[:, cc, NOWN + NS + 2:NT], PS[b][:, NS + 2:NC_], cvh[:, 0:NH - 2], ALU.mult),
                       reads=[t_PS[b], t_cv], writes=[t_YC])
                dve.op(lambda cc=cc: V.memset(YC[:, cc, NOWN + NS:NOWN + NS + 2], 0.0), writes=[t_YC])
    for t_ in range(2):
        c.dma(sp, conv_p[t_, :].rearrange("(c p) -> p c", p=128), ulast[:, :, t_], reads=[t_ulast], allow_slow_non_contiguous=True)
        c.dma(sp, conv_s[t_, :].rearrange("(c p) -> p c", p=128), uslast[:, :, t_], reads=[t_ulast], allow_slow_non_contiguous=True)
    c.barrier()

    AT = hT
    t_AT = T("AT")
    if STAGE >= 3:
        kmT = sb("kmT", [128, 8, 16], BF16); t_km = T("km")
        dve.op(lambda: V.tensor_scalar(kmT[:, :, :], ksum[:, :, :], 1.0 / 256.0, None, ALU.mult), reads=[t_ksum], writes=[t_km])
        BT = A1.take([16, 8, NTP], BF16); t_BT = T("BT")
        g2 = sb("g2", [128, 8, 16], F32); top8 = sb("top8", [128, 8, 8], F32); thr = sb("thr", [128, 8], F32)
        selb = sb("selb", [128, 8, 16], BF16); t_g = T("g")
        qtiles = [(i * 128, 128, addc[:, i, :]) for i in range(8)] + [(NOWN + NS, NH, addh[:, :])]
        for (q0, qn, adc) in qtiles:
            gb = 3
            for h in range(8):
                pe.group([lambda h=h: PEn.matmul(PS[gb][0:qn, h * 16:(h + 1) * 16], QT[:, h, q0:q0 + qn], kmT[:, h, :], start=True, stop=True)],
                         reads=[t_QT, t_km], writes=[t_PS[gb]])
            dve.op(lambda: V.tensor_tensor(g2[0:qn, :, :], PS[gb][0:qn, 0:128].rearrange("p (a b) -> p a b", a=8),
                                           adc.unsqueeze(1).broadcast_to([qn, 8, 16]), ALU.add),
                   reads=[t_PS[gb], t_const], writes=[t_g])
            for h in range(8):
                dve.op(lambda h=h: V.max(top8[0:qn, h, :], g2[0:qn, h, :]), reads=[t_g], writes=[t_g])
            dve.op(lambda: V.tensor_scalar(thr[0:qn, :], top8[0:qn, :, 3], -1e29, None, ALU.max), reads=[t_g], writes=[t_g])
            dve.op(lambda: V.tensor_tensor(g2[0:qn, :, :], g2[0:qn, :, :], thr[0:qn, :].unsqueeze(2).broadcast_to([qn, 8, 16]), ALU.is_ge),
                   reads=[t_g], writes=[t_g])
            dve.op(lambda: V.tensor_scalar(selb[0:qn, :, :], g2[0:qn, :, :], -NEGB, NEGB, ALU.mult, ALU.add), reads=[t_g], writes=[t_g])
            pb16 = PS[7][:, :].bitcast(BF16)
            pe.group([(lambda h=h: PEn.transpose(pb16[0:16, h * 128:h * 128 + qn], selb[0:qn, h, :], identb[0:qn, 0:qn])) for h in range(8)],
                     reads=[t_g, t_const], writes=[t_PS[7]])
            dve.op(lambda: V.tensor_copy(BT[:, :, q0:q0 + qn], pb16[0:16, :].rearrange("p (a b) -> p a b", a=8)[:, :, 0:qn]),
                   reads=[t_PS[7]], writes=[t_BT])

        KT2 = [A1.take([128, 4096], BF16) for _ in range(2)]; t_KT2 = [T("KTa"), T("KTb")]
        VV2 = [A1.take([128, 32, 128], BF16) for _ in range(2)]; t_VV2 = [T("Va"), T("Vb")]
        A3.off = (8 * NTP * 2 + 31) // 32 * 32
        PT = [A3.take([128, 512], BF16) for i in range(3)]; t_PT = [T(f"PT{i}") for i in range(3)]
        rden = A3.take([128, 512], F32); t_rden = T("rden")
        pti = 0
        sbank = 0
        for h in range(8):
            kb = h % 2
            c.dma(sp, KT2[kb][:, :], kt_scr[h, :, :], writes=[t_KT2[kb]])
            c.dma(sp, VV2[kb][:, :, :], v_scr[h, :, :, :].rearrange("s p d -> p s d"), writes=[t_VV2[kb]])
            for qi, (q0, qn, nkt) in enumerate([(0, 512, 28), (512, 512, 32), (NOWN + NS, NH, 24)]):
                ob, db = 4, 5
                for kt in range(nkt):
                    sbk = sbank; sbank ^= 1
                    fns = [lambda kt=kt, sbk=sbk: PEn.matmul(PS[sbk][:, 0:qn], KT2[kb][:, kt * 128:(kt + 1) * 128], QT[:, h, q0:q0 + qn],
                                                             start=True, stop=False)]
                    extra = None
                    if qi == 2:
                        extra = hb[:, kt, :]
                    elif kt >= 24:
                        off = (kt - 24) * 128 - q0
                        if off >= 0:
                            extra = cm[:, off // 128, :]
                    fns.append(lambda kt=kt, sbk=sbk, extra=extra: PEn.matmul(PS[sbk][:, 0:qn], eall[:, kt // 2, :], BT[:, h, q0:q0 + qn],
                                                                              start=False, stop=(extra is None)))
                    if extra is not None:
                        fns.append(lambda sbk=sbk, extra=extra: PEn.matmul(PS[sbk][:, 0:qn], identb[:, :], extra, start=False, stop=True))
                    pe.group(fns, reads=[t_KT2[kb], t_QT, t_BT, t_const], writes=[t_PS[sbk]])
                    p = pti; pti = (pti + 1) % 3
                    act.op(lambda sbk=sbk, p=p: S.activation(PT[p][:, 0:qn], PS[sbk][:, 0:qn], AF.Exp, scale=SCALE),
                           reads=[t_PS[sbk]], writes=[t_PT[p]])
                    pe.deps([t_PT[p], t_VV2[kb], t_const], [t_PS[ob], t_PS[db]] if kt == 0 else [])
                    PEn.matmul(PS[ob][:, 0:qn], VV2[kb][:, kt, :], PT[p][:, 0:qn], start=(kt == 0), stop=(kt == nkt - 1))
                    ins = PEn.matmul(PS[db][:, 0:qn], onesb[:, :], PT[p][:, 0:qn], start=(kt == 0), stop=(kt == nkt - 1))
                    pe.count += 1
                    ins.then_inc(pe.sem, 1)
                    tok = (pe, pe.count)
                    pe.commit(tok, [t_PT[p], t_VV2[kb]], [])
                    t_PS[ob].w = tok; t_PS[db].w = tok
                    if kt == 0:
                        t_PS[ob].rd = []; t_PS[db].rd = []
                dve.op(lambda: V.reciprocal(rden[:, 0:qn], PS[db][:, 0:qn]), reads=[t_PS[db]], writes=[t_rden])
                dve.op(lambda h=h: V.tensor_tensor(AT[:, h, q0:q0 + qn], PS[ob][:, 0:qn], rden[:, 0:qn], ALU.mult),
                       reads=[t_PS[ob], t_rden], writes=[t_AT])
    else:
        dve.op(lambda: V.memset(AT[:, 0:8, :], 0.0), writes=[t_AT])
    if STAGE >= 5:
        c.barrier()
        A1.reset()
        kpg = [A1.take([128, 2, 1024], F32) for _ in range(2)]; t_kpg = [T("kpg0"), T("kpg1")]
        Kg = [A1.take([128, 24, 128], F32) for _ in range(2)]; t_Kg = [T("Kg0"), T("Kg1")]
        Vg = [A1.take([128, 24, 128], F32) for _ in range(2)]; t_Vg = [T("Vg0"), T("Vg1")]
        A3.off = (8 * NTP * 2 + 31) // 32 * 32
        kmS = A3.take([128, 8, 64], F32); t_kmS = T("kmS")
        G = A3.take([128, 32, 64], F32); top8s = A3.take([128, 32, 8], F32); idxs = A3.take([128, 32, 8], U32); t_G = T("G")
        nf = A3.take([128, 32, 3], F32)
        OH = A3.take([128, 12, 64], F32); PR = A3.take([128, 12, 64], F32); physf = A3.take([128, 12, 2], F32); t_OH = T("OH")
        idxG = A3.take([128, 8, 24], I32); t_idxG = T("idxG")
        Qrep = [A3.take([128, 128], F32) for _ in range(2)]; t_Qrep = [T("qr0"), T("qr1")]
        qb = A3.take([128, 4, 128], F32); t_qb = T("qb")
        junk = A3.take([128, 128], F32); t_junk = T("junk")
        Ssm = A3.take([128, 24], F32); Psm = A3.take([128, 24], F32); t_S = T("Ssm"); t_P = T("Psm")
        Sown = A3.take([4, 4], F32); Pown = A3.take([4, 4], F32)
        den = A3.take([128, 4], F32); rd = A3.take([128, 4], F32); t_den = T("den")
        ptb = sb("ptb", [128, 128], I32); ptf = sb("ptf", [128, 128], F32); idxP = sb("idxP", [128, 128], I32); t_pt = T("pt")
        pio = sb("pio", [128, 2], F32); iota64 = sb("iota64", [128, 64], F32)
        onesf = sb("onesf", [128, 128], F32)
        esel = sb("esel", [4, 4, 128], F32); smask = sb("smask", [4, 4], F32)
        dve.op(lambda: V.memset(onesf[:, :], 1.0), writes=[t_const])
        c.dma(sp, esel[:, :, :], c_esel.ap(), writes=[t_const])
        c.dma(sp, smask[:, :], c_smask.ap(), writes=[t_const])
        c.dma(sp, pio[:, :], c_pio.ap(), writes=[t_const])
        c.dma(sp, iota64[:, :], c_iota64.ap(), writes=[t_const])
        c.dma(sp, ptb[:, :], ptab.ap().partition_broadcast(128), writes=[t_pt])
        dve.op(lambda: V.tensor_copy(ptf[:, :], ptb[:, :]), reads=[t_pt], writes=[t_pt])
        dve.op(lambda: V.tensor_scalar(idxP[:, :], ptf[:, :], 128.0, pio[:, 0:1], ALU.mult, ALU.add), reads=[t_pt, t_const], writes=[t_pt])
        ck_rows = cache_k.ap().rearrange("n t h d -> (n t) (h d)")
        ck_hrows = cache_k.ap().rearrange("n t h d -> (n t h) d")
        cv_hrows = cache_v.ap().rearrange("n t h d -> (n t h) d")
        for n in range(64):
            kb = n % 2
            for pg in range(2):
                c.swdma(4 + kb, kpg[kb][:, pg, :], ck_rows, reads=[t_pt], writes=[t_kpg[kb]],
                        indirect=idxP[:, 2 * n + pg:2 * n + pg + 1])
            for h in range(8):
                col = h * 64 + n
                pe.group([lambda kb=kb, h=h, col=col: PEn.matmul(PS[0][:, col:col + 1], kpg[kb][:, 0, h * 128:(h + 1) * 128], onesf[:, 0:1],
                                                                 start=True, stop=False),
                          lambda kb=kb, h=h, col=col: PEn.matmul(PS[0][:, col:col + 1], kpg[kb][:, 1, h * 128:(h + 1) * 128], onesf[:, 0:1],
                                                                 start=False, stop=True)],
                         reads=[t_kpg[kb], t_const], writes=[t_PS[0]])
        dve.op(lambda: V.tensor_scalar(kmS[:, :, :].rearrange("p a b -> p (a b)"), PS[0][:, :], 1.0 / 256.0, None, ALU.mult),
               reads=[t_PS[0]], writes=[t_kmS])
        for hq in range(32):
            h, q = hq // 4, hq % 4
            bank = 4 + hq // 8; col = (hq % 8) * 64
            k2 = hq % 2
            dve.op(lambda h=h, q=q, k2=k2: V.tensor_copy(Qrep[k2][:, :], QsT[:, h, q:q + 1].broadcast_to([128, 128])),
                   reads=[t_sm], writes=[t_Qrep[k2]])
            pe.group([lambda h=h, k2=k2, bank=bank, col=col: PEn.matmul(PS[bank][:, col:col + 64], Qrep[k2][:, :], kmS[:, h, :],
                                                                        start=True, stop=True)],
                     reads=[t_Qrep[k2], t_kmS], writes=[t_PS[bank]])
        for b4 in range(4):
            dve.op(lambda b4=b4: V.tensor_copy(G[:, b4 * 8:(b4 + 1) * 8, :].rearrange("p a b -> p (a b)"), PS[4 + b4][:, :]),
                   reads=[t_PS[4 + b4]], writes=[t_G])
        for hq in range(32):
            dve.op(lambda hq=hq: V.max(top8s[:, hq, :], G[:, hq, :]), reads=[t_G], writes=[t_G])
            dve.op(lambda hq=hq: V.max_index(idxs[:, hq, :], top8s[:, hq, :], G[:, hq, :]), reads=[t_G], writes=[t_G])
        dve.op(lambda: V.tensor_copy(nf[:, :, :], idxs[:, :, 0:3]), reads=[t_G], writes=[t_G])
        for h in range(8):
            dve.op(lambda h=h: V.tensor_tensor(OH[:, :, :], iota64[:, :].unsqueeze(1).broadcast_to([128, 12, 64]),
                                               nf[:, h * 4:(h + 1) * 4, :].rearrange("p a b -> p (a b)").unsqueeze(2).broadcast_to([128, 12, 64]),
                                               ALU.is_equal), reads=[t_G, t_const], writes=[t_OH])
            for pgi in range(2):
                dve.op(lambda pgi=pgi: V.tensor_tensor(PR[:, :, :], OH[:, :, :],
                                                       ptf[:, :].rearrange("p (n g) -> p n g", g=2)[:, :, pgi].unsqueeze(1).broadcast_to([128, 12, 64]),
                                                       ALU.mult), reads=[t_OH, t_pt], writes=[t_OH])
                dve.op(lambda pgi=pgi: V.tensor_reduce(physf[:, :, pgi], PR[:, :, :], AX.X, ALU.add), reads=[t_OH], writes=[t_OH])
            dve.op(lambda h=h: V.tensor_scalar(idxG[:, h, :], physf[:, :, :].rearrange("p a b -> p (a b)"), 1024.0, pio[:, 1:2], ALU.mult, ALU.add),
                   reads=[t_OH, t_const], writes=[t_idxG])
        for h in range(8):
            gb = h % 2
            for s_ in range(24):
                c.swdma(6 + gb, Kg[gb][:, s_, :], ck_hrows, reads=[t_idxG], writes=[t_Kg[gb]], indirect=idxG[:, h, s_:s_ + 1],
                        element_offset=h * 128)
            for s_ in range(24):
                c.swdma(8 + gb, Vg[gb][:, s_, :], cv_hrows, reads=[t_idxG], writes=[t_Vg[gb]], indirect=idxG[:, h, s_:s_ + 1],
                        element_offset=h * 128)
            pe.group([(lambda q=q, h=h: PEn.matmul(PS[1][:, q * 128:(q + 1) * 128], esel[0:4, q, :], Qs[0:4, h, :], start=True, stop=True))
                      for q in range(4)], reads=[t_sm, t_const], writes=[t_PS[1]])
            act.op(lambda: S.copy(qb[:, :, :].rearrange("p a b -> p (a b)"), PS[1][:, :]), reads=[t_PS[1]], writes=[t_qb])
            for s_ in range(24):
                dve.op(lambda s_=s_, gb=gb: V.scalar_tensor_tensor(junk[:, :], Kg[gb][:, s_, :], SCALE, qb[:, s_ // 6, :], ALU.mult, ALU.mult,
                                                                   accum_out=Ssm[:, s_:s_ + 1]),
                       reads=[t_Kg[gb], t_qb], writes=[t_junk, t_S])
            for q in range(4):
                dve.op(lambda q=q, h=h: V.scalar_tensor_tensor(junk[0:4, :], Ks[0:4, h, :], SCALE, qb[0:4, q, :], ALU.mult, ALU.mult,
                                                               accum_out=Sown[0:4, q:q + 1]),
                       reads=[t_sm, t_qb], writes=[t_junk, t_S])
            dve.op(lambda: V.tensor_tensor(Sown[0:4, :], Sown[0:4, :], smask[0:4, :], ALU.add), reads=[t_S, t_const], writes=[t_S])
            act.op(lambda: S.activation(Psm[:, :], Ssm[:, :], AF.Exp), reads=[t_S], writes=[t_P])
            act.op(lambda: S.activation(Pown[0:4, :], Sown[0:4, :], AF.Exp), reads=[t_S], writes=[t_P])
            pe.group([lambda: PEn.matmul(PS[2][:, 0:24], onesf[:, :], Psm[:, :], start=True, stop=True)],
                     reads=[t_P, t_const], writes=[t_PS[2]])
            dve.op(lambda: V.tensor_reduce(den[:, 0:4], PS[2][:, 0:24].rearrange("p (a b) -> p a b", a=4), AX.X, ALU.add),
                   reads=[t_PS[2]], writes=[t_den])
            pe.group([lambda: PEn.matmul(PS[2][:, 32:36], onesf[0:4, :], Pown[0:4, :], start=True, stop=True)],
                     reads=[t_P, t_const], writes=[t_PS[2]])
            dve.op(lambda: V.tensor_tensor(den[:, 0:4], den[:, 0:4], PS[2][:, 32:36], ALU.add), reads=[t_PS[2], t_den], writes=[t_den])
            dve.op(lambda: V.reciprocal(rd[:, 0:4], den[:, 0:4]), reads=[t_den], writes=[t_den])
            for q in range(4):
                fns = [(lambda q=q, i=i, gb=gb: PEn.matmul(PS[3][:, q:q + 1], Vg[gb][:, q * 6 + i, :], Psm[:, q * 6 + i:q * 6 + i + 1],
                                                           start=(i == 0), stop=False)) for i in range(6)]
                fns.append(lambda q=q, h=h: PEn.matmul(PS[3][:, q:q + 1], Vs[0:4, h, :], Pown[0:4, q:q + 1], start=False, stop=True))
                pe.group(fns, reads=[t_Vg[gb], t_P, t_sm], writes=[t_PS[3]])
            dve.op(lambda h=h: V.tensor_tensor(AT[:, h, NOWN:NOWN + NS], PS[3][:, 0:4], rd[:, 0:4], ALU.mult),
                   reads=[t_PS[3], t_den], writes=[t_AT])
    else:
        dve.op(lambda: V.memset(AT[:, 0:8, NOWN:NOWN + NS], 0.0), writes=[t_AT])
    c.barrier()

    c.dma(sp, R1[:, :], xT_scr.ap(), writes=t_xT)

    def resid_add(banks, oc, scale_ap=None):
        for (c0, n), b in zip(TOKT, banks):
            if scale_ap is None:
                dve.op(lambda c0=c0, n=n, b=b: V.tensor_tensor(xT[:, oc, c0:c0 + n], PS[b][:, 0:n], xT[:, oc, c0:c0 + n], ALU.add),
                       reads=[t_PS[b], t_xT[oc]], writes=[t_xT[oc]])
            else:
                dve.op(lambda c0=c0, n=n, b=b: V.scalar_tensor_tensor(xT[:, oc, c0:c0 + n], PS[b][:, 0:n], scale_ap, xT[:, oc, c0:c0 + n],
                                                                      ALU.mult, ALU.add),
                       reads=[t_PS[b], t_xT[oc], t_const], writes=[t_xT[oc]])

    for oc in range(NCH):
        wt, t_w = wload(w_o[:, oc * 128:(oc + 1) * 128], NCH)
        banks = bank_sets[bsi]; bsi ^= 1
        proj(wt, t_w, NCH, lambda ch, c0, n: (AT[:, ch, c0:c0 + n] if ch < 8 else YC[:, ch - 8, c0:c0 + n]), [t_AT, t_YC], banks)
        resid_add(banks, oc)
    c.barrier()

    def ffn(layer, gidx):
        norm_to_hT(gidx)
        A3.reset()
        NG = 4
        per = NFT // NG
        aT = A3.take([128, per, NTP], BF16); t_aT = T("aT")
        sg = [A3.take([128, 512], F32) for i in range(2)]; t_sg = [T("sg0"), T("sg1")]
        sgi = 0
        for g in range(NG):
            for fi in range(per):
                ft = g * per + fi
                wg, t_wg = wload(w_gate[layer, :, ft * 128:(ft + 1) * 128], NCH)
                wu, t_wu = wload(w_up[layer, :, ft * 128:(ft + 1) * 128], NCH)
                proj(wg, t_wg, NCH, hrhs, [t_hT], [0, 1, 2])
                proj(wu, t_wu, NCH, hrhs, [t_hT], [4, 5, 6])
                for (c0, n), bg, bu in zip(TOKT, [0, 1, 2], [4, 5, 6]):
                    k = sgi; sgi ^= 1
                    act.op(lambda k=k, bg=bg, n=n: S.activation(sg[k][:, 0:n], PS[bg][:, 0:n], AF.Silu), reads=[t_PS[bg]], writes=[t_sg[k]])
                    dve.op(lambda k=k, bu=bu, n=n, c0=c0, fi=fi: V.tensor_tensor(aT[:, fi, c0:c0 + n], PS[bu][:, 0:n], sg[k][:, 0:n], ALU.mult),
                           reads=[t_PS[bu], t_sg[k]], writes=[t_aT])
            for oc in range(NCH):
                wd, t_wd = wload(w_down[layer, g * per * 128:(g + 1) * per * 128, oc * 128:(oc + 1) * 128], per)
                banks = bank_sets[bsi_box[0]]; bsi_box[0] ^= 1
                proj(wd, t_wd, per, lambda ch, c0, n: aT[:, ch, c0:c0 + n], [t_aT], banks)
                resid_add(banks, oc)
        c.barrier()

    bsi_box = [bsi]
    if STAGE >= 4:
        ffn(0, 1)

    if STAGE >= 4:
        A3.reset()
        AR = A3
        dT = hT; t_dT = t_hT
        stpT = AR.take([128, NCH, NPH], F32)
        h1l = AR.take([128, NCH, NPH], F32); h1s = AR.take([128, NCH, NPH], F32); t_h1l = T("h1l")
        stp_sb = AR.take([NPH, D], F32); t_stp = T("stp")
        c.dma(sp, stp_sb[:, :], st_pool[:, :], writes=[t_stp])
        for q4 in range(4):
            pe.group([(lambda k=k: PEn.transpose(PS[7][:, k * 128:k * 128 + NPH], stp_sb[0:NPH, (q4 * 4 + k) * 128:(q4 * 4 + k + 1) * 128],
                                                  ident[0:NPH, 0:NPH])) for k in range(4)], reads=[t_stp, t_const], writes=[t_PS[7]])
            dve.op(lambda q4=q4: V.tensor_copy(stpT[:, q4 * 4:q4 * 4 + 4, :], PS[7][:, :].rearrange("p (k n) -> p k n", k=4)[:, :, 0:NPH]),
                   reads=[t_PS[7]], writes=[t_stp])
        rs_all = AR.take([128, NTP], F32); t_rsall = T("rsall")
        for (c0, n) in TOKT:
            rmsnorm_fm(xT, t_xT, (c0, n), 2, lambda ch, rs: None, sq2, t_sq2, rstd, t_rstd, 3)
            dve.op(lambda c0=c0, n=n: V.tensor_copy(rs_all[:, c0:c0 + n], rstd[:, 0:n]), reads=[t_rstd], writes=[t_rsall])
        hx = AR.take([128, NPH + NOWN], F32); hxs = AR.take([128, NPH + NS], F32); t_hx = T("hx")
        sA = AR.take([128, NPH + NOWN], F32); sB = AR.take([128, NPH + NOWN], F32); t_s = T("s")
        for ch in range(NCH):
            g = ch // 4
            gsc = gall[:, 2, ch:ch + 1]
            dve.op(lambda ch=ch: V.scalar_tensor_tensor(hx[:, NPH:NPH + NOWN], xT[:, ch, 0:NOWN], gsc, rs_all[:, 0:NOWN], ALU.mult, ALU.mult),
                   reads=[t_xT[ch], t_rsall, t_const], writes=[t_hx])
            dve.op(lambda ch=ch: V.scalar_tensor_tensor(hx[:, 0:NPH], xT[:, ch, NT - NPH:NT], gsc, rs_all[:, NT - NPH:NT], ALU.mult, ALU.mult),
                   reads=[t_xT[ch], t_rsall, t_const], writes=[t_hx])
            dve.op(lambda: V.tensor_tensor(hx[:, 0:NPH], hx[:, 0:NPH], hv[:, 0:NPH], ALU.mult), reads=[t_hx, t_const], writes=[t_hx])
            dve.op(lambda ch=ch: V.scalar_tensor_tensor(hxs[:, NPH:NPH + NS], xT[:, ch, NOWN:NOWN + NS], gsc, rs_all[:, NOWN:NOWN + NS], ALU.mult, ALU.mult),
                   reads=[t_xT[ch], t_rsall, t_const], writes=[t_hx])
            dve.op(lambda ch=ch: V.tensor_copy(hxs[:, 0:NPH], stpT[:, ch, :]), reads=[t_stp], writes=[t_hx])
            dve.op(lambda ch=ch: V.tensor_copy(h1l[:, ch, :], hx[:, NOWN:NOWN + NPH]), reads=[t_hx], writes=[t_h1l])
            dve.op(lambda ch=ch: V.tensor_copy(h1s[:, ch, :], hxs[:, NS:NS + NPH]), reads=[t_hx], writes=[t_h1l])
            w = 2 ** (g + 1)
            for (src, L, c0out, nout) in ((hx, NPH + NOWN, 0, NOWN), (hxs, NPH + NS, NOWN, NS)):
                cur = src
                sh = 1
                bufs = [sA, sB]
                bi = 0
                while sh < w:
                    nxt = bufs[bi]; bi ^= 1
                    dve.op(lambda cur=cur, nxt=nxt, sh=sh, L=L: V.tensor_tensor(nxt[:, sh:L], cur[:, sh:L], cur[:, 0:L - sh], ALU.add),
                           reads=[t_hx, t_s], writes=[t_s])
                    if sh > 1 or True:
                        dve.op(lambda cur=cur, nxt=nxt, sh=sh: V.tensor_copy(nxt[:, 0:sh], cur[:, 0:sh]), reads=[t_hx, t_s], writes=[t_s])
                    cur = nxt
                    sh *= 2
                dve.op(lambda cur=cur, src=src, c0out=c0out, nout=nout, ch=ch, w=w: V.scalar_tensor_tensor(
                    dT[:, ch, c0out:c0out + nout], cur[:, NPH:NPH + nout], 1.0 / w, src[:, NPH:NPH + nout], ALU.mult, ALU.subtract),
                    reads=[t_s, t_hx], writes=[t_dT])
                if nout == NOWN:
                    fx = sA if cur is sB else sB
                    dve.op(lambda cur=cur, fx=fx, g=g: V.tensor_tensor(fx[:, 0:NPH], cur[:, NPH:2 * NPH], invc[:, g, 0:NPH], ALU.mult),
                           reads=[t_s, t_const], writes=[t_s])
                    dve.op(lambda fx=fx, src=src, ch=ch: V.tensor_tensor(dT[:, ch, 0:NPH], fx[:, 0:NPH], src[:, NPH:2 * NPH], ALU.subtract),
                           reads=[t_s, t_hx], writes=[t_dT])
            dve.op(lambda ch=ch: V.memset(dT[:, ch, NOWN + NS:NT], 0.0), writes=[t_dT])
        for (src, dst) in ((h1l, pool_p), (h1s, pool_s)):
            po = stp_sb; t_po = t_stp
            for q4 in range(4):
                pe.group([(lambda k=k: PEn.transpose(PS[7][0:NPH, k * 128:(k + 1) * 128], src[:, q4 * 4 + k, :], ident[:, :])) for k in range(4)],
                         reads=[t_h1l, t_const], writes=[t_PS[7]])
                dve.op(lambda q4=q4, po=po: V.tensor_copy(po[:, q4 * 512:(q4 + 1) * 512], PS[7][0:NPH, :]), reads=[t_PS[7]], writes=[t_po])
            c.dma(sp, dst[:, :], po[:, :], reads=[t_po])
        for g in range(4):
            for oc4 in range(4):
                oc = g * 4 + oc4
                wt, t_w = wload(w_pool[g, :, oc4 * 128:(oc4 + 1) * 128], 4)
                banks = bank_sets[bsi_box[0]]; bsi_box[0] ^= 1
                proj(wt, t_w, 4, lambda ch, c0, n, g=g: dT[:, g * 4 + ch, c0:c0 + n], [t_dT], banks)
                resid_add(banks, oc, psc[:, oc:oc + 1])
        c.barrier()
        ffn(1, 3)

    A3.reset()
    stage2 = [A3.take([128, D], F32) for i in range(2)]; t_stage2 = [T("st0"), T("st1")]
    yT = A3.take([128, NCH, 128], F32); t_yT = T("yT")
    for (c0, n) in TOKT:
        rmsnorm_fm(xT, t_xT, (c0, n), 4, lambda ch, rs: None, sq2, t_sq2, rstd, t_rstd, 3)
        nsub = 4 if n == 512 else 1
        for sub in range(nsub):
            nr = 128 if n == 512 else NS
            k2 = sub % 2
            for ch in range(NCH):
                dve.op(lambda ch=ch, sub=sub, nr=nr, c0=c0: V.scalar_tensor_tensor(
                    yT[:, ch, 0:nr], xT[:, ch, c0 + sub * 128:c0 + sub * 128 + nr], gall[:, 4, ch:ch + 1],
                    rstd[:, sub * 128:sub * 128 + nr], ALU.mult, ALU.mult),
                    reads=[t_xT[ch], t_rstd, t_const], writes=[t_yT])
            for q4 in range(4):
                b = 4 + q4
                pe.group([(lambda k=k: PEn.transpose(PS[b][0:nr, k * 128:(k + 1) * 128], yT[:, q4 * 4 + k, 0:nr], ident[:, :]))
                          for k in range(4)], reads=[t_yT, t_const], writes=[t_PS[b]])
                evac(stage2[k2][0:nr, q4 * 512:(q4 + 1) * 512], PS[b][0:nr, :], [t_PS[b]], [t_stage2[k2]])
            if n == 512:
                c.dma(sp, y_own[c0 + sub * 128:c0 + (sub + 1) * 128, :], stage2[k2][:, :], reads=[t_stage2[k2]])
            else:
                c.dma(sp, y_smp[:, :], stage2[k2][0:NS, :], reads=[t_stage2[k2]])
    c.finish()
    return nc


_CACHE = {}


def _consts(j):
    bf = ml_dtypes.bfloat16
    half = 64
    inv = (np.float32(10000.0) ** (-np.arange(half, dtype=np.float32) / np.float32(half))).astype(np.float32)

    def cs(pos):
        ang = pos.astype(np.float32)[:, None] * inv[None, :]
        co, si = np.cos(ang).astype(np.float32), np.sin(ang).astype(np.float32)
        return np.stack([np.concatenate([co, co], 1).T, np.concatenate([si, si], 1).T], axis=1)
    T0 = j * 1024
    pos = np.zeros(NTP, np.float32)
    pos[0:NOWN] = T0 + np.arange(NOWN)
    pos[NOWN:NOWN + NS] = 16384 + np.arange(NS)
    pos[NOWN + NS:NT] = np.maximum(T0 - NH + np.arange(NH), 0)
    d = {}
    d["cs_own"] = np.ascontiguousarray(cs(pos))
    d["cs_past"] = np.ascontiguousarray(cs(np.arange(NPAST, dtype=np.float32)))
    d["c_ident"] = np.eye(128, dtype=np.float32)
    pr = np.zeros((128, 128), np.float32)
    for dd in range(64):
        pr[dd + 64, dd] = -1.0
        pr[dd, dd + 64] = 1.0
    d["c_prot"] = pr.astype(bf)
    ea = np.zeros((16, 16, 128), np.float32)
    for r in range(16):
        ea[r, r, :] = 1.0
    d["c_eall"] = ea.astype(bf)
    k = np.arange(128)[:, None]; q = np.arange(512)[None, :]
    cmm = np.stack([np.where(off * 128 + k <= q, 0.0, NEGB) for off in range(4)], axis=1)
    d["c_cm"] = cmm.astype(bf)
    hbm = np.zeros((128, 24, NH), np.float32)
    for kt in range(24):
        for qq in range(NH):
            p = T0 - NH + qq
            s = kt * 128 + np.arange(128)
            vis = (s <= p) if j > 0 else np.full(128, kt == 0)
            hbm[:, kt, qq] = np.where(vis, 0.0, NEGB)
    d["c_hb"] = hbm.astype(bf)
    adc = np.zeros((128, 8, 16), np.float32)
    for qt in range(8):
        ob = (qt * 128) // 256
        for sbk in range(16):
            if sbk < 12:
                v = 0.0 if sbk < 4 * j else -2e30
            else:
                o = sbk - 12
                v = 0.0 if o < ob else (1e30 if o == ob else -2e30)
            adc[:, qt, sbk] = v
    d["c_addc"] = adc
    adh = np.full((NH, 16), -2e30, np.float32)
    if j > 0:
        adh[:, :4 * j - 1] = 0.0
        adh[:, 4 * j - 1] = 1e30
    else:
        adh[:, 0] = 1e30
    d["c_addh"] = adh
    d["c_hv"] = np.full((128, NH), 1.0 if j > 0 else 0.0, np.float32)
    ic = np.zeros((128, 4, 16), np.float32)
    for g in range(4):
        w = 2 ** (g + 1)
        for i in range(16):
            ic[:, g, i] = 1.0 / min(T0 + i + 1, w)
    d["c_invc"] = ic
    sm = np.zeros((4, 4), np.float32)
    for kk in range(4):
        for qq in range(4):
            sm[kk, qq] = 0.0 if kk <= qq else NEGB
    d["c_smask"] = sm
    es = np.zeros((4, 4, 128), np.float32)
    for r in range(4):
        es[r, r, :] = 1.0
    d["c_esel"] = es
    d["c_pio"] = np.stack([np.arange(128, dtype=np.float32), 8.0 * np.arange(128, dtype=np.float32)], axis=1)
    d["c_iota64"] = np.ascontiguousarray(np.broadcast_to(np.arange(64, dtype=np.float32)[None, :], (128, 64)))
    return d


def kernel(x_prompt, x_sample, cache_k, cache_v, page_table, state_conv, state_pool, norm_mix, norm_ffn, norm_final,
           w_in, conv_w, w_o, w_pool, pool_scale, w_gate, w_up, w_down):
    f = lambda a: np.ascontiguousarray(np.asarray(a, dtype=np.float32))
    x_prompt, x_sample = f(x_prompt), f(x_sample)
    if "nc" not in _CACHE:
        _CACHE["nc"] = build_program()
    nc = _CACHE["nc"]
    fm = lambda v: np.asarray(v, np.float32).reshape(16, 128).T
    g_all = np.ascontiguousarray(np.stack([fm(norm_mix[0]), fm(norm_ffn[0]), fm(norm_mix[1]), fm(norm_ffn[1]), fm(norm_final)], axis=1))
    shared = {
        "g_all": g_all, "w_in": f(w_in)[0], "conv_w": np.ascontiguousarray(f(conv_w)[0].reshape(3, 8, 128).transpose(2, 1, 0)),
        "w_o": f(w_o)[0], "w_pool": f(w_pool)[0], "pscale": np.ascontiguousarray(fm(pool_scale[0])),
        "w_gate": f(w_gate), "w_up": f(w_up), "w_down": f(w_down),
    }
    if STAGE >= 5:
        shared["cache_k"] = f(cache_k)[0]; shared["cache_v"] = f(cache_v)[0]
    in_maps = []
    for c in range(8):
        b, j = c // 4, c % 4
        T0 = j * 1024
        m = dict(shared)
        m["x_own"] = np.ascontiguousarray(x_prompt[b, T0:T0 + 1024])
        halo = np.zeros((NH, D), np.float32)
        if j > 0:
            halo[:] = x_prompt[b, T0 - NH:T0]
        m["x_c"] = np.ascontiguousarray(np.concatenate([x_sample[c], halo], axis=0))
        m["x_past"] = np.ascontiguousarray(x_prompt[b, 0:NPAST])
        m["ptab"] = np.ascontiguousarray(np.asarray(page_table, np.int32)[c:c + 1])
        m["st_conv"] = np.ascontiguousarray(f(state_conv)[0, c])
        m["st_pool"] = np.ascontiguousarray(f(state_pool)[0, c])
        m.update(_consts(j))
        in_maps.append(m)
    res = run_bass_kernel_spmd(nc, in_maps, core_ids=list(range(8)))
    R = res.results
    y_prompt = np.stack([np.concatenate([R[b * 4 + j]["y_own"] for j in range(4)], 0) for b in range(2)], 0)
    y_sample = np.stack([R[c]["y_smp"] for c in range(8)], 0)
    k_prompt = np.stack([np.concatenate([R[b * 4 + j]["k_own"] for j in range(4)], 0) for b in range(2)], 0)[None]
    v_prompt = np.stack([np.concatenate([R[b * 4 + j]["v_own"] for j in range(4)], 0) for b in range(2)], 0)[None]
    k_sample = np.stack([R[c]["k_smp"] for c in range(8)], 0)[None]
    v_sample = np.stack([R[c]["v_smp"] for c in range(8)], 0)[None]
    conv_prompt = np.stack([R[3]["conv_p"], R[7]["conv_p"]], 0)[None]
    conv_sample = np.stack([R[c]["conv_s"] for c in range(8)], 0)[None]
    pool_prompt = np.stack([R[3]["pool_p"], R[7]["pool_p"]], 0)[None]
    pool_sample = np.stack([R[c]["pool_s"] for c in range(8)], 0)[None]
    outs = (y_prompt, y_sample, k_prompt, v_prompt, k_sample, v_sample, conv_prompt, conv_sample, pool_prompt, pool_sample)
    return tuple(np.ascontiguousarray(o.astype(np.float32)) for o in outs)
```

```python
import os
import numpy as np
import ml_dtypes
import concourse.bass as bass
import concourse.mybir as mybir
from concourse.bass_utils import run_bass_kernel_spmd

F32 = mybir.dt.float32
BF16 = mybir.dt.bfloat16
I32 = mybir.dt.int32
U32 = mybir.dt.uint32
AF = mybir.ActivationFunctionType
ALU = mybir.AluOpType
AX = mybir.AxisListType

D = 2048
NCH = 16
DFF = 5632
NFT = 44
NPAST = 3072
NOWN = 1024
NS = 4
NH = 17
NPH = 15
NC_ = NS + NH
NT = NOWN + NC_
NTP = 1048
SCALE = 128 ** -0.5
NEGB = -30000.0
TOKT = [(0, 512), (512, 512), (1024, NC_)]
STAGE = int(os.environ.get("MK_STAGE", "5"))


class T:
    __slots__ = ("name", "w", "rd", "excl")

    def __init__(self, name="", excl=False):
        self.name = name
        self.w = None
        self.rd = []
        self.excl = excl


def _prune(rd):
    best = {}
    for tok in rd:
        if tok[0] in ("dma", "sw"):
            k = (tok[0], tok[1])
            if k not in best or best[k][2] < tok[2]:
                best[k] = tok
        else:
            k = tok[0].name
            if k not in best or best[k][1] < tok[1]:
                best[k] = tok
    return list(best.values())


class Eng:
    def __init__(self, ctx, name, e, sem):
        self.ctx, self.name, self.e, self.sem = ctx, name, e, sem
        self.count = 0
        self.seen = {}
        self.seen_dma = {}
        self.seen_sw = {}
        self.is_pe = name == "pe"

    def need(self, tok, war=False):
        if tok is None:
            return
        if tok[0] == "sw":
            _, s, g = tok
            if self.seen_sw.get(s, 0) >= g:
                return
            self.e.wait_ge(self.ctx.sw_sems[s], 16 * g)
            self.seen_sw[s] = g
            return
        if tok[0] == "dma":
            _, s, v = tok
            if self.seen_dma.get(s, 0) >= v:
                return
            self.e.wait_ge(self.ctx.dma_sems[s], v)
            self.seen_dma[s] = v
            return
        src, o = tok
        if src is self:
            if self.is_pe:
                return
            if self.seen.get(self.name, 0) >= o:
                return
            self.e.wait_ge(self.sem, o)
            self.seen[self.name] = o
            return
        if self.seen.get(src.name, 0) >= o:
            return
        self.e.wait_ge(src.sem, o)
        self.seen[src.name] = o

    def deps(self, reads, writes):
        for t in reads:
            self.need(t.w)
            if t.excl:
                for r in t.rd:
                    if r[0] is not self:
                        self.need(r)
        for t in writes:
            self.need(t.w)
            for r in t.rd:
                self.need(r, war=True)

    def commit(self, tok, reads, writes):
        for t in reads:
            t.rd.append(tok)
            if len(t.rd) > 16:
                t.rd = _prune(t.rd)
        for t in writes:
            t.w = tok
            t.rd = []

    def op(self, fn, reads=(), writes=()):
        self.deps(reads, writes)
        ins = fn()
        self.count += 1
        ins.then_inc(self.sem, 1)
        tok = (self, self.count)
        self.commit(tok, reads, writes)
        return tok

    def group(self, fns, reads=(), writes=()):
        self.deps(reads, writes)
        ins = None
        for fn in fns:
            ins = fn()
        self.count += 1
        ins.then_inc(self.sem, 1)
        tok = (self, self.count)
        self.commit(tok, reads, writes)
        return tok


class Ctx:
    def __init__(self, nc, n_dma_sems=40):
        self.nc = nc
        self.pe = Eng(self, "pe", nc.tensor, nc.alloc_semaphore("s_pe"))
        self.dve = Eng(self, "dve", nc.vector, nc.alloc_semaphore("s_dve"))
        self.act = Eng(self, "act", nc.scalar, nc.alloc_semaphore("s_act"))
        self.pool = Eng(self, "pool", nc.gpsimd, nc.alloc_semaphore("s_pool"))
        self.sp = Eng(self, "sp", nc.sync, nc.alloc_semaphore("s_sp"))
        self.dma_sems = [nc.alloc_semaphore(f"s_dma{i}") for i in range(n_dma_sems)]
        self.dma_val = [0] * n_dma_sems
        self.dma_next = 0
        self.sw_sems = [nc.alloc_semaphore(f"s_sw{i}") for i in range(10)]
        self.sw_gen = [0] * 10

    def swdma(self, i, out, in_, reads=(), writes=(), indirect=None, element_offset=0, **kw):
        q = self.pool
        q.deps(reads, writes)
        if self.sw_gen[i] > 0:
            q.need(("sw", i, self.sw_gen[i]))
        if indirect is not None:
            ins = q.e.indirect_dma_start(out=out, out_offset=None, in_=in_,
                                         in_offset=bass.IndirectOffsetOnAxis(ap=indirect, axis=0), element_offset=element_offset)
        else:
            ins = q.e.dma_start(out=out, in_=in_, **kw)
        self.sw_gen[i] += 1
        ins.then_inc(self.sw_sems[i], 16)
        tok = ("sw", i, self.sw_gen[i])
        q.commit(tok, reads, writes)
        return tok

    def dma(self, q, out, in_, reads=(), writes=(), **kw):
        q.deps(reads, writes)
        s = self.dma_next
        self.dma_next = (self.dma_next + 1) % len(self.dma_sems)
        if self.dma_val[s] > 0:
            q.need(("dma", s, self.dma_val[s]))
        ins = q.e.dma_start(out=out, in_=in_, **kw)
        self.dma_val[s] += 16
        ins.then_inc(self.dma_sems[s], 16)
        tok = ("dma", s, self.dma_val[s])
        q.commit(tok, reads, writes)
        return tok

    def barrier(self):
        engs = (self.pe, self.dve, self.act, self.pool, self.sp)
        for q in engs:
            for s, g in enumerate(self.sw_gen):
                if g > 0:
                    q.need(("sw", s, g))
            for s, v in enumerate(self.dma_val):
                if v > 0:
                    q.need(("dma", s, v))
            for e in engs:
                if e is not q and e.count > 0:
                    q.need((e, e.count))

    def finish(self):
        q = self.sp
        for s, v in enumerate(self.dma_val):
            if v > 0:
                q.need(("dma", s, v))
        for e in (self.pe, self.dve, self.act, self.pool):
            if e.count > 0:
                q.need((e, e.count))


def build_program():
    nc = bass.Bass("TRN2", target_bir_lowering=False)
    c = Ctx(nc)
    pe, dve, act, pool, sp = c.pe, c.dve, c.act, c.pool, c.sp
    V, S, PEn = nc.vector, nc.scalar, nc.tensor

    def din(name, shape, dt=F32):
        return nc.dram_tensor(name, list(shape), dt, kind="ExternalInput")

    def dout(name, shape, dt=F32):
        return nc.dram_tensor(name, list(shape), dt, kind="ExternalOutput")

    x_own = din("x_own", [NOWN, D]); x_c = din("x_c", [NC_, D]); x_past = din("x_past", [NPAST, D])
    if STAGE >= 5:
        cache_k = din("cache_k", [1280, 128, 8, 128]); cache_v = din("cache_v", [1280, 128, 8, 128])
    ptab = din("ptab", [1, 128], I32)
    st_conv = din("st_conv", [2, 1024]); st_pool = din("st_pool", [15, D])
    g_all = din("g_all", [128, 5, 16])
    w_in = din("w_in", [D, 6144]); conv_w = din("conv_w", [128, 8, 3]); w_o = din("w_o", [D, D])
    w_pool = din("w_pool", [4, 512, 512]); pscale = din("pscale", [128, 16])
    w_gate = din("w_gate", [2, D, DFF]); w_up = din("w_up", [2, D, DFF]); w_down = din("w_down", [2, DFF, D])
    cs_own = din("cs_own", [128, 2, NTP]); cs_past = din("cs_past", [128, 2, NPAST])
    c_ident = din("c_ident", [128, 128]); c_prot = din("c_prot", [128, 128], BF16)
    c_eall = din("c_eall", [16, 16, 128], BF16); c_cm = din("c_cm", [128, 4, 512], BF16)
    c_hb = din("c_hb", [128, 24, NH], BF16); c_addc = din("c_addc", [128, 8, 16]); c_addh = din("c_addh", [NH, 16])
    c_hv = din("c_hv", [128, NH]); c_invc = din("c_invc", [128, 4, 16])
    c_smask = din("c_smask", [4, 4]); c_esel = din("c_esel", [4, 4, 128])
    c_pio = din("c_pio", [128, 2]); c_iota64 = din("c_iota64", [128, 64])
    y_own = dout("y_own", [NOWN, D]); y_smp = dout("y_smp", [NS, D])
    k_own = dout("k_own", [NOWN, 8, 128]); v_own = dout("v_own", [NOWN, 8, 128])
    k_smp = dout("k_smp", [NS, 8, 128]); v_smp = dout("v_smp", [NS, 8, 128])
    conv_p = dout("conv_p", [2, 1024]); conv_s = dout("conv_s", [2, 1024])
    pool_p = dout("pool_p", [15, D]); pool_s = dout("pool_s", [15, D])
    kt_scr = nc.dram_tensor("kt_scr", [8, 128, 4096], BF16)
    v_scr = nc.dram_tensor("v_scr", [8, 32, 128, 128], BF16)

    sbuf_used = [0]

    def sb(name, shape, dt):
        return nc.alloc_sbuf_tensor(name, list(shape), dt)

    R1 = sb("R1", [128, NCH * NTP], F32)
    R2 = sb("R2", [128, NCH * NTP // 2], F32)
    R3 = sb("R3", [128, 11264], F32)
    xT = R1[:, :].rearrange("p (a b) -> p a b", a=NCH); t_xT = [T(f"xT{i}") for i in range(NCH)]
    hT = R2[:, :].bitcast(BF16).rearrange("p (a b) -> p a b", a=NCH); t_hT = T("hT")
    xT_scr = nc.dram_tensor("xT_scr", [128, NCH * NTP], F32)
    ident = sb("ident", [128, 128], F32); identb = sb("identb", [128, 128], BF16)
    onesb = sb("onesb", [128, 128], BF16); prot = sb("prot", [128, 128], BF16)
    eall = sb("eall", [16, 16, 128], BF16); cm = sb("cm", [128, 4, 512], BF16); hb = sb("hb", [128, 24, NH], BF16)
    addc = sb("addc", [128, 8, 16], F32); addh = sb("addh", [NH, 16], F32)
    hv = sb("hv", [128, NH], F32); invc = sb("invc", [128, 4, 16], F32)
    gall = sb("gall", [128, 5, 16], F32); convw = sb("convw", [128, 8, 3], F32); psc = sb("psc", [128, 16], F32)
    csown = sb("csown", [128, 2, NTP], F32)
    t_const = T("const")
    for dst, src in [(ident, c_ident), (prot, c_prot), (eall, c_eall), (cm, c_cm), (hb, c_hb), (addc, c_addc),
                     (addh, c_addh), (hv, c_hv), (invc, c_invc), (gall, g_all), (convw, conv_w), (psc, pscale),
                     (csown, cs_own)]:
        c.dma(sp, dst.ap(), src.ap(), writes=[t_const])
    dve.op(lambda: V.tensor_copy(identb[:, :], ident[:, :]), reads=[t_const], writes=[t_const])
    dve.op(lambda: V.memset(onesb[:, :], 1.0), writes=[t_const])
    c.barrier()

    PS = [nc.alloc_psum_tensor(f"ps{i}", [128, 512], F32) for i in range(8)]
    t_PS = [T(f"ps{i}", excl=True) for i in range(8)]

    class Arena:
        def __init__(self, base, nbytes):
            self.base = base
            self.nbytes = nbytes
            self.off = 0

        def reset(self):
            self.off = 0

        def take(self, shape, dt):
            esz = 4 if dt in (F32, I32, U32) else 2
            n = int(np.prod(shape[1:]))
            nbytes = (n * esz + 31) // 32 * 32
            assert self.off + nbytes <= self.nbytes, ("arena overflow", self.off, nbytes, self.nbytes)
            a = self.base[:, self.off // 4:(self.off + nbytes) // 4]
            self.off += nbytes
            if esz == 2:
                a = a.bitcast(BF16)[:, 0:n]
            elif dt != F32:
                a = a.bitcast(dt)[:, 0:n]
            else:
                a = a[:, 0:n]
            if len(shape) == 3:
                a = a.rearrange("p (a b) -> p a b", a=shape[1])
            if len(shape) == 4:
                a = a.rearrange("p (a b c) -> p a b c", a=shape[1], b=shape[2])
            if shape[0] < 128:
                a = a[0:shape[0]]
            return a

    A1 = Arena(R1, NCH * NTP * 4)
    A2 = Arena(R2, NCH * NTP * 2)
    A3 = Arena(R3, 11264 * 4)

    def take(ar, name, shape, dt):
        return ar.take(shape, dt)

    WB = [sb(f"wb{i}", [128, 16, 128], BF16) for i in range(4)]
    t_WB = [T(f"wb{i}") for i in range(4)]
    wb_next = [0]

    def wload(src_ap, nchunk):
        i = wb_next[0]
        wb_next[0] = (i + 1) % 4
        c.swdma(i, WB[i][:, 0:nchunk, :], src_ap.rearrange("(c p) n -> p c n", p=128), writes=[t_WB[i]])
        return WB[i], t_WB[i]

    evac_flip = [0]

    def evac(out_ap, in_ap, reads, writes):
        evac_flip[0] ^= 1
        if evac_flip[0]:
            return act.op(lambda: S.copy(out_ap, in_ap), reads=reads, writes=writes)
        return dve.op(lambda: V.tensor_copy(out_ap, in_ap), reads=reads, writes=writes)

    def load_transpose(src_rows_ap, nrows, dstT, t_dst, col0, stage, t_stage, psb):
        c.dma(sp, stage[0:nrows, :], src_rows_ap, writes=[t_stage])
        for q4 in range(4):
            b = psb[q4 % len(psb)]
            pe.group([(lambda k=k: PEn.transpose(PS[b][:, k * 128:k * 128 + nrows],
                                                  stage[0:nrows, (q4 * 4 + k) * 128:(q4 * 4 + k + 1) * 128],
                                                  ident[0:nrows, 0:nrows])) for k in range(4)],
                     reads=[t_stage, t_const], writes=[t_PS[b]])
            wr = t_dst[q4 * 4:q4 * 4 + 4] if isinstance(t_dst, list) else [t_dst]
            evac(dstT[:, q4 * 4:q4 * 4 + 4, col0:col0 + nrows],
                 PS[b][:, :].rearrange("p (k n) -> p k n", k=4)[:, :, 0:nrows], [t_PS[b]], wr)

    def rmsnorm_fm(srcT, t_src, cols, gidx, out_fn, tmp_sq, t_sq, rstd, t_rstd, psb):
        c0, n = cols
        rd = t_src if isinstance(t_src, list) else [t_src]
        fns = []
        for ch in range(NCH):
            k = ch % 2
            act.op(lambda ch=ch, k=k: S.activation(tmp_sq[k][:, 0:n], srcT[:, ch, c0:c0 + n], AF.Square),
                   reads=[rd[ch] if len(rd) > 1 else rd[0]], writes=[t_sq[k]])
            pe.deps([t_sq[k], t_const], [t_PS[psb]] if ch == 0 else [])
            ins = PEn.matmul(PS[psb][:, 0:n], onesb[:, :], tmp_sq[k][:, 0:n], start=(ch == 0), stop=(ch == NCH - 1))
            pe.count += 1
            ins.then_inc(pe.sem, 1)
            tok = (pe, pe.count)
            pe.commit(tok, [t_sq[k]], [t_PS[psb]] if ch == NCH - 1 else [])
            if ch != NCH - 1:
                t_PS[psb].w = tok
        act.op(lambda: S.activation(rstd[:, 0:n], PS[psb][:, 0:n], AF.Sqrt, bias=eps_ap[:, 0:1], scale=1.0 / D),
               reads=[t_PS[psb], t_const], writes=[t_rstd])
        dve.op(lambda: V.reciprocal(rstd[:, 0:n], rstd[:, 0:n]), reads=[t_rstd], writes=[t_rstd])
        for ch in range(NCH):
            out_fn(ch, rstd[:, 0:n])

    eps_ap = sb("eps", [128, 1], F32)
    dve.op(lambda: V.memset(eps_ap[:, :], 1e-6), writes=[t_const])

    def proj(wt, t_w, nchunk, rhs_fn, t_rhs, banks, tiles=TOKT):
        for (c0, n), b in zip(tiles, banks):
            pe.group([(lambda ch=ch, c0=c0, n=n, b=b: PEn.matmul(PS[b][:, 0:n], wt[:, ch, :], rhs_fn(ch, c0, n),
                                                              start=(ch == 0), stop=(ch == nchunk - 1)))
                      for ch in range(nchunk)], reads=[t_w] + list(t_rhs), writes=[t_PS[b]])

    def rope(psb, n, cos_ap, sin_ap, out_ap, t_out, tmp, t_tmp, rotb, t_cs=None):
        t_cs = t_cs or t_const
        act.op(lambda: S.copy(tmp["qb"][:, 0:n], PS[psb][:, 0:n]), reads=[t_PS[psb]], writes=[t_tmp["qb"]])
        pe.group([lambda: PEn.matmul(PS[rotb][:, 0:n], prot[:, :], tmp["qb"][:, 0:n], start=True, stop=True)],
                 reads=[t_tmp["qb"], t_const], writes=[t_PS[rotb]])
        dve.op(lambda: V.tensor_tensor(tmp["t1"][:, 0:n], PS[psb][:, 0:n], cos_ap, ALU.mult),
               reads=[t_PS[psb], t_cs], writes=[t_tmp["t1"]])
        dve.op(lambda: V.tensor_tensor(tmp["t2"][:, 0:n], PS[rotb][:, 0:n], sin_ap, ALU.mult),
               reads=[t_PS[rotb], t_cs], writes=[t_tmp["t2"]])
        dve.op(lambda: V.tensor_tensor(out_ap, tmp["t1"][:, 0:n], tmp["t2"][:, 0:n], ALU.add),
               reads=[t_tmp["t1"], t_tmp["t2"]], writes=[t_out])

    ksum = sb("ksum", [128, 8, 16], F32); t_ksum = T("ksum")
    sq2 = [sb(f"sq{i}", [128, 512], BF16) for i in range(2)]; t_sq2 = [T("sq0"), T("sq1")]
    rstd = sb("rstd", [128, 512], F32); t_rstd = T("rstd")
    rt = {"qb": sb("r_qb", [128, 512], BF16), "t1": sb("r_t1", [128, 512], F32), "t2": sb("r_t2", [128, 512], F32)}
    t_rt = {k: T(k) for k in rt}

    if STAGE >= 2:
        A1.reset(); A2.reset(); A3.reset()
        hpT = A1.take([128, NCH, 1536], BF16); t_hpT = T("hpT")
        xpT = A2.take([128, NCH, 512], F32); t_xpT = T("xpT")
        stage2 = [A3.take([128, D], F32) for i in range(2)]; t_stage2 = [T("st0"), T("st1")]
        cspast = A3.take([128, 3, 2, 512], F32); t_csp = T("csp")
        cntb = [0]
        kf = A3.take([128, 512], F32); t_kf = T("kf")
        kst = [A3.take([128, 512], BF16) for i in range(2)]; t_kst = [T("kst0"), T("kst1")]
        vst = [A3.take([128, 4, 128], BF16) for i in range(2)]; t_vst = [T("vst0"), T("vst1")]
        vtb = A3.take([128, 512], BF16); t_vtb = T("vtb")
        cnt = 0
        for grp in range(2):
            for tl in range(3):
                tok0 = grp * 1536 + tl * 512
                for sub in range(4):
                    k = cnt % 2; cnt += 1
                    load_transpose(x_past[tok0 + sub * 128: tok0 + (sub + 1) * 128, :], 128, xpT, t_xpT, sub * 128,
                                   stage2[k], t_stage2[k], [4, 5, 6, 7])

                def out_fn(ch, rs, tl=tl):
                    dve.op(lambda: V.scalar_tensor_tensor(hpT[:, ch, tl * 512:(tl + 1) * 512], xpT[:, ch, :],
                                                          gall[:, 0, ch:ch + 1], rs, ALU.mult, ALU.mult),
                           reads=[t_xpT, t_rstd, t_const], writes=[t_hpT])
                rmsnorm_fm(xpT, t_xpT, (0, 512), 0, out_fn, sq2, t_sq2, rstd, t_rstd, 3)
            for tl in range(3):
                tok0 = grp * 1536 + tl * 512
                c.dma(sp, cspast[:, tl, :, :], cs_past[:, :, tok0:tok0 + 512], writes=[t_csp])
            ptiles = [(tl * 512, 512) for tl in range(3)]
            pend = None
            psi = 0
            for which in (1, 2):
                for h in range(8):
                    wt, t_w = wload(w_in[:, which * 1024 + h * 128: which * 1024 + (h + 1) * 128], NCH)
                    banks = [[0, 1, 2], [4, 5, 6]][psi]; tb = [3, 7][psi]; psi ^= 1
                    proj(wt, t_w, NCH, lambda ch, c0, n: hpT[:, ch, c0:c0 + n], [t_hpT], banks, tiles=ptiles)
                    if pend is not None:
                        pend()

                    def post(which=which, h=h, banks=banks, tb=tb, grp=grp):
                        for tl in range(3):
                            tok0 = grp * 1536 + tl * 512
                            b = banks[tl]
                            kk = cntb[0] % 2; cntb[0] += 1
                            if which == 1:
                                rope(b, 512, cspast[:, tl, 0, :], cspast[:, tl, 1, :], kf[:, :], t_kf, rt, t_rt, tb, t_cs=t_csp)
                                act.op(lambda kk=kk: S.copy(kst[kk][:, :], kf[:, :]), reads=[t_kf], writes=[t_kst[kk]])
                                c.dma(sp, kt_scr[h, :, tok0:tok0 + 512], kst[kk][:, :], reads=[t_kst[kk]])
                                sb0 = tok0 // 256
                                dve.op(lambda sb0=sb0, h=h: V.tensor_reduce(ksum[:, h, sb0:sb0 + 2],
                                                                            kf[:, :].rearrange("p (a b) -> p a b", a=2), AX.X, ALU.add),
                                       reads=[t_kf], writes=[t_ksum])
                            else:
                                act.op(lambda b=b: S.copy(vtb[:, :], PS[b][:, :]), reads=[t_PS[b]], writes=[t_vtb])
                                pb16 = PS[tb][:, :].bitcast(BF16)
                                pe.group([(lambda s4=s4, pb16=pb16: PEn.transpose(pb16[:, s4 * 128:(s4 + 1) * 128],
                                                                                   vtb[:, s4 * 128:(s4 + 1) * 128], identb[:, :]))
                                          for s4 in range(4)], reads=[t_vtb, t_const], writes=[t_PS[tb]])
                                dve.op(lambda kk=kk, pb16=pb16: V.tensor_copy(vst[kk][:, :, :].rearrange("p a b -> p (a b)"), pb16[:, 0:512]),
                                       reads=[t_PS[tb]], writes=[t_vst[kk]])
                                c.dma(sp, v_scr[h, tok0 // 128: tok0 // 128 + 4, :, :].rearrange("s p d -> p s d"), vst[kk][:, :, :],
                                      reads=[t_vst[kk]])
                    pend = post
            pend()
        c.barrier()

    A3.reset()
    stage2 = [A3.take([128, D], F32) for i in range(2)]; t_stage2 = [T("st0"), T("st1")]
    dve.op(lambda: V.memset(xT[:, :, NT:NTP], 0.0), writes=t_xT)
    dve.op(lambda: V.memset(hT[:, :, NT:NTP], 0.0), writes=[t_hT])
    for sub in range(8):
        load_transpose(x_own[sub * 128:(sub + 1) * 128, :], 128, xT, t_xT, sub * 128, stage2[sub % 2], t_stage2[sub % 2],
                       [4, 5, 6, 7])
    load_transpose(x_c[:, :], NC_, xT, t_xT, NOWN, stage2[0], t_stage2[0], [4, 5, 6, 7])

    def norm_to_hT(gidx):
        for (c0, n) in TOKT:
            def out_fn(ch, rs, c0=c0, n=n):
                dve.op(lambda: V.scalar_tensor_tensor(hT[:, ch, c0:c0 + n], xT[:, ch, c0:c0 + n],
                                                      gall[:, gidx, ch:ch + 1], rs, ALU.mult, ALU.mult),
                       reads=[t_xT[ch], t_rstd, t_const], writes=[t_hT])
            rmsnorm_fm(xT, t_xT, (c0, n), gidx, out_fn, sq2, t_sq2, rstd, t_rstd, 3)

    norm_to_hT(0)
    c.dma(sp, xT_scr.ap(), R1[:, :], reads=t_xT)
    c.barrier()
    A1.reset(); A3.reset()
    QT = A1.take([128, 8, NTP], BF16); t_QT = T("QT")
    YC = A3.take([128, 8, NTP], BF16); t_YC = T("YC")
    kf = A3.take([128, 512], F32); t_kf = T("kf")
    kst = [A3.take([128, 512], BF16) for i in range(2)]; t_kst = [T("kst0"), T("kst1")]
    vst = [A3.take([128, 4, 128], BF16) for i in range(2)]; t_vst = [T("vst0"), T("vst1")]

    hrhs = lambda ch, c0, n: hT[:, ch, c0:c0 + n]
    QsT = sb("QsT", [128, 8, NS], F32); KsT = sb("KsT", [128, 8, NS], F32); t_sm = T("smp")
    Vs = sb("Vs", [NS, 8, 128], F32); Ks = sb("Ks", [NS, 8, 128], F32); Qs = sb("Qs", [NS, 8, 128], F32)
    ost = [A3.take([128, 4, 128], F32) for i in range(2)]; t_ost = [T("ost0"), T("ost1")]
    vf = A3.take([128, 512], F32); t_vf = T("vf")
    ulast = sb("ulast", [128, 8, 2], F32); uslast = sb("uslast", [128, 8, 2], F32); t_ulast = T("ulast")
    stcT = sb("stcT", [128, 8, 2], F32)
    for t_ in range(2):
        c.dma(sp, stcT[:, :, t_], st_conv[t_, :].rearrange("(c p) -> p c", p=128), writes=[t_const],
              allow_slow_non_contiguous=True)
    gcs = A3.take([128, NTP], F32); t_gcs = T("gcs")
    uext = A3.take([128, NOWN + 2], F32); usx = sb("usx", [128, NS + 2], F32); t_u = T("u")
    cva = A3.take([128, NOWN], F32); cvs = sb("cvs", [128, NS], F32); t_cv = T("cv")
    uh = sb("uh", [128, NH], F32); cvh = sb("cvh", [128, NH], F32)
    cnt = 0
    bank_sets = [[0, 1, 2], [4, 5, 6]]
    bsi = 0
    cnto = [0]
    pend = None
    for h in range(8):
        for which in range(3):
            wt, t_w = wload(w_in[:, which * 1024 + h * 128: which * 1024 + (h + 1) * 128], NCH)
            banks = bank_sets[bsi]; bsi ^= 1
            tb = 3 if banks[0] == 0 else 7
            proj(wt, t_w, NCH, hrhs, [t_hT], banks)
            if pend is not None:
                pend()

            def post(h=h, which=which, banks=banks, tb=tb):
                for (c0, n), b in zip(TOKT, banks):
                    cosap, sinap = csown[:, 0, c0:c0 + n], csown[:, 1, c0:c0 + n]
                    if which == 0:
                        rope(b, n, cosap, sinap, QT[:, h, c0:c0 + n], t_QT, rt, t_rt, tb)
                        if n == NC_:
                            dve.op(lambda h=h: V.tensor_tensor(QsT[:, h, :], rt["t1"][:, 0:NS], rt["t2"][:, 0:NS], ALU.add),
                                   reads=[t_rt["t1"], t_rt["t2"]], writes=[t_sm])
                            pe.group([lambda h=h: PEn.transpose(PS[tb][0:NS, 0:128], QsT[:, h, :], ident[:, :])],
                                     reads=[t_sm, t_const], writes=[t_PS[tb]])
                            dve.op(lambda h=h: V.tensor_copy(Qs[:, h, :], PS[tb][0:NS, 0:128]), reads=[t_PS[tb]], writes=[t_sm])
                    elif which == 1:
                        rope(b, n, cosap, sinap, kf[:, 0:n], t_kf, rt, t_rt, tb)
                        if n == 512:
                            kk = cnto[0] % 2; cnto[0] += 1
                            act.op(lambda kk=kk: S.copy(kst[kk][:, :], kf[:, :]), reads=[t_kf], writes=[t_kst[kk]])
                            c.dma(sp, kt_scr[h, :, NPAST + c0:NPAST + c0 + 512], kst[kk][:, :], reads=[t_kst[kk]])
                            sb0 = 12 + c0 // 256
                            dve.op(lambda sb0=sb0, h=h: V.tensor_reduce(ksum[:, h, sb0:sb0 + 2],
                                                                        kf[:, :].rearrange("p (a b) -> p a b", a=2), AX.X, ALU.add),
                                   reads=[t_kf], writes=[t_ksum])
                            pe.group([(lambda s4=s4: PEn.transpose(PS[tb][:, s4 * 128:(s4 + 1) * 128], kf[:, s4 * 128:(s4 + 1) * 128],
                                                                    ident[:, :])) for s4 in range(4)],
                                     reads=[t_kf, t_const], writes=[t_PS[tb]])
                            kk = cnto[0] % 2; cnto[0] += 1
                            evac(ost[kk][:, :, :].rearrange("p a b -> p (a b)"), PS[tb][:, :], [t_PS[tb]], [t_ost[kk]])
                            c.dma(sp, k_own[c0:c0 + 512, h, :].rearrange("(s p) d -> p s d", p=128), ost[kk][:, :, :], reads=[t_ost[kk]])
                        else:
                            dve.op(lambda h=h: V.tensor_copy(KsT[:, h, :], kf[:, 0:NS]), reads=[t_kf], writes=[t_sm])
                            pe.group([lambda h=h: PEn.transpose(PS[tb][0:NS, 0:128], KsT[:, h, :], ident[:, :])],
                                     reads=[t_sm, t_const], writes=[t_PS[tb]])
                            dve.op(lambda h=h: V.tensor_copy(Ks[:, h, :], PS[tb][0:NS, 0:128]), reads=[t_PS[tb]], writes=[t_sm])
                    else:
                        evac(vf[:, 0:n], PS[b][:, 0:n], [t_PS[b]], [t_vf])
                        if n == 512:
                            pe.group([(lambda s4=s4: PEn.transpose(PS[tb][:, s4 * 128:(s4 + 1) * 128], vf[:, s4 * 128:(s4 + 1) * 128],
                                                                    ident[:, :])) for s4 in range(4)],
                                     reads=[t_vf, t_const], writes=[t_PS[tb]])
                            kk = cnto[0] % 2; cnto[0] += 1
                            evac(ost[kk][:, :, :].rearrange("p a b -> p (a b)"), PS[tb][:, :], [t_PS[tb]], [t_ost[kk]])
                            c.dma(sp, v_own[c0:c0 + 512, h, :].rearrange("(s p) d -> p s d", p=128), ost[kk][:, :, :], reads=[t_ost[kk]])
                            dve.op(lambda kk=kk: V.tensor_copy(vst[kk][:, :, :].rearrange("p a b -> p (a b)"), PS[tb][:, :]),
                                   reads=[t_PS[tb]], writes=[t_vst[kk]])
                            st = (NPAST + c0) // 128
                            c.dma(sp, v_scr[h, st:st + 4, :, :].rearrange("s p d -> p s d"), vst[kk][:, :, :], reads=[t_vst[kk]])
                        else:
                            pe.group([lambda: PEn.transpose(PS[tb][0:NS, 0:128], vf[:, 0:NS], ident[:, :])],
                                     reads=[t_vf, t_const], writes=[t_PS[tb]])
                            dve.op(lambda h=h: V.tensor_copy(Vs[:, h, :], PS[tb][0:NS, 0:128]), reads=[t_PS[tb]], writes=[t_sm])

            pend = post
    pend()
    c.dma(sp, k_smp[:, :, :], Ks[:, :, :], reads=[t_sm])
    c.dma(sp, v_smp[:, :, :], Vs[:, :, :], reads=[t_sm])

    for cc in range(8):
        wts = [wload(w_in[:, 3072 + which * 1024 + cc * 128: 3072 + which * 1024 + (cc + 1) * 128], NCH) for which in (1, 2, 0)]
        banks = bank_sets[bsi]; bsi ^= 1
        proj(wts[0][0], wts[0][1], NCH, hrhs, [t_hT], banks)
        for (c0, n), b in zip(TOKT, banks):
            evac(gcs[:, c0:c0 + n], PS[b][:, 0:n], [t_PS[b]], [t_gcs])
        banks = bank_sets[bsi]; bsi ^= 1
        proj(wts[1][0], wts[1][1], NCH, hrhs, [t_hT], banks)
        for (c0, n), b in zip(TOKT, banks):
            if n == 512:
                dve.op(lambda c0=c0, b=b: V.tensor_tensor(uext[:, 2 + c0:2 + c0 + 512], PS[b][:, 0:512], gcs[:, c0:c0 + 512], ALU.mult),
                       reads=[t_PS[b], t_gcs], writes=[t_u])
            else:
                dve.op(lambda b=b: V.tensor_tensor(usx[:, 2:2 + NS], PS[b][:, 0:NS], gcs[:, NOWN:NOWN + NS], ALU.mult),
                       reads=[t_PS[b], t_gcs], writes=[t_u])
                dve.op(lambda b=b: V.tensor_tensor(uext[:, 0:2], PS[b][:, NC_ - 2:NC_], gcs[:, NT - 2:NT], ALU.mult),
                       reads=[t_PS[b], t_gcs], writes=[t_u])
                dve.op(lambda b=b: V.tensor_tensor(uh[:, 0:NH], PS[b][:, NS:NC_], gcs[:, NOWN + NS:NT], ALU.mult),
                       reads=[t_PS[b], t_gcs], writes=[t_u])
                dve.op(lambda cc=cc: V.tensor_copy(usx[:, 0:2], stcT[:, cc, :]), reads=[t_const], writes=[t_u])
        dve.op(lambda cc=cc: V.tensor_copy(ulast[:, cc, :], uext[:, NOWN:NOWN + 2]), reads=[t_u], writes=[t_ulast])
        dve.op(lambda cc=cc: V.tensor_copy(uslast[:, cc, :], usx[:, NS:NS + 2]), reads=[t_u], writes=[t_ulast])
        for (ux, cv, n) in ((uext, cva, NOWN), (usx, cvs, NS), (uh, cvh, NH - 2)):
            dve.op(lambda ux=ux, cv=cv, n=n, cc=cc: V.tensor_scalar(cv[:, 0:n], ux[:, 0:n], convw[:, cc, 0:1], None, ALU.mult),
                   reads=[t_u, t_const], writes=[t_cv])
            for j in (1, 2):
                dve.op(lambda ux=ux, cv=cv, n=n, cc=cc, j=j: V.scalar_tensor_tensor(cv[:, 0:n], ux[:, j:j + n], convw[:, cc, j:j + 1],
                                                                                 cv[:, 0:n], ALU.mult, ALU.add),
                       reads=[t_u, t_cv, t_const], writes=[t_cv])
        banks = bank_sets[bsi]; bsi ^= 1
        proj(wts[2][0], wts[2][1], NCH, hrhs, [t_hT], banks)
        for (c0, n), b in zip(TOKT, banks):
            if n == 512:
                dve.op(lambda c0=c0, b=b, cc=cc: V.tensor_tensor(YC[:, cc, c0:c0 + 512], PS[b][:, 0:512], cva[:, c0:c0 + 512], ALU.mult),
                       reads=[t_PS[b], t_cv], writes=[t_YC])
            else:
                dve.op(lambda b=b, cc=cc: V.tensor_tensor(YC[:, cc, NOWN:NOWN + NS], PS[b][:, 0:NS], cvs[:, 0:NS], ALU.mult),
                       reads=[t_PS[b], t_cv], writes=[t_YC])
                dve.op(lambda b=b, cc=cc: V.tensor_tensor(YC[:, cc, NOWN + NS + 2:NT], PS[b][:, NS + 2:NC_], cvh[:, 0:NH - 2], ALU.mult),
                       reads=[t_PS[b], t_cv], writes=[t_YC])
                dve.op(lambda cc=cc: V.memset(YC[:, cc, NOWN + NS:NOWN + NS + 2], 0.0), writes=[t_YC])
    for t_ in range(2):
        c.dma(sp, conv_p[t_, :].rearrange("(c p) -> p c", p=128), ulast[:, :, t_], reads=[t_ulast], allow_slow_non_contiguous=True)
        c.dma(sp, conv_s[t_, :].rearrange("(c p) -> p c", p=128), uslast[:, :, t_], reads=[t_ulast], allow_slow_non_contiguous=True)
    c.barrier()

    AT = hT
    t_AT = T("AT")
    if STAGE >= 3:
        kmT = sb("kmT", [128, 8, 16], BF16); t_km = T("km")
        dve.op(lambda: V.tensor_scalar(kmT[:, :, :], ksum[:, :, :], 1.0 / 256.0, None, ALU.mult), reads=[t_ksum], writes=[t_km])
        BT = A1.take([16, 8, NTP], BF16); t_BT = T("BT")
        g2 = sb("g2", [128, 8, 16], F32); top8 = sb("top8", [128, 8, 8], F32); thr = sb("thr", [128, 8], F32)
        selb = sb("selb", [128, 8, 16], BF16); t_g = T("g")
        qtiles = [(i * 128, 128, addc[:, i, :]) for i in range(8)] + [(NOWN + NS, NH, addh[:, :])]
        for (q0, qn, adc) in qtiles:
            gb = 3
            for h in range(8):
                pe.group([lambda h=h: PEn.matmul(PS[gb][0:qn, h * 16:(h + 1) * 16], QT[:, h, q0:q0 + qn], kmT[:, h, :], start=True, stop=True)],
                         reads=[t_QT, t_km], writes=[t_PS[gb]])
            dve.op(lambda: V.tensor_tensor(g2[0:qn, :, :], PS[gb][0:qn, 0:128].rearrange("p (a b) -> p a b", a=8),
                                           adc.unsqueeze(1).broadcast_to([qn, 8, 16]), ALU.add),
                   reads=[t_PS[gb], t_const], writes=[t_g])
            for h in range(8):
                dve.op(lambda h=h: V.max(top8[0:qn, h, :], g2[0:qn, h, :]), reads=[t_g], writes=[t_g])
            dve.op(lambda: V.tensor_scalar(thr[0:qn, :], top8[0:qn, :, 3], -1e29, None, ALU.max), reads=[t_g], writes=[t_g])
            dve.op(lambda: V.tensor_tensor(g2[0:qn, :, :], g2[0:qn, :, :], thr[0:qn, :].unsqueeze(2).broadcast_to([qn, 8, 16]), ALU.is_ge),
                   reads=[t_g], writes=[t_g])
            dve.op(lambda: V.tensor_scalar(selb[0:qn, :, :], g2[0:qn, :, :], -NEGB, NEGB, ALU.mult, ALU.add), reads=[t_g], writes=[t_g])
            pb16 = PS[7][:, :].bitcast(BF16)
            pe.group([(lambda h=h: PEn.transpose(pb16[0:16, h * 128:h * 128 + qn], selb[0:qn, h, :], identb[0:qn, 0:qn])) for h in range(8)],
                     reads=[t_g, t_const], writes=[t_PS[7]])
            dve.op(lambda: V.tensor_copy(BT[:, :, q0:q0 + qn], pb16[0:16, :].rearrange("p (a b) -> p a b", a=8)[:, :, 0:qn]),
                   reads=[t_PS[7]], writes=[t_BT])

        KT2 = [A1.take([128, 4096], BF16) for _ in range(2)]; t_KT2 = [T("KTa"), T("KTb")]
        VV2 = [A1.take([128, 32, 128], BF16) for _ in range(2)]; t_VV2 = [T("Va"), T("Vb")]
        A3.off = (8 * NTP * 2 + 31) // 32 * 32
        PT = [A3.take([128, 512], BF16) for i in range(3)]; t_PT = [T(f"PT{i}") for i in range(3)]
        rden = A3.take([128, 512], F32); t_rden = T("rden")
        pti = 0
        sctr = [0]
        for h in range(8):
            kb = h % 2
            c.dma(sp, KT2[kb][:, :], kt_scr[h, :, :], writes=[t_KT2[kb]])
            c.dma(sp, VV2[kb][:, :, :], v_scr[h, :, :, :].rearrange("s p d -> p s d"), writes=[t_VV2[kb]])
            for qi, (q0, qn, nkt) in enumerate([(0, 512, 28), (512, 512, 32), (NOWN + NS, NH, 24)]):
                ob, db = 4, 5

                def emit_S(kt, h=h, kb=kb, qi=qi, q0=q0, qn=qn):
                    sbk = sctr[0] % 3; sctr[0] += 1
                    fns = [lambda: PEn.matmul(PS[sbk][:, 0:qn], KT2[kb][:, kt * 128:(kt + 1) * 128], QT[:, h, q0:q0 + qn],
                                              start=True, stop=False)]
                    extra = None
                    if qi == 2:
                        extra = hb[:, kt, :]
                    elif kt >= 24:
                        off = (kt - 24) * 128 - q0
                        if off >= 0:
                            extra = cm[:, off // 128, :]
                    fns.append(lambda: PEn.matmul(PS[sbk][:, 0:qn], eall[:, kt // 2, :], BT[:, h, q0:q0 + qn],
                                                  start=False, stop=(extra is None)))
                    if extra is not None:
                        fns.append(lambda: PEn.matmul(PS[sbk][:, 0:qn], identb[:, :], extra, start=False, stop=True))
                    pe.group(fns, reads=[t_KT2[kb], t_QT, t_BT, t_const], writes=[t_PS[sbk]])
                    return sbk

                LOOK = 2
                sb_of = {}
                for kt in range(min(LOOK, nkt)):
                    sb_of[kt] = emit_S(kt)
                for kt in range(nkt):
                    if kt + LOOK < nkt:
                        sb_of[kt + LOOK] = emit_S(kt + LOOK)
                    sbk = sb_of.pop(kt)
                    p = pti; pti = (pti + 1) % 3
                    act.op(lambda sbk=sbk, p=p: S.activation(PT[p][:, 0:qn], PS[sbk][:, 0:qn], AF.Exp, scale=SCALE),
                           reads=[t_PS[sbk]], writes=[t_PT[p]])
                    pe.deps([t_PT[p], t_VV2[kb], t_const], [t_PS[ob], t_PS[db]] if kt == 0 else [])
                    PEn.matmul(PS[ob][:, 0:qn], VV2[kb][:, kt, :], PT[p][:, 0:qn], start=(kt == 0), stop=(kt == nkt - 1))
                    ins = PEn.matmul(PS[db][:, 0:qn], onesb[:, :], PT[p][:, 0:qn], start=(kt == 0), stop=(kt == nkt - 1))
                    pe.count += 1
                    ins.then_inc(pe.sem, 1)
                    tok = (pe, pe.count)
                    pe.commit(tok, [t_PT[p], t_VV2[kb]], [])
                    t_PS[ob].w = tok; t_PS[db].w = tok
                    if kt == 0:
                        t_PS[ob].rd = []; t_PS[db].rd = []
                dve.op(lambda: V.reciprocal(rden[:, 0:qn], PS[db][:, 0:qn]), reads=[t_PS[db]], writes=[t_rden])
                dve.op(lambda h=h: V.tensor_tensor(AT[:, h, q0:q0 + qn], PS[ob][:, 0:qn], rden[:, 0:qn], ALU.mult),
                       reads=[t_PS[ob], t_rden], writes=[t_AT])
    else:
        dve.op(lambda: V.memset(AT[:, 0:8, :], 0.0), writes=[t_AT])
    if STAGE >= 5:
        c.barrier()
        A1.reset()
        kpg = [A1.take([128, 2, 1024], F32) for _ in range(2)]; t_kpg = [T("kpg0"), T("kpg1")]
        Kg = [A1.take([128, 24, 128], F32) for _ in range(2)]; t_Kg = [T("Kg0"), T("Kg1")]
        Vg = [A1.take([128, 24, 128], F32) for _ in range(2)]; t_Vg = [T("Vg0"), T("Vg1")]
        A3.off = (8 * NTP * 2 + 31) // 32 * 32
        kmS = A3.take([128, 8, 64], F32); t_kmS = T("kmS")
        G = A3.take([128, 32, 64], F32); top8s = A3.take([128, 32, 8], F32); idxs = A3.take([128, 32, 8], U32); t_G = T("G")
        nf = A3.take([128, 32, 3], F32)
        OH = A3.take([128, 12, 64], F32); PR = A3.take([128, 12, 64], F32); physf = A3.take([128, 12, 2], F32); t_OH = T("OH")
        idxG = A3.take([128, 8, 24], I32); t_idxG = T("idxG")
        Qrep = [A3.take([128, 128], F32) for _ in range(2)]; t_Qrep = [T("qr0"), T("qr1")]
        qb = A3.take([128, 4, 128], F32); t_qb = T("qb")
        junk = A3.take([128, 128], F32); t_junk = T("junk")
        Ssm = A3.take([128, 24], F32); Psm = A3.take([128, 24], F32); t_S = T("Ssm"); t_P = T("Psm")
        Sown = A3.take([4, 4], F32); Pown = A3.take([4, 4], F32)
        den = A3.take([128, 4], F32); rd = A3.take([128, 4], F32); t_den = T("den")
        ptb = sb("ptb", [128, 128], I32); ptf = sb("ptf", [128, 128], F32); idxP = sb("idxP", [128, 128], I32); t_pt = T("pt")
        pio = sb("pio", [128, 2], F32); iota64 = sb("iota64", [128, 64], F32)
        onesf = sb("onesf", [128, 128], F32)
        esel = sb("esel", [4, 4, 128], F32); smask = sb("smask", [4, 4], F32)
        dve.op(lambda: V.memset(onesf[:, :], 1.0), writes=[t_const])
        c.dma(sp, esel[:, :, :], c_esel.ap(), writes=[t_const])
        c.dma(sp, smask[:, :], c_smask.ap(), writes=[t_const])
        c.dma(sp, pio[:, :], c_pio.ap(), writes=[t_const])
        c.dma(sp, iota64[:, :], c_iota64.ap(), writes=[t_const])
        c.dma(sp, ptb[:, :], ptab.ap().partition_broadcast(128), writes=[t_pt])
        dve.op(lambda: V.tensor_copy(ptf[:, :], ptb[:, :]), reads=[t_pt], writes=[t_pt])
        dve.op(lambda: V.tensor_scalar(idxP[:, :], ptf[:, :], 128.0, pio[:, 0:1], ALU.mult, ALU.add), reads=[t_pt, t_const], writes=[t_pt])
        ck_rows = cache_k.ap().rearrange("n t h d -> (n t) (h d)")
        ck_hrows = cache_k.ap().rearrange("n t h d -> (n t h) d")
        cv_hrows = cache_v.ap().rearrange("n t h d -> (n t h) d")
        for n in range(64):
            kb = n % 2
            for pg in range(2):
                c.swdma(4 + kb, kpg[kb][:, pg, :], ck_rows, reads=[t_pt], writes=[t_kpg[kb]],
                        indirect=idxP[:, 2 * n + pg:2 * n + pg + 1])
            for h in range(8):
                col = h * 64 + n
                pe.group([lambda kb=kb, h=h, col=col: PEn.matmul(PS[0][:, col:col + 1], kpg[kb][:, 0, h * 128:(h + 1) * 128], onesf[:, 0:1],
                                                                 start=True, stop=False),
                          lambda kb=kb, h=h, col=col: PEn.matmul(PS[0][:, col:col + 1], kpg[kb][:, 1, h * 128:(h + 1) * 128], onesf[:, 0:1],
                                                                 start=False, stop=True)],
                         reads=[t_kpg[kb], t_const], writes=[t_PS[0]])
        dve.op(lambda: V.tensor_scalar(kmS[:, :, :].rearrange("p a b -> p (a b)"), PS[0][:, :], 1.0 / 256.0, None, ALU.mult),
               reads=[t_PS[0]], writes=[t_kmS])
        for hq in range(32):
            h, q = hq // 4, hq % 4
            bank = 4 + hq // 8; col = (hq % 8) * 64
            k2 = hq % 2
            dve.op(lambda h=h, q=q, k2=k2: V.tensor_copy(Qrep[k2][:, :], QsT[:, h, q:q + 1].broadcast_to([128, 128])),
                   reads=[t_sm], writes=[t_Qrep[k2]])
            pe.group([lambda h=h, k2=k2, bank=bank, col=col: PEn.matmul(PS[bank][:, col:col + 64], Qrep[k2][:, :], kmS[:, h, :],
                                                                        start=True, stop=True)],
                     reads=[t_Qrep[k2], t_kmS], writes=[t_PS[bank]])
        for b4 in range(4):
            dve.op(lambda b4=b4: V.tensor_copy(G[:, b4 * 8:(b4 + 1) * 8, :].rearrange("p a b -> p (a b)"), PS[4 + b4][:, :]),
                   reads=[t_PS[4 + b4]], writes=[t_G])
        for hq in range(32):
            dve.op(lambda hq=hq: V.max(top8s[:, hq, :], G[:, hq, :]), reads=[t_G], writes=[t_G])
            dve.op(lambda hq=hq: V.max_index(idxs[:, hq, :], top8s[:, hq, :], G[:, hq, :]), reads=[t_G], writes=[t_G])
        dve.op(lambda: V.tensor_copy(nf[:, :, :], idxs[:, :, 0:3]), reads=[t_G], writes=[t_G])
        for h in range(8):
            dve.op(lambda h=h: V.tensor_tensor(OH[:, :, :], iota64[:, :].unsqueeze(1).broadcast_to([128, 12, 64]),
                                               nf[:, h * 4:(h + 1) * 4, :].rearrange("p a b -> p (a b)").unsqueeze(2).broadcast_to([128, 12, 64]),
                                               ALU.is_equal), reads=[t_G, t_const], writes=[t_OH])
            for pgi in range(2):
                dve.op(lambda pgi=pgi: V.tensor_tensor(PR[:, :, :], OH[:, :, :],
                                                       ptf[:, :].rearrange("p (n g) -> p n g", g=2)[:, :, pgi].unsqueeze(1).broadcast_to([128, 12, 64]),
                                                       ALU.mult), reads=[t_OH, t_pt], writes=[t_OH])
                dve.op(lambda pgi=pgi: V.tensor_reduce(physf[:, :, pgi], PR[:, :, :], AX.X, ALU.add), reads=[t_OH], writes=[t_OH])
            dve.op(lambda h=h: V.tensor_scalar(idxG[:, h, :], physf[:, :, :].rearrange("p a b -> p (a b)"), 1024.0, pio[:, 1:2], ALU.mult, ALU.add),
                   reads=[t_OH, t_const], writes=[t_idxG])
        for h in range(8):
            gb = h % 2
            for s_ in range(24):
                c.swdma(6 + gb, Kg[gb][:, s_, :], ck_hrows, reads=[t_idxG], writes=[t_Kg[gb]], indirect=idxG[:, h, s_:s_ + 1],
                        element_offset=h * 128)
            for s_ in range(24):
                c.swdma(8 + gb, Vg[gb][:, s_, :], cv_hrows, reads=[t_idxG], writes=[t_Vg[gb]], indirect=idxG[:, h, s_:s_ + 1],
                        element_offset=h * 128)
            pe.group([(lambda q=q, h=h: PEn.matmul(PS[1][:, q * 128:(q + 1) * 128], esel[0:4, q, :], Qs[0:4, h, :], start=True, stop=True))
                      for q in range(4)], reads=[t_sm, t_const], writes=[t_PS[1]])
            act.op(lambda: S.copy(qb[:, :, :].rearrange("p a b -> p (a b)"), PS[1][:, :]), reads=[t_PS[1]], writes=[t_qb])
            for s_ in range(24):
                dve.op(lambda s_=s_, gb=gb: V.scalar_tensor_tensor(junk[:, :], Kg[gb][:, s_, :], SCALE, qb[:, s_ // 6, :], ALU.mult, ALU.mult,
                                                                   accum_out=Ssm[:, s_:s_ + 1]),
                       reads=[t_Kg[gb], t_qb], writes=[t_junk, t_S])
            for q in range(4):
                dve.op(lambda q=q, h=h: V.scalar_tensor_tensor(junk[0:4, :], Ks[0:4, h, :], SCALE, qb[0:4, q, :], ALU.mult, ALU.mult,
                                                               accum_out=Sown[0:4, q:q + 1]),
                       reads=[t_sm, t_qb], writes=[t_junk, t_S])
            dve.op(lambda: V.tensor_tensor(Sown[0:4, :], Sown[0:4, :], smask[0:4, :], ALU.add), reads=[t_S, t_const], writes=[t_S])
            act.op(lambda: S.activation(Psm[:, :], Ssm[:, :], AF.Exp), reads=[t_S], writes=[t_P])
            act.op(lambda: S.activation(Pown[0:4, :], Sown[0:4, :], AF.Exp), reads=[t_S], writes=[t_P])
            pe.group([lambda: PEn.matmul(PS[2][:, 0:24], onesf[:, :], Psm[:, :], start=True, stop=True)],
                     reads=[t_P, t_const], writes=[t_PS[2]])
            dve.op(lambda: V.tensor_reduce(den[:, 0:4], PS[2][:, 0:24].rearrange("p (a b) -> p a b", a=4), AX.X, ALU.add),
                   reads=[t_PS[2]], writes=[t_den])
            pe.group([lambda: PEn.matmul(PS[2][:, 32:36], onesf[0:4, :], Pown[0:4, :], start=True, stop=True)],
                     reads=[t_P, t_const], writes=[t_PS[2]])
            dve.op(lambda: V.tensor_tensor(den[:, 0:4], den[:, 0:4], PS[2][:, 32:36], ALU.add), reads=[t_PS[2], t_den], writes=[t_den])
            dve.op(lambda: V.reciprocal(rd[:, 0:4], den[:, 0:4]), reads=[t_den], writes=[t_den])
            for q in range(4):
                fns = [(lambda q=q, i=i, gb=gb: PEn.matmul(PS[3][:, q:q + 1], Vg[gb][:, q * 6 + i, :], Psm[:, q * 6 + i:q * 6 + i + 1],
                                                           start=(i == 0), stop=False)) for i in range(6)]
                fns.append(lambda q=q, h=h: PEn.matmul(PS[3][:, q:q + 1], Vs[0:4, h, :], Pown[0:4, q:q + 1], start=False, stop=True))
                pe.group(fns, reads=[t_Vg[gb], t_P, t_sm], writes=[t_PS[3]])
            dve.op(lambda h=h: V.tensor_tensor(AT[:, h, NOWN:NOWN + NS], PS[3][:, 0:4], rd[:, 0:4], ALU.mult),
                   reads=[t_PS[3], t_den], writes=[t_AT])
    else:
        dve.op(lambda: V.memset(AT[:, 0:8, NOWN:NOWN + NS], 0.0), writes=[t_AT])
    c.barrier()

    c.dma(sp, R1[:, :], xT_scr.ap(), writes=t_xT)

    def resid_add(banks, oc, scale_ap=None):
        for (c0, n), b in zip(TOKT, banks):
            if scale_ap is None:
                dve.op(lambda c0=c0, n=n, b=b: V.tensor_tensor(xT[:, oc, c0:c0 + n], PS[b][:, 0:n], xT[:, oc, c0:c0 + n], ALU.add),
                       reads=[t_PS[b], t_xT[oc]], writes=[t_xT[oc]])
            else:
                dve.op(lambda c0=c0, n=n, b=b: V.scalar_tensor_tensor(xT[:, oc, c0:c0 + n], PS[b][:, 0:n], scale_ap, xT[:, oc, c0:c0 + n],
                                                                      ALU.mult, ALU.add),
                       reads=[t_PS[b], t_xT[oc], t_const], writes=[t_xT[oc]])

    for oc in range(NCH):
        wt, t_w = wload(w_o[:, oc * 128:(oc + 1) * 128], NCH)
        banks = bank_sets[bsi]; bsi ^= 1
        proj(wt, t_w, NCH, lambda ch, c0, n: (AT[:, ch, c0:c0 + n] if ch < 8 else YC[:, ch - 8, c0:c0 + n]), [t_AT, t_YC], banks)
        resid_add(banks, oc)
    c.barrier()

    def ffn(layer, gidx):
        norm_to_hT(gidx)
        A3.reset()
        NG = 4
        per = NFT // NG
        aT = A3.take([128, per, NTP], BF16); t_aT = T("aT")
        sg = [A3.take([128, 512], F32) for i in range(2)]; t_sg = [T("sg0"), T("sg1")]
        sgi = 0
        for g in range(NG):
            for fi in range(per):
                ft = g * per + fi
                wg, t_wg = wload(w_gate[layer, :, ft * 128:(ft + 1) * 128], NCH)
                wu, t_wu = wload(w_up[layer, :, ft * 128:(ft + 1) * 128], NCH)
                proj(wg, t_wg, NCH, hrhs, [t_hT], [0, 1, 2])
                proj(wu, t_wu, NCH, hrhs, [t_hT], [4, 5, 6])
                for (c0, n), bg, bu in zip(TOKT, [0, 1, 2], [4, 5, 6]):
                    k = sgi; sgi ^= 1
                    act.op(lambda k=k, bg=bg, n=n: S.activation(sg[k][:, 0:n], PS[bg][:, 0:n], AF.Silu), reads=[t_PS[bg]], writes=[t_sg[k]])
                    dve.op(lambda k=k, bu=bu, n=n, c0=c0, fi=fi: V.tensor_tensor(aT[:, fi, c0:c0 + n], PS[bu][:, 0:n], sg[k][:, 0:n], ALU.mult),
                           reads=[t_PS[bu], t_sg[k]], writes=[t_aT])
            for oc in range(NCH):
                wd, t_wd = wload(w_down[layer, g * per * 128:(g + 1) * per * 128, oc * 128:(oc + 1) * 128], per)
                banks = bank_sets[bsi_box[0]]; bsi_box[0] ^= 1
                proj(wd, t_wd, per, lambda ch, c0, n: aT[:, ch, c0:c0 + n], [t_aT], banks)
                resid_add(banks, oc)
        c.barrier()

    bsi_box = [bsi]
    if STAGE >= 4:
        ffn(0, 1)

    if STAGE >= 4:
        A3.reset()
        AR = A3
        dT = hT; t_dT = t_hT
        stpT = AR.take([128, NCH, NPH], F32)
        h1l = AR.take([128, NCH, NPH], F32); h1s = AR.take([128, NCH, NPH], F32); t_h1l = T("h1l")
        stp_sb = AR.take([NPH, D], F32); t_stp = T("stp")
        c.dma(sp, stp_sb[:, :], st_pool[:, :], writes=[t_stp])
        for q4 in range(4):
            pe.group([(lambda k=k: PEn.transpose(PS[7][:, k * 128:k * 128 + NPH], stp_sb[0:NPH, (q4 * 4 + k) * 128:(q4 * 4 + k + 1) * 128],
                                                  ident[0:NPH, 0:NPH])) for k in range(4)], reads=[t_stp, t_const], writes=[t_PS[7]])
            dve.op(lambda q4=q4: V.tensor_copy(stpT[:, q4 * 4:q4 * 4 + 4, :], PS[7][:, :].rearrange("p (k n) -> p k n", k=4)[:, :, 0:NPH]),
                   reads=[t_PS[7]], writes=[t_stp])
        rs_all = AR.take([128, NTP], F32); t_rsall = T("rsall")
        for (c0, n) in TOKT:
            rmsnorm_fm(xT, t_xT, (c0, n), 2, lambda ch, rs: None, sq2, t_sq2, rstd, t_rstd, 3)
            dve.op(lambda c0=c0, n=n: V.tensor_copy(rs_all[:, c0:c0 + n], rstd[:, 0:n]), reads=[t_rstd], writes=[t_rsall])
        hx = AR.take([128, NPH + NOWN], F32); hxs = AR.take([128, NPH + NS], F32); t_hx = T("hx")
        sA = AR.take([128, NPH + NOWN], F32); sB = AR.take([128, NPH + NOWN], F32); t_s = T("s")
        for ch in range(NCH):
            g = ch // 4
            gsc = gall[:, 2, ch:ch + 1]
            dve.op(lambda ch=ch: V.scalar_tensor_tensor(hx[:, NPH:NPH + NOWN], xT[:, ch, 0:NOWN], gsc, rs_all[:, 0:NOWN], ALU.mult, ALU.mult),
                   reads=[t_xT[ch], t_rsall, t_const], writes=[t_hx])
            dve.op(lambda ch=ch: V.scalar_tensor_tensor(hx[:, 0:NPH], xT[:, ch, NT - NPH:NT], gsc, rs_all[:, NT - NPH:NT], ALU.mult, ALU.mult),
                   reads=[t_xT[ch], t_rsall, t_const], writes=[t_hx])
            dve.op(lambda: V.tensor_tensor(hx[:, 0:NPH], hx[:, 0:NPH], hv[:, 0:NPH], ALU.mult), reads=[t_hx, t_const], writes=[t_hx])
            dve.op(lambda ch=ch: V.scalar_tensor_tensor(hxs[:, NPH:NPH + NS], xT[:, ch, NOWN:NOWN + NS], gsc, rs_all[:, NOWN:NOWN + NS], ALU.mult, ALU.mult),
                   reads=[t_xT[ch], t_rsall, t_const], writes=[t_hx])
            dve.op(lambda ch=ch: V.tensor_copy(hxs[:, 0:NPH], stpT[:, ch, :]), reads=[t_stp], writes=[t_hx])
            dve.op(lambda ch=ch: V.tensor_copy(h1l[:, ch, :], hx[:, NOWN:NOWN + NPH]), reads=[t_hx], writes=[t_h1l])
            dve.op(lambda ch=ch: V.tensor_copy(h1s[:, ch, :], hxs[:, NS:NS + NPH]), reads=[t_hx], writes=[t_h1l])
            w = 2 ** (g + 1)
            for (src, L, c0out, nout) in ((hx, NPH + NOWN, 0, NOWN), (hxs, NPH + NS, NOWN, NS)):
                cur = src
                sh = 1
                bufs = [sA, sB]
                bi = 0
                while sh < w:
                    nxt = bufs[bi]; bi ^= 1
                    dve.op(lambda cur=cur, nxt=nxt, sh=sh, L=L: V.tensor_tensor(nxt[:, sh:L], cur[:, sh:L], cur[:, 0:L - sh], ALU.add),
                           reads=[t_hx, t_s], writes=[t_s])
                    if sh > 1 or True:
                        dve.op(lambda cur=cur, nxt=nxt, sh=sh: V.tensor_copy(nxt[:, 0:sh], cur[:, 0:sh]), reads=[t_hx, t_s], writes=[t_s])
                    cur = nxt
                    sh *= 2
                dve.op(lambda cur=cur, src=src, c0out=c0out, nout=nout, ch=ch, w=w: V.scalar_tensor_tensor(
                    dT[:, ch, c0out:c0out + nout], cur[:, NPH:NPH + nout], 1.0 / w, src[:, NPH:NPH + nout], ALU.mult, ALU.subtract),
                    reads=[t_s, t_hx], writes=[t_dT])
                if nout == NOWN:
                    fx = sA if cur is sB else sB
                    dve.op(lambda cur=cur, fx=fx, g=g: V.tensor_tensor(fx[:, 0:NPH], cur[:, NPH:2 * NPH], invc[:, g, 0:NPH], ALU.mult),
                           reads=[t_s, t_const], writes=[t_s])
                    dve.op(lambda fx=fx, src=src, ch=ch: V.tensor_tensor(dT[:, ch, 0:NPH], fx[:, 0:NPH], src[:, NPH:2 * NPH], ALU.subtract),
                           reads=[t_s, t_hx], writes=[t_dT])
            dve.op(lambda ch=ch: V.memset(dT[:, ch, NOWN + NS:NT], 0.0), writes=[t_dT])
        for (src, dst) in ((h1l, pool_p), (h1s, pool_s)):
            po = stp_sb; t_po = t_stp
            for q4 in range(4):
                pe.group([(lambda k=k: PEn.transpose(PS[7][0:NPH, k * 128:(k + 1) * 128], src[:, q4 * 4 + k, :], ident[:, :])) for k in range(4)],
                         reads=[t_h1l, t_const], writes=[t_PS[7]])
                dve.op(lambda q4=q4, po=po: V.tensor_copy(po[:, q4 * 512:(q4 + 1) * 512], PS[7][0:NPH, :]), reads=[t_PS[7]], writes=[t_po])
            c.dma(sp, dst[:, :], po[:, :], reads=[t_po])
        for g in range(4):
            for oc4 in range(4):
                oc = g * 4 + oc4
                wt, t_w = wload(w_pool[g, :, oc4 * 128:(oc4 + 1) * 128], 4)
                banks = bank_sets[bsi_box[0]]; bsi_box[0] ^= 1
                proj(wt, t_w, 4, lambda ch, c0, n, g=g: dT[:, g * 4 + ch, c0:c0 + n], [t_dT], banks)
                resid_add(banks, oc, psc[:, oc:oc + 1])
        c.barrier()
        ffn(1, 3)

    A3.reset()
    stage2 = [A3.take([128, D], F32) for i in range(2)]; t_stage2 = [T("st0"), T("st1")]
    yT = A3.take([128, NCH, 128], F32); t_yT = T("yT")
    for (c0, n) in TOKT:
        rmsnorm_fm(xT, t_xT, (c0, n), 4, lambda ch, rs: None, sq2, t_sq2, rstd, t_rstd, 3)
        nsub = 4 if n == 512 else 1
        for sub in range(nsub):
            nr = 128 if n == 512 else NS
            k2 = sub % 2
            for ch in range(NCH):
                dve.op(lambda ch=ch, sub=sub, nr=nr, c0=c0: V.scalar_tensor_tensor(
                    yT[:, ch, 0:nr], xT[:, ch, c0 + sub * 128:c0 + sub * 128 + nr], gall[:, 4, ch:ch + 1],
                    rstd[:, sub * 128:sub * 128 + nr], ALU.mult, ALU.mult),
                    reads=[t_xT[ch], t_rstd, t_const], writes=[t_yT])
            for q4 in range(4):
                b = 4 + q4
                pe.group([(lambda k=k: PEn.transpose(PS[b][0:nr, k * 128:(k + 1) * 128], yT[:, q4 * 4 + k, 0:nr], ident[:, :]))
                          for k in range(4)], reads=[t_yT, t_const], writes=[t_PS[b]])
                evac(stage2[k2][0:nr, q4 * 512:(q4 + 1) * 512], PS[b][0:nr, :], [t_PS[b]], [t_stage2[k2]])
            if n == 512:
                c.dma(sp, y_own[c0 + sub * 128:c0 + (sub + 1) * 128, :], stage2[k2][:, :], reads=[t_stage2[k2]])
            else:
                c.dma(sp, y_smp[:, :], stage2[k2][0:NS, :], reads=[t_stage2[k2]])
    c.finish()
    return nc


_CACHE = {}


def _consts(j):
    bf = ml_dtypes.bfloat16
    half = 64
    inv = (np.float32(10000.0) ** (-np.arange(half, dtype=np.float32) / np.float32(half))).astype(np.float32)

    def cs(pos):
        ang = pos.astype(np.float32)[:, None] * inv[None, :]
        co, si = np.cos(ang).astype(np.float32), np.sin(ang).astype(np.float32)
        return np.stack([np.concatenate([co, co], 1).T, np.concatenate([si, si], 1).T], axis=1)
    T0 = j * 1024
    pos = np.zeros(NTP, np.float32)
    pos[0:NOWN] = T0 + np.arange(NOWN)
    pos[NOWN:NOWN + NS] = 16384 + np.arange(NS)
    pos[NOWN + NS:NT] = np.maximum(T0 - NH + np.arange(NH), 0)
    d = {}
    d["cs_own"] = np.ascontiguousarray(cs(pos))
    d["cs_past"] = np.ascontiguousarray(cs(np.arange(NPAST, dtype=np.float32)))
    d["c_ident"] = np.eye(128, dtype=np.float32)
    pr = np.zeros((128, 128), np.float32)
    for dd in range(64):
        pr[dd + 64, dd] = -1.0
        pr[dd, dd + 64] = 1.0
    d["c_prot"] = pr.astype(bf)
    ea = np.zeros((16, 16, 128), np.float32)
    for r in range(16):
        ea[r, r, :] = 1.0
    d["c_eall"] = ea.astype(bf)
    k = np.arange(128)[:, None]; q = np.arange(512)[None, :]
    cmm = np.stack([np.where(off * 128 + k <= q, 0.0, NEGB) for off in range(4)], axis=1)
    d["c_cm"] = cmm.astype(bf)
    hbm = np.zeros((128, 24, NH), np.float32)
    for kt in range(24):
        for qq in range(NH):
            p = T0 - NH + qq
            s = kt * 128 + np.arange(128)
            vis = (s <= p) if j > 0 else np.full(128, kt == 0)
            hbm[:, kt, qq] = np.where(vis, 0.0, NEGB)
    d["c_hb"] = hbm.astype(bf)
    adc = np.zeros((128, 8, 16), np.float32)
    for qt in range(8):
        ob = (qt * 128) // 256
        for sbk in range(16):
            if sbk < 12:
                v = 0.0 if sbk < 4 * j else -2e30
            else:
                o = sbk - 12
                v = 0.0 if o < ob else (1e30 if o == ob else -2e30)
            adc[:, qt, sbk] = v
    d["c_addc"] = adc
    adh = np.full((NH, 16), -2e30, np.float32)
    if j > 0:
        adh[:, :4 * j - 1] = 0.0
        adh[:, 4 * j - 1] = 1e30
    else:
        adh[:, 0] = 1e30
    d["c_addh"] = adh
    d["c_hv"] = np.full((128, NH), 1.0 if j > 0 else 0.0, np.float32)
    ic = np.zeros((128, 4, 16), np.float32)
    for g in range(4):
        w = 2 ** (g + 1)
        for i in range(16):
            ic[:, g, i] = 1.0 / min(T0 + i + 1, w)
    d["c_invc"] = ic
    sm = np.zeros((4, 4), np.float32)
    for kk in range(4):
        for qq in range(4):
            sm[kk, qq] = 0.0 if kk <= qq else NEGB
    d["c_smask"] = sm
    es = np.zeros((4, 4, 128), np.float32)
    for r in range(4):
        es[r, r, :] = 1.0
    d["c_esel"] = es
    d["c_pio"] = np.stack([np.arange(128, dtype=np.float32), 8.0 * np.arange(128, dtype=np.float32)], axis=1)
    d["c_iota64"] = np.ascontiguousarray(np.broadcast_to(np.arange(64, dtype=np.float32)[None, :], (128, 64)))
    return d


def kernel(x_prompt, x_sample, cache_k, cache_v, page_table, state_conv, state_pool, norm_mix, norm_ffn, norm_final,
           w_in, conv_w, w_o, w_pool, pool_scale, w_gate, w_up, w_down):
    f = lambda a: np.ascontiguousarray(np.asarray(a, dtype=np.float32))
    x_prompt, x_sample = f(x_prompt), f(x_sample)
    if "nc" not in _CACHE:
        _CACHE["nc"] = build_program()
    nc = _CACHE["nc"]
    fm = lambda v: np.asarray(v, np.float32).reshape(16, 128).T
    g_all = np.ascontiguousarray(np.stack([fm(norm_mix[0]), fm(norm_ffn[0]), fm(norm_mix[1]), fm(norm_ffn[1]), fm(norm_final)], axis=1))
    shared = {
        "g_all": g_all, "w_in": f(w_in)[0], "conv_w": np.ascontiguousarray(f(conv_w)[0].reshape(3, 8, 128).transpose(2, 1, 0)),
        "w_o": f(w_o)[0], "w_pool": f(w_pool)[0], "pscale": np.ascontiguousarray(fm(pool_scale[0])),
        "w_gate": f(w_gate), "w_up": f(w_up), "w_down": f(w_down),
    }
    if STAGE >= 5:
        shared["cache_k"] = f(cache_k)[0]; shared["cache_v"] = f(cache_v)[0]
    in_maps = []
    for c in range(8):
        b, j = c // 4, c % 4
        T0 = j * 1024
        m = dict(shared)
        m["x_own"] = np.ascontiguousarray(x_prompt[b, T0:T0 + 1024])
        halo = np.zeros((NH, D), np.float32)
        if j > 0:
            halo[:] = x_prompt[b, T0 - NH:T0]
        m["x_c"] = np.ascontiguousarray(np.concatenate([x_sample[c], halo], axis=0))
        m["x_past"] = np.ascontiguousarray(x_prompt[b, 0:NPAST])
        m["ptab"] = np.ascontiguousarray(np.asarray(page_table, np.int32)[c:c + 1])
        m["st_conv"] = np.ascontiguousarray(f(state_conv)[0, c])
        m["st_pool"] = np.ascontiguousarray(f(state_pool)[0, c])
        m.update(_consts(j))
        in_maps.append(m)
    res = run_bass_kernel_spmd(nc, in_maps, core_ids=list(range(8)))
    R = res.results
    y_prompt = np.stack([np.concatenate([R[b * 4 + j]["y_own"] for j in range(4)], 0) for b in range(2)], 0)
    y_sample = np.stack([R[c]["y_smp"] for c in range(8)], 0)
    k_prompt = np.stack([np.concatenate([R[b * 4 + j]["k_own"] for j in range(4)], 0) for b in range(2)], 0)[None]
    v_prompt = np.stack([np.concatenate([R[b * 4 + j]["v_own"] for j in range(4)], 0) for b in range(2)], 0)[None]
    k_sample = np.stack([R[c]["k_smp"] for c in range(8)], 0)[None]
    v_sample = np.stack([R[c]["v_smp"] for c in range(8)], 0)[None]
    conv_prompt = np.stack([R[3]["conv_p"], R[7]["conv_p"]], 0)[None]
    conv_sample = np.stack([R[c]["conv_s"] for c in range(8)], 0)[None]
    pool_prompt = np.stack([R[3]["pool_p"], R[7]["pool_p"]], 0)[None]
    pool_sample = np.stack([R[c]["pool_s"] for c in range(8)], 0)[None]
    outs = (y_prompt, y_sample, k_prompt, v_prompt, k_sample, v_sample, conv_prompt, conv_sample, pool_prompt, pool_sample)
    return tuple(np.ascontiguousarray(o.astype(np.float32)) for o in outs)
```

```python
import os
import numpy as np
import ml_dtypes
import concourse.bass as bass
import concourse.mybir as mybir
from concourse.bass_utils import run_bass_kernel_spmd

F32 = mybir.dt.float32
BF16 = mybir.dt.bfloat16
I32 = mybir.dt.int32
U32 = mybir.dt.uint32
AF = mybir.ActivationFunctionType
ALU = mybir.AluOpType
AX = mybir.AxisListType

D = 2048
NCH = 16
DFF = 5632
NFT = 44
NPAST = 3072
NOWN = 1024
NS = 4
NH = 17
NPH = 15
NC_ = NS + NH
NT = NOWN + NC_
NTP = 1048
SCALE = 128 ** -0.5
NEGB = -30000.0
TOKT = [(0, 512), (512, 512), (1024, NC_)]
STAGE = int(os.environ.get("MK_STAGE", "5"))


class T:
    __slots__ = ("name", "w", "rd", "excl")

    def __init__(self, name="", excl=False):
        self.name = name
        self.w = None
        self.rd = []
        self.excl = excl


def _prune(rd):
    best = {}
    for tok in rd:
        if tok[0] in ("dma", "sw"):
            k = (tok[0], tok[1])
            if k not in best or best[k][2] < tok[2]:
                best[k] = tok
        else:
            k = tok[0].name
            if k not in best or best[k][1] < tok[1]:
                best[k] = tok
    return list(best.values())


class Eng:
    def __init__(self, ctx, name, e, sem):
        self.ctx, self.name, self.e, self.sem = ctx, name, e, sem
        self.count = 0
        self.seen = {}
        self.seen_dma = {}
        self.seen_sw = {}
        self.is_pe = name == "pe"

    def need(self, tok, war=False):
        if tok is None:
            return
        if tok[0] == "sw":
            _, s, g = tok
            if self.seen_sw.get(s, 0) >= g:
                return
            self.e.wait_ge(self.ctx.sw_sems[s], 16 * g)
            self.seen_sw[s] = g
            return
        if tok[0] == "dma":
            _, s, v = tok
            if self.seen_dma.get(s, 0) >= v:
                return
            self.e.wait_ge(self.ctx.dma_sems[s], v)
            self.seen_dma[s] = v
            return
        src, o = tok
        if src is self:
            if self.is_pe:
                return
            if self.seen.get(self.name, 0) >= o:
                return
            self.e.wait_ge(self.sem, o)
            self.seen[self.name] = o
            return
        if self.seen.get(src.name, 0) >= o:
            return
        self.e.wait_ge(src.sem, o)
        self.seen[src.name] = o

    def deps(self, reads, writes):
        for t in reads:
            self.need(t.w)
            if t.excl:
                for r in t.rd:
                    if r[0] is not self:
                        self.need(r)
        for t in writes:
            self.need(t.w)
            for r in t.rd:
                self.need(r, war=True)

    def commit(self, tok, reads, writes):
        for t in reads:
            t.rd.append(tok)
            if len(t.rd) > 16:
                t.rd = _prune(t.rd)
        for t in writes:
            t.w = tok
            t.rd = []

    def op(self, fn, reads=(), writes=()):
        self.deps(reads, writes)
        ins = fn()
        self.count += 1
        ins.then_inc(self.sem, 1)
        tok = (self, self.count)
        self.commit(tok, reads, writes)
        return tok

    def group(self, fns, reads=(), writes=()):
        self.deps(reads, writes)
        ins = None
        for fn in fns:
            ins = fn()
        self.count += 1
        ins.then_inc(self.sem, 1)
        tok = (self, self.count)
        self.commit(tok, reads, writes)
        return tok


class Ctx:
    def __init__(self, nc, n_dma_sems=40):
        self.nc = nc
        self.pe = Eng(self, "pe", nc.tensor, nc.alloc_semaphore("s_pe"))
        self.dve = Eng(self, "dve", nc.vector, nc.alloc_semaphore("s_dve"))
        self.act = Eng(self, "act", nc.scalar, nc.alloc_semaphore("s_act"))
        self.pool = Eng(self, "pool", nc.gpsimd, nc.alloc_semaphore("s_pool"))
        self.sp = Eng(self, "sp", nc.sync, nc.alloc_semaphore("s_sp"))
        self.dma_sems = [nc.alloc_semaphore(f"s_dma{i}") for i in range(n_dma_sems)]
        self.dma_val = [0] * n_dma_sems
        self.dma_next = 0
        self.sw_sems = [nc.alloc_semaphore(f"s_sw{i}") for i in range(10)]
        self.sw_gen = [0] * 10

    def swdma(self, i, out, in_, reads=(), writes=(), indirect=None, element_offset=0, **kw):
        q = self.pool
        for t in reads:
            q.need(t.w)
        for t in writes:
            if not (t.w is not None and t.w[0] == "sw" and t.w[1] == i):
                q.need(t.w)
            for r in t.rd:
                q.need(r, war=True)
        if indirect is not None:
            ins = q.e.indirect_dma_start(out=out, out_offset=None, in_=in_,
                                         in_offset=bass.IndirectOffsetOnAxis(ap=indirect, axis=0), element_offset=element_offset)
        else:
            ins = q.e.dma_start(out=out, in_=in_, **kw)
        self.sw_gen[i] += 1
        ins.then_inc(self.sw_sems[i], 16)
        tok = ("sw", i, self.sw_gen[i])
        q.commit(tok, reads, writes)
        return tok

    def dma(self, q, out, in_, reads=(), writes=(), **kw):
        q.deps(reads, writes)
        s = self.dma_next
        self.dma_next = (self.dma_next + 1) % len(self.dma_sems)
        if self.dma_val[s] > 0:
            q.need(("dma", s, self.dma_val[s]))
        ins = q.e.dma_start(out=out, in_=in_, **kw)
        self.dma_val[s] += 16
        ins.then_inc(self.dma_sems[s], 16)
        tok = ("dma", s, self.dma_val[s])
        q.commit(tok, reads, writes)
        return tok

    def barrier(self):
        engs = (self.pe, self.dve, self.act, self.pool, self.sp)
        for q in engs:
            for s, g in enumerate(self.sw_gen):
                if g > 0:
                    q.need(("sw", s, g))
            for s, v in enumerate(self.dma_val):
                if v > 0:
                    q.need(("dma", s, v))
            for e in engs:
                if e is not q and e.count > 0:
                    q.need((e, e.count))

    def finish(self):
        q = self.sp
        for s, v in enumerate(self.dma_val):
            if v > 0:
                q.need(("dma", s, v))
        for e in (self.pe, self.dve, self.act, self.pool):
            if e.count > 0:
                q.need((e, e.count))


def build_program():
    nc = bass.Bass("TRN2", target_bir_lowering=False)
    c = Ctx(nc)
    pe, dve, act, pool, sp = c.pe, c.dve, c.act, c.pool, c.sp
    V, S, PEn = nc.vector, nc.scalar, nc.tensor

    def din(name, shape, dt=F32):
        return nc.dram_tensor(name, list(shape), dt, kind="ExternalInput")

    def dout(name, shape, dt=F32):
        return nc.dram_tensor(name, list(shape), dt, kind="ExternalOutput")

    x_own = din("x_own", [NOWN, D]); x_c = din("x_c", [NC_, D]); x_past = din("x_past", [NPAST, D])
    if STAGE >= 5:
        cache_k = din("cache_k", [1280, 128, 8, 128]); cache_v = din("cache_v", [1280, 128, 8, 128])
    ptab = din("ptab", [1, 128], I32)
    st_conv = din("st_conv", [2, 1024]); st_pool = din("st_pool", [15, D])
    g_all = din("g_all", [128, 5, 16])
    w_in = din("w_in", [D, 6144]); conv_w = din("conv_w", [128, 8, 3]); w_o = din("w_o", [D, D])
    w_pool = din("w_pool", [4, 512, 512]); pscale = din("pscale", [128, 16])
    w_gate = din("w_gate", [2, D, DFF]); w_up = din("w_up", [2, D, DFF]); w_down = din("w_down", [2, DFF, D])
    cs_own = din("cs_own", [128, 2, NTP]); cs_past = din("cs_past", [128, 2, NPAST])
    c_ident = din("c_ident", [128, 128]); c_prot = din("c_prot", [128, 128], BF16)
    c_eall = din("c_eall", [16, 16, 128], BF16); c_cm = din("c_cm", [128, 4, 512], BF16)
    c_hb = din("c_hb", [128, 24, NH], BF16); c_addc = din("c_addc", [128, 8, 16]); c_addh = din("c_addh", [NH, 16])
    c_hv = din("c_hv", [128, NH]); c_invc = din("c_invc", [128, 4, 16])
    c_smask = din("c_smask", [4, 4]); c_esel = din("c_esel", [4, 4, 128])
    c_pio = din("c_pio", [128, 2]); c_iota64 = din("c_iota64", [128, 64])
    y_own = dout("y_own", [NOWN, D]); y_smp = dout("y_smp", [NS, D])
    k_own = dout("k_own", [NOWN, 8, 128]); v_own = dout("v_own", [NOWN, 8, 128])
    k_smp = dout("k_smp", [NS, 8, 128]); v_smp = dout("v_smp", [NS, 8, 128])
    conv_p = dout("conv_p", [2, 1024]); conv_s = dout("conv_s", [2, 1024])
    pool_p = dout("pool_p", [15, D]); pool_s = dout("pool_s", [15, D])
    kt_scr = nc.dram_tensor("kt_scr", [8, 128, 4096], BF16)
    v_scr = nc.dram_tensor("v_scr", [8, 32, 128, 128], BF16)

    sbuf_used = [0]

    def sb(name, shape, dt):
        return nc.alloc_sbuf_tensor(name, list(shape), dt)

    R1 = sb("R1", [128, NCH * NTP], F32)
    R2 = sb("R2", [128, NCH * NTP // 2], F32)
    R3 = sb("R3", [128, 11264], F32)
    xT = R1[:, :].rearrange("p (a b) -> p a b", a=NCH); t_xT = [T(f"xT{i}") for i in range(NCH)]
    hT = R2[:, :].bitcast(BF16).rearrange("p (a b) -> p a b", a=NCH); t_hT = T("hT")
    xT_scr = nc.dram_tensor("xT_scr", [128, NCH * NTP], F32)
    ident = sb("ident", [128, 128], F32); identb = sb("identb", [128, 128], BF16)
    onesb = sb("onesb", [128, 128], BF16); prot = sb("prot", [128, 128], BF16)
    eall = sb("eall", [16, 16, 128], BF16); cm = sb("cm", [128, 4, 512], BF16); hb = sb("hb", [128, 24, NH], BF16)
    addc = sb("addc", [128, 8, 16], F32); addh = sb("addh", [NH, 16], F32)
    hv = sb("hv", [128, NH], F32); invc = sb("invc", [128, 4, 16], F32)
    gall = sb("gall", [128, 5, 16], F32); convw = sb("convw", [128, 8, 3], F32); psc = sb("psc", [128, 16], F32)
    csown = sb("csown", [128, 2, NTP], F32)
    t_const = T("const")
    for dst, src in [(ident, c_ident), (prot, c_prot), (eall, c_eall), (cm, c_cm), (hb, c_hb), (addc, c_addc),
                     (addh, c_addh), (hv, c_hv), (invc, c_invc), (gall, g_all), (convw, conv_w), (psc, pscale),
                     (csown, cs_own)]:
        c.dma(sp, dst.ap(), src.ap(), writes=[t_const])
    dve.op(lambda: V.tensor_copy(identb[:, :], ident[:, :]), reads=[t_const], writes=[t_const])
    dve.op(lambda: V.memset(onesb[:, :], 1.0), writes=[t_const])
    c.barrier()

    PS = [nc.alloc_psum_tensor(f"ps{i}", [128, 512], F32) for i in range(8)]
    t_PS = [T(f"ps{i}", excl=True) for i in range(8)]

    class Arena:
        def __init__(self, base, nbytes):
            self.base = base
            self.nbytes = nbytes
            self.off = 0

        def reset(self):
            self.off = 0

        def take(self, shape, dt):
            esz = 4 if dt in (F32, I32, U32) else 2
            n = int(np.prod(shape[1:]))
            nbytes = (n * esz + 31) // 32 * 32
            assert self.off + nbytes <= self.nbytes, ("arena overflow", self.off, nbytes, self.nbytes)
            a = self.base[:, self.off // 4:(self.off + nbytes) // 4]
            self.off += nbytes
            if esz == 2:
                a = a.bitcast(BF16)[:, 0:n]
            elif dt != F32:
                a = a.bitcast(dt)[:, 0:n]
            else:
                a = a[:, 0:n]
            if len(shape) == 3:
                a = a.rearrange("p (a b) -> p a b", a=shape[1])
            if len(shape) == 4:
                a = a.rearrange("p (a b c) -> p a b c", a=shape[1], b=shape[2])
            if shape[0] < 128:
                a = a[0:shape[0]]
            return a

    A1 = Arena(R1, NCH * NTP * 4)
    A2 = Arena(R2, NCH * NTP * 2)
    A3 = Arena(R3, 11264 * 4)

    def take(ar, name, shape, dt):
        return ar.take(shape, dt)

    WB = [sb(f"wb{i}", [128, 16, 128], BF16) for i in range(4)]
    t_WB = [T(f"wb{i}") for i in range(4)]
    wb_next = [0]

    def wload(src_ap, nchunk):
        i = wb_next[0]
        wb_next[0] = (i + 1) % 4
        c.swdma(i, WB[i][:, 0:nchunk, :], src_ap.rearrange("(c p) n -> p c n", p=128), writes=[t_WB[i]])
        return WB[i], t_WB[i]

    evac_flip = [0]

    def evac(out_ap, in_ap, reads, writes):
        evac_flip[0] ^= 1
        if evac_flip[0]:
            return act.op(lambda: S.copy(out_ap, in_ap), reads=reads, writes=writes)
        return dve.op(lambda: V.tensor_copy(out_ap, in_ap), reads=reads, writes=writes)

    def load_transpose(src_rows_ap, nrows, dstT, t_dst, col0, stage, t_stage, psb):
        c.dma(sp, stage[0:nrows, :], src_rows_ap, writes=[t_stage])
        for q4 in range(4):
            b = psb[q4 % len(psb)]
            pe.group([(lambda k=k: PEn.transpose(PS[b][:, k * 128:k * 128 + nrows],
                                                  stage[0:nrows, (q4 * 4 + k) * 128:(q4 * 4 + k + 1) * 128],
                                                  ident[0:nrows, 0:nrows])) for k in range(4)],
                     reads=[t_stage, t_const], writes=[t_PS[b]])
            wr = t_dst[q4 * 4:q4 * 4 + 4] if isinstance(t_dst, list) else [t_dst]
            evac(dstT[:, q4 * 4:q4 * 4 + 4, col0:col0 + nrows],
                 PS[b][:, :].rearrange("p (k n) -> p k n", k=4)[:, :, 0:nrows], [t_PS[b]], wr)

    def rmsnorm_fm(srcT, t_src, cols, gidx, out_fn, tmp_sq, t_sq, rstd, t_rstd, psb):
        c0, n = cols
        rd = t_src if isinstance(t_src, list) else [t_src]
        fns = []
        for ch in range(NCH):
            k = ch % 2
            act.op(lambda ch=ch, k=k: S.activation(tmp_sq[k][:, 0:n], srcT[:, ch, c0:c0 + n], AF.Square),
                   reads=[rd[ch] if len(rd) > 1 else rd[0]], writes=[t_sq[k]])
            pe.deps([t_sq[k], t_const], [t_PS[psb]] if ch == 0 else [])
            ins = PEn.matmul(PS[psb][:, 0:n], onesb[:, :], tmp_sq[k][:, 0:n], start=(ch == 0), stop=(ch == NCH - 1))
            pe.count += 1
            ins.then_inc(pe.sem, 1)
            tok = (pe, pe.count)
            pe.commit(tok, [t_sq[k]], [t_PS[psb]] if ch == NCH - 1 else [])
            if ch != NCH - 1:
                t_PS[psb].w = tok
        act.op(lambda: S.activation(rstd[:, 0:n], PS[psb][:, 0:n], AF.Sqrt, bias=eps_ap[:, 0:1], scale=1.0 / D),
               reads=[t_PS[psb], t_const], writes=[t_rstd])
        dve.op(lambda: V.reciprocal(rstd[:, 0:n], rstd[:, 0:n]), reads=[t_rstd], writes=[t_rstd])
        for ch in range(NCH):
            out_fn(ch, rstd[:, 0:n])

    eps_ap = sb("eps", [128, 1], F32)
    dve.op(lambda: V.memset(eps_ap[:, :], 1e-6), writes=[t_const])

    def proj(wt, t_w, nchunk, rhs_fn, t_rhs, banks, tiles=TOKT):
        for (c0, n), b in zip(tiles, banks):
            pe.group([(lambda ch=ch, c0=c0, n=n, b=b: PEn.matmul(PS[b][:, 0:n], wt[:, ch, :], rhs_fn(ch, c0, n),
                                                              start=(ch == 0), stop=(ch == nchunk - 1)))
                      for ch in range(nchunk)], reads=[t_w] + list(t_rhs), writes=[t_PS[b]])

    def rope(psb, n, cos_ap, sin_ap, out_ap, t_out, tmp, t_tmp, rotb, t_cs=None):
        t_cs = t_cs or t_const
        act.op(lambda: S.copy(tmp["qb"][:, 0:n], PS[psb][:, 0:n]), reads=[t_PS[psb]], writes=[t_tmp["qb"]])
        pe.group([lambda: PEn.matmul(PS[rotb][:, 0:n], prot[:, :], tmp["qb"][:, 0:n], start=True, stop=True)],
                 reads=[t_tmp["qb"], t_const], writes=[t_PS[rotb]])
        dve.op(lambda: V.tensor_tensor(tmp["t1"][:, 0:n], PS[psb][:, 0:n], cos_ap, ALU.mult),
               reads=[t_PS[psb], t_cs], writes=[t_tmp["t1"]])
        dve.op(lambda: V.tensor_tensor(tmp["t2"][:, 0:n], PS[rotb][:, 0:n], sin_ap, ALU.mult),
               reads=[t_PS[rotb], t_cs], writes=[t_tmp["t2"]])
        dve.op(lambda: V.tensor_tensor(out_ap, tmp["t1"][:, 0:n], tmp["t2"][:, 0:n], ALU.add),
               reads=[t_tmp["t1"], t_tmp["t2"]], writes=[t_out])

    ksum = sb("ksum", [128, 8, 16], F32); t_ksum = T("ksum")
    sq2 = [sb(f"sq{i}", [128, 512], BF16) for i in range(2)]; t_sq2 = [T("sq0"), T("sq1")]
    rstd = sb("rstd", [128, 512], F32); t_rstd = T("rstd")
    rt = {"qb": sb("r_qb", [128, 512], BF16), "t1": sb("r_t1", [128, 512], F32), "t2": sb("r_t2", [128, 512], F32)}
    t_rt = {k: T(k) for k in rt}

    if STAGE >= 2:
        A1.reset(); A2.reset(); A3.reset()
        hpT = A1.take([128, NCH, 1536], BF16); t_hpT = T("hpT")
        xpT = A2.take([128, NCH, 512], F32); t_xpT = T("xpT")
        stage2 = [A3.take([128, D], F32) for i in range(2)]; t_stage2 = [T("st0"), T("st1")]
        cspast = A3.take([128, 3, 2, 512], F32); t_csp = T("csp")
        cntb = [0]
        kf = A3.take([128, 512], F32); t_kf = T("kf")
        kst = [A3.take([128, 512], BF16) for i in range(2)]; t_kst = [T("kst0"), T("kst1")]
        vst = [A3.take([128, 4, 128], BF16) for i in range(2)]; t_vst = [T("vst0"), T("vst1")]
        vtb = A3.take([128, 512], BF16); t_vtb = T("vtb")
        cnt = 0
        for grp in range(2):
            for tl in range(3):
                tok0 = grp * 1536 + tl * 512
                for sub in range(4):
                    k = cnt % 2; cnt += 1
                    load_transpose(x_past[tok0 + sub * 128: tok0 + (sub + 1) * 128, :], 128, xpT, t_xpT, sub * 128,
                                   stage2[k], t_stage2[k], [4, 5, 6, 7])

                def out_fn(ch, rs, tl=tl):
                    dve.op(lambda: V.scalar_tensor_tensor(hpT[:, ch, tl * 512:(tl + 1) * 512], xpT[:, ch, :],
                                                          gall[:, 0, ch:ch + 1], rs, ALU.mult, ALU.mult),
                           reads=[t_xpT, t_rstd, t_const], writes=[t_hpT])
                rmsnorm_fm(xpT, t_xpT, (0, 512), 0, out_fn, sq2, t_sq2, rstd, t_rstd, 3)
            for tl in range(3):
                tok0 = grp * 1536 + tl * 512
                c.dma(sp, cspast[:, tl, :, :], cs_past[:, :, tok0:tok0 + 512], writes=[t_csp])
            ptiles = [(tl * 512, 512) for tl in range(3)]
            pend = None
            psi = 0
            for which in (1, 2):
                for h in range(8):
                    wt, t_w = wload(w_in[:, which * 1024 + h * 128: which * 1024 + (h + 1) * 128], NCH)
                    banks = [[0, 1, 2], [4, 5, 6]][psi]; tb = [3, 7][psi]; psi ^= 1
                    proj(wt, t_w, NCH, lambda ch, c0, n: hpT[:, ch, c0:c0 + n], [t_hpT], banks, tiles=ptiles)
                    if pend is not None:
                        pend()

                    def post(which=which, h=h, banks=banks, tb=tb, grp=grp):
                        for tl in range(3):
                            tok0 = grp * 1536 + tl * 512
                            b = banks[tl]
                            kk = cntb[0] % 2; cntb[0] += 1
                            if which == 1:
                                rope(b, 512, cspast[:, tl, 0, :], cspast[:, tl, 1, :], kf[:, :], t_kf, rt, t_rt, tb, t_cs=t_csp)
                                act.op(lambda kk=kk: S.copy(kst[kk][:, :], kf[:, :]), reads=[t_kf], writes=[t_kst[kk]])
                                c.dma(sp, kt_scr[h, :, tok0:tok0 + 512], kst[kk][:, :], reads=[t_kst[kk]])
                                sb0 = tok0 // 256
                                dve.op(lambda sb0=sb0, h=h: V.tensor_reduce(ksum[:, h, sb0:sb0 + 2],
                                                                            kf[:, :].rearrange("p (a b) -> p a b", a=2), AX.X, ALU.add),
                                       reads=[t_kf], writes=[t_ksum])
                            else:
                                act.op(lambda b=b: S.copy(vtb[:, :], PS[b][:, :]), reads=[t_PS[b]], writes=[t_vtb])
                                pb16 = PS[tb][:, :].bitcast(BF16)
                                pe.group([(lambda s4=s4, pb16=pb16: PEn.transpose(pb16[:, s4 * 128:(s4 + 1) * 128],
                                                                                   vtb[:, s4 * 128:(s4 + 1) * 128], identb[:, :]))
                                          for s4 in range(4)], reads=[t_vtb, t_const], writes=[t_PS[tb]])
                                dve.op(lambda kk=kk, pb16=pb16: V.tensor_copy(vst[kk][:, :, :].rearrange("p a b -> p (a b)"), pb16[:, 0:512]),
                                       reads=[t_PS[tb]], writes=[t_vst[kk]])
                                c.dma(sp, v_scr[h, tok0 // 128: tok0 // 128 + 4, :, :].rearrange("s p d -> p s d"), vst[kk][:, :, :],
                                      reads=[t_vst[kk]])
                    pend = post
            pend()
        c.barrier()

    A3.reset()
    stage2 = [A3.take([128, D], F32) for i in range(2)]; t_stage2 = [T("st0"), T("st1")]
    dve.op(lambda: V.memset(xT[:, :, NT:NTP], 0.0), writes=t_xT)
    dve.op(lambda: V.memset(hT[:, :, NT:NTP], 0.0), writes=[t_hT])
    for sub in range(8):
        load_transpose(x_own[sub * 128:(sub + 1) * 128, :], 128, xT, t_xT, sub * 128, stage2[sub % 2], t_stage2[sub % 2],
                       [4, 5, 6, 7])
    load_transpose(x_c[:, :], NC_, xT, t_xT, NOWN, stage2[0], t_stage2[0], [4, 5, 6, 7])

    def norm_to_hT(gidx):
        for (c0, n) in TOKT:
            def out_fn(ch, rs, c0=c0, n=n):
                dve.op(lambda: V.scalar_tensor_tensor(hT[:, ch, c0:c0 + n], xT[:, ch, c0:c0 + n],
                                                      gall[:, gidx, ch:ch + 1], rs, ALU.mult, ALU.mult),
                       reads=[t_xT[ch], t_rstd, t_const], writes=[t_hT])
            rmsnorm_fm(xT, t_xT, (c0, n), gidx, out_fn, sq2, t_sq2, rstd, t_rstd, 3)

    norm_to_hT(0)
    c.dma(sp, xT_scr.ap(), R1[:, :], reads=t_xT)
    c.barrier()
    A1.reset(); A3.reset()
    QT = A1.take([128, 8, NTP], BF16); t_QT = T("QT")
    YC = A3.take([128, 8, NTP], BF16); t_YC = T("YC")
    kf = A3.take([128, 512], F32); t_kf = T("kf")
    kst = [A3.take([128, 512], BF16) for i in range(2)]; t_kst = [T("kst0"), T("kst1")]
    vst = [A3.take([128, 4, 128], BF16) for i in range(2)]; t_vst = [T("vst0"), T("vst1")]

    hrhs = lambda ch, c0, n: hT[:, ch, c0:c0 + n]
    QsT = sb("QsT", [128, 8, NS], F32); KsT = sb("KsT", [128, 8, NS], F32); t_sm = T("smp")
    Vs = sb("Vs", [NS, 8, 128], F32); Ks = sb("Ks", [NS, 8, 128], F32); Qs = sb("Qs", [NS, 8, 128], F32)
    ost = [A3.take([128, 4, 128], F32) for i in range(2)]; t_ost = [T("ost0"), T("ost1")]
    vf = A3.take([128, 512], F32); t_vf = T("vf")
    ulast = sb("ulast", [128, 8, 2], F32); uslast = sb("uslast", [128, 8, 2], F32); t_ulast = T("ulast")
    stcT = sb("stcT", [128, 8, 2], F32)
    for t_ in range(2):
        c.dma(sp, stcT[:, :, t_], st_conv[t_, :].rearrange("(c p) -> p c", p=128), writes=[t_const],
              allow_slow_non_contiguous=True)
    gcs = A3.take([128, NTP], F32); t_gcs = T("gcs")
    uext = A3.take([128, NOWN + 2], F32); usx = sb("usx", [128, NS + 2], F32); t_u = T("u")
    cva = A3.take([128, NOWN], F32); cvs = sb("cvs", [128, NS], F32); t_cv = T("cv")
    uh = sb("uh", [128, NH], F32); cvh = sb("cvh", [128, NH], F32)
    cnt = 0
    bank_sets = [[0, 1, 2], [4, 5, 6]]
    bsi = 0
    cnto = [0]
    pend = None
    for h in range(8):
        for which in range(3):
            wt, t_w = wload(w_in[:, which * 1024 + h * 128: which * 1024 + (h + 1) * 128], NCH)
            banks = bank_sets[bsi]; bsi ^= 1
            tb = 3 if banks[0] == 0 else 7
            proj(wt, t_w, NCH, hrhs, [t_hT], banks)
            if pend is not None:
                pend()

            def post(h=h, which=which, banks=banks, tb=tb):
                for (c0, n), b in zip(TOKT, banks):
                    cosap, sinap = csown[:, 0, c0:c0 + n], csown[:, 1, c0:c0 + n]
                    if which == 0:
                        rope(b, n, cosap, sinap, QT[:, h, c0:c0 + n], t_QT, rt, t_rt, tb)
                        if n == NC_:
                            dve.op(lambda h=h: V.tensor_tensor(QsT[:, h, :], rt["t1"][:, 0:NS], rt["t2"][:, 0:NS], ALU.add),
                                   reads=[t_rt["t1"], t_rt["t2"]], writes=[t_sm])
                            pe.group([lambda h=h: PEn.transpose(PS[tb][0:NS, 0:128], QsT[:, h, :], ident[:, :])],
                                     reads=[t_sm, t_const], writes=[t_PS[tb]])
                            dve.op(lambda h=h: V.tensor_copy(Qs[:, h, :], PS[tb][0:NS, 0:128]), reads=[t_PS[tb]], writes=[t_sm])
                    elif which == 1:
                        rope(b, n, cosap, sinap, kf[:, 0:n], t_kf, rt, t_rt, tb)
                        if n == 512:
                            kk = cnto[0] % 2; cnto[0] += 1
                            act.op(lambda kk=kk: S.copy(kst[kk][:, :], kf[:, :]), reads=[t_kf], writes=[t_kst[kk]])
                            c.dma(sp, kt_scr[h, :, NPAST + c0:NPAST + c0 + 512], kst[kk][:, :], reads=[t_kst[kk]])
                            sb0 = 12 + c0 // 256
                            dve.op(lambda sb0=sb0, h=h: V.tensor_reduce(ksum[:, h, sb0:sb0 + 2],
                                                                        kf[:, :].rearrange("p (a b) -> p a b", a=2), AX.X, ALU.add),
                                   reads=[t_kf], writes=[t_ksum])
                            pe.group([(lambda s4=s4: PEn.transpose(PS[tb][:, s4 * 128:(s4 + 1) * 128], kf[:, s4 * 128:(s4 + 1) * 128],
                                                                    ident[:, :])) for s4 in range(4)],
                                     reads=[t_kf, t_const], writes=[t_PS[tb]])
                            kk = cnto[0] % 2; cnto[0] += 1
                            evac(ost[kk][:, :, :].rearrange("p a b -> p (a b)"), PS[tb][:, :], [t_PS[tb]], [t_ost[kk]])
                            c.dma(sp, k_own[c0:c0 + 512, h, :].rearrange("(s p) d -> p s d", p=128), ost[kk][:, :, :], reads=[t_ost[kk]])
                        else:
                            dve.op(lambda h=h: V.tensor_copy(KsT[:, h, :], kf[:, 0:NS]), reads=[t_kf], writes=[t_sm])
                            pe.group([lambda h=h: PEn.transpose(PS[tb][0:NS, 0:128], KsT[:, h, :], ident[:, :])],
                                     reads=[t_sm, t_const], writes=[t_PS[tb]])
                            dve.op(lambda h=h: V.tensor_copy(Ks[:, h, :], PS[tb][0:NS, 0:128]), reads=[t_PS[tb]], writes=[t_sm])
                    else:
                        evac(vf[:, 0:n], PS[b][:, 0:n], [t_PS[b]], [t_vf])
                        if n == 512:
                            pe.group([(lambda s4=s4: PEn.transpose(PS[tb][:, s4 * 128:(s4 + 1) * 128], vf[:, s4 * 128:(s4 + 1) * 128],
                                                                    ident[:, :])) for s4 in range(4)],
                                     reads=[t_vf, t_const], writes=[t_PS[tb]])
                            kk = cnto[0] % 2; cnto[0] += 1
                            evac(ost[kk][:, :, :].rearrange("p a b -> p (a b)"), PS[tb][:, :], [t_PS[tb]], [t_ost[kk]])
                            c.dma(sp, v_own[c0:c0 + 512, h, :].rearrange("(s p) d -> p s d", p=128), ost[kk][:, :, :], reads=[t_ost[kk]])
                            dve.op(lambda kk=kk: V.tensor_copy(vst[kk][:, :, :].rearrange("p a b -> p (a b)"), PS[tb][:, :]),
                                   reads=[t_PS[tb]], writes=[t_vst[kk]])
                            st = (NPAST + c0) // 128
                            c.dma(sp, v_scr[h, st:st + 4, :, :].rearrange("s p d -> p s d"), vst[kk][:, :, :], reads=[t_vst[kk]])
                        else:
                            pe.group([lambda: PEn.transpose(PS[tb][0:NS, 0:128], vf[:, 0:NS], ident[:, :])],
                                     reads=[t_vf, t_const], writes=[t_PS[tb]])
                            dve.op(lambda h=h: V.tensor_copy(Vs[:, h, :], PS[tb][0:NS, 0:128]), reads=[t_PS[tb]], writes=[t_sm])

            pend = post
    pend()
    c.dma(sp, k_smp[:, :, :], Ks[:, :, :], reads=[t_sm])
    c.dma(sp, v_smp[:, :, :], Vs[:, :, :], reads=[t_sm])

    for cc in range(8):
        wts = [wload(w_in[:, 3072 + which * 1024 + cc * 128: 3072 + which * 1024 + (cc + 1) * 128], NCH) for which in (1, 2, 0)]
        banks = bank_sets[bsi]; bsi ^= 1
        proj(wts[0][0], wts[0][1], NCH, hrhs, [t_hT], banks)
        for (c0, n), b in zip(TOKT, banks):
            evac(gcs[:, c0:c0 + n], PS[b][:, 0:n], [t_PS[b]], [t_gcs])
        banks = bank_sets[bsi]; bsi ^= 1
        proj(wts[1][0], wts[1][1], NCH, hrhs, [t_hT], banks)
        for (c0, n), b in zip(TOKT, banks):
            if n == 512:
                dve.op(lambda c0=c0, b=b: V.tensor_tensor(uext[:, 2 + c0:2 + c0 + 512], PS[b][:, 0:512], gcs[:, c0:c0 + 512], ALU.mult),
                       reads=[t_PS[b], t_gcs], writes=[t_u])
            else:
                dve.op(lambda b=b: V.tensor_tensor(usx[:, 2:2 + NS], PS[b][:, 0:NS], gcs[:, NOWN:NOWN + NS], ALU.mult),
                       reads=[t_PS[b], t_gcs], writes=[t_u])
                dve.op(lambda b=b: V.tensor_tensor(uext[:, 0:2], PS[b][:, NC_ - 2:NC_], gcs[:, NT - 2:NT], ALU.mult),
                       reads=[t_PS[b], t_gcs], writes=[t_u])
                dve.op(lambda b=b: V.tensor_tensor(uh[:, 0:NH], PS[b][:, NS:NC_], gcs[:, NOWN + NS:NT], ALU.mult),
                       reads=[t_PS[b], t_gcs], writes=[t_u])
                dve.op(lambda cc=cc: V.tensor_copy(usx[:, 0:2], stcT[:, cc, :]), reads=[t_const], writes=[t_u])
        dve.op(lambda cc=cc: V.tensor_copy(ulast[:, cc, :], uext[:, NOWN:NOWN + 2]), reads=[t_u], writes=[t_ulast])
        dve.op(lambda cc=cc: V.tensor_copy(uslast[:, cc, :], usx[:, NS:NS + 2]), reads=[t_u], writes=[t_ulast])
        for (ux, cv, n) in ((uext, cva, NOWN), (usx, cvs, NS), (uh, cvh, NH - 2)):
            dve.op(lambda ux=ux, cv=cv, n=n, cc=cc: V.tensor_scalar(cv[:, 0:n], ux[:, 0:n], convw[:, cc, 0:1], None, ALU.mult),
                   reads=[t_u, t_const], writes=[t_cv])
            for j in (1, 2):
                dve.op(lambda ux=ux, cv=cv, n=n, cc=cc, j=j: V.scalar_tensor_tensor(cv[:, 0:n], ux[:, j:j + n], convw[:, cc, j:j + 1],
                                                                                 cv[:, 0:n], ALU.mult, ALU.add),
                       reads=[t_u, t_cv, t_const], writes=[t_cv])
        banks = bank_sets[bsi]; bsi ^= 1
        proj(wts[2][0], wts[2][1], NCH, hrhs, [t_hT], banks)
        for (c0, n), b in zip(TOKT, banks):
            if n == 512:
                dve.op(lambda c0=c0, b=b, cc=cc: V.tensor_tensor(YC[:, cc, c0:c0 + 512], PS[b][:, 0:512], cva[:, c0:c0 + 512], ALU.mult),
                       reads=[t_PS[b], t_cv], writes=[t_YC])
            else:
                dve.op(lambda b=b, cc=cc: V.tensor_tensor(YC[:, cc, NOWN:NOWN + NS], PS[b][:, 0:NS], cvs[:, 0:NS], ALU.mult),
                       reads=[t_PS[b], t_cv], writes=[t_YC])
                dve.op(lambda b=b, cc=cc: V.tensor_tensor(YC[:, cc, NOWN + NS + 2:NT], PS[b][:, NS + 2:NC_], cvh[:, 0:NH - 2], ALU.mult),
                       reads=[t_PS[b], t_cv], writes=[t_YC])
                dve.op(lambda cc=cc: V.memset(YC[:, cc, NOWN + NS:NOWN + NS + 2], 0.0), writes=[t_YC])
    for t_ in range(2):
        c.dma(sp, conv_p[t_, :].rearrange("(c p) -> p c", p=128), ulast[:, :, t_], reads=[t_ulast], allow_slow_non_contiguous=True)
        c.dma(sp, conv_s[t_, :].rearrange("(c p) -> p c", p=128), uslast[:, :, t_], reads=[t_ulast], allow_slow_non_contiguous=True)
    c.barrier()

    AT = hT
    t_AT = T("AT")
    if STAGE >= 3:
        kmT = sb("kmT", [128, 8, 16], BF16); t_km = T("km")
        dve.op(lambda: V.tensor_scalar(kmT[:, :, :], ksum[:, :, :], 1.0 / 256.0, None, ALU.mult), reads=[t_ksum], writes=[t_km])
        BT = A1.take([16, 8, NTP], BF16); t_BT = T("BT")
        g2 = sb("g2", [128, 8, 16], F32); top8 = sb("top8", [128, 8, 8], F32); thr = sb("thr", [128, 8], F32)
        selb = sb("selb", [128, 8, 16], BF16); t_g = T("g")
        qtiles = [(i * 128, 128, addc[:, i, :]) for i in range(8)] + [(NOWN + NS, NH, addh[:, :])]
        for (q0, qn, adc) in qtiles:
            gb = 3
            for h in range(8):
                pe.group([lambda h=h: PEn.matmul(PS[gb][0:qn, h * 16:(h + 1) * 16], QT[:, h, q0:q0 + qn], kmT[:, h, :], start=True, stop=True)],
                         reads=[t_QT, t_km], writes=[t_PS[gb]])
            dve.op(lambda: V.tensor_tensor(g2[0:qn, :, :], PS[gb][0:qn, 0:128].rearrange("p (a b) -> p a b", a=8),
                                           adc.unsqueeze(1).broadcast_to([qn, 8, 16]), ALU.add),
                   reads=[t_PS[gb], t_const], writes=[t_g])
            for h in range(8):
                dve.op(lambda h=h: V.max(top8[0:qn, h, :], g2[0:qn, h, :]), reads=[t_g], writes=[t_g])
            dve.op(lambda: V.tensor_scalar(thr[0:qn, :], top8[0:qn, :, 3], -1e29, None, ALU.max), reads=[t_g], writes=[t_g])
            dve.op(lambda: V.tensor_tensor(g2[0:qn, :, :], g2[0:qn, :, :], thr[0:qn, :].unsqueeze(2).broadcast_to([qn, 8, 16]), ALU.is_ge),
                   reads=[t_g], writes=[t_g])
            dve.op(lambda: V.tensor_scalar(selb[0:qn, :, :], g2[0:qn, :, :], -NEGB, NEGB, ALU.mult, ALU.add), reads=[t_g], writes=[t_g])
            pb16 = PS[7][:, :].bitcast(BF16)
            pe.group([(lambda h=h: PEn.transpose(pb16[0:16, h * 128:h * 128 + qn], selb[0:qn, h, :], identb[0:qn, 0:qn])) for h in range(8)],
                     reads=[t_g, t_const], writes=[t_PS[7]])
            dve.op(lambda: V.tensor_copy(BT[:, :, q0:q0 + qn], pb16[0:16, :].rearrange("p (a b) -> p a b", a=8)[:, :, 0:qn]),
                   reads=[t_PS[7]], writes=[t_BT])

        KT2 = [A1.take([128, 4096], BF16) for _ in range(2)]; t_KT2 = [T("KTa"), T("KTb")]
        VV2 = [A1.take([128, 32, 128], BF16) for _ in range(2)]; t_VV2 = [T("Va"), T("Vb")]
        A3.off = (8 * NTP * 2 + 31) // 32 * 32
        PT = [A3.take([128, 512], BF16) for i in range(3)]; t_PT = [T(f"PT{i}") for i in range(3)]
        rden = A3.take([128, 512], F32); t_rden = T("rden")
        pti = 0
        sctr = [0]
        for h in range(8):
            kb = h % 2
            c.dma(sp, KT2[kb][:, :], kt_scr[h, :, :], writes=[t_KT2[kb]])
            c.dma(sp, VV2[kb][:, :, :], v_scr[h, :, :, :].rearrange("s p d -> p s d"), writes=[t_VV2[kb]])
            for qi, (q0, qn, nkt) in enumerate([(0, 512, 28), (512, 512, 32), (NOWN + NS, NH, 24)]):
                ob, db = 4, 5

                def emit_S(kt, h=h, kb=kb, qi=qi, q0=q0, qn=qn):
                    sbk = sctr[0] % 3; sctr[0] += 1
                    fns = [lambda: PEn.matmul(PS[sbk][:, 0:qn], KT2[kb][:, kt * 128:(kt + 1) * 128], QT[:, h, q0:q0 + qn],
                                              start=True, stop=False)]
                    extra = None
                    if qi == 2:
                        extra = hb[:, kt, :]
                    elif kt >= 24:
                        off = (kt - 24) * 128 - q0
                        if off >= 0:
                            extra = cm[:, off // 128, :]
                    fns.append(lambda: PEn.matmul(PS[sbk][:, 0:qn], eall[:, kt // 2, :], BT[:, h, q0:q0 + qn],
                                                  start=False, stop=(extra is None)))
                    if extra is not None:
                        fns.append(lambda: PEn.matmul(PS[sbk][:, 0:qn], identb[:, :], extra, start=False, stop=True))
                    pe.group(fns, reads=[t_KT2[kb], t_QT, t_BT, t_const], writes=[t_PS[sbk]])
                    return sbk

                LOOK = 2
                sb_of = {}
                for kt in range(min(LOOK, nkt)):
                    sb_of[kt] = emit_S(kt)
                for kt in range(nkt):
                    if kt + LOOK < nkt:
                        sb_of[kt + LOOK] = emit_S(kt + LOOK)
                    sbk = sb_of.pop(kt)
                    p = pti; pti = (pti + 1) % 3
                    act.op(lambda sbk=sbk, p=p: S.activation(PT[p][:, 0:qn], PS[sbk][:, 0:qn], AF.Exp, scale=SCALE),
                           reads=[t_PS[sbk]], writes=[t_PT[p]])
                    pe.deps([t_PT[p], t_VV2[kb], t_const], [t_PS[ob], t_PS[db]] if kt == 0 else [])
                    PEn.matmul(PS[ob][:, 0:qn], VV2[kb][:, kt, :], PT[p][:, 0:qn], start=(kt == 0), stop=(kt == nkt - 1))
                    ins = PEn.matmul(PS[db][:, 0:qn], onesb[:, :], PT[p][:, 0:qn], start=(kt == 0), stop=(kt == nkt - 1))
                    pe.count += 1
                    ins.then_inc(pe.sem, 1)
                    tok = (pe, pe.count)
                    pe.commit(tok, [t_PT[p], t_VV2[kb]], [])
                    t_PS[ob].w = tok; t_PS[db].w = tok
                    if kt == 0:
                        t_PS[ob].rd = []; t_PS[db].rd = []
                dve.op(lambda: V.reciprocal(rden[:, 0:qn], PS[db][:, 0:qn]), reads=[t_PS[db]], writes=[t_rden])
                dve.op(lambda h=h: V.tensor_tensor(AT[:, h, q0:q0 + qn], PS[ob][:, 0:qn], rden[:, 0:qn], ALU.mult),
                       reads=[t_PS[ob], t_rden], writes=[t_AT])
    else:
        dve.op(lambda: V.memset(AT[:, 0:8, :], 0.0), writes=[t_AT])
    if STAGE >= 5:
        c.barrier()
        A1.reset()
        kpg = [A1.take([128, 2, 1024], F32) for _ in range(2)]; t_kpg = [T("kpg0"), T("kpg1")]
        Kg = [A1.take([128, 24, 128], F32) for _ in range(2)]; t_Kg = [T("Kg0"), T("Kg1")]
        Vg = [A1.take([128, 24, 128], F32) for _ in range(2)]; t_Vg = [T("Vg0"), T("Vg1")]
        A3.off = (8 * NTP * 2 + 31) // 32 * 32
        kmS = A3.take([128, 8, 64], F32); t_kmS = T("kmS")
        G = A3.take([128, 32, 64], F32); top8s = A3.take([128, 32, 8], F32); idxs = A3.take([128, 32, 8], U32); t_G = T("G")
        nf = A3.take([128, 32, 3], F32)
        OH = A3.take([128, 12, 64], F32); PR = A3.take([128, 12, 64], F32); physf = A3.take([128, 12, 2], F32); t_OH = T("OH")
        idxG = A3.take([128, 8, 24], I32); t_idxG = T("idxG")
        Qrep = [A3.take([128, 128], F32) for _ in range(2)]; t_Qrep = [T("qr0"), T("qr1")]
        qb = A3.take([128, 4, 128], F32); t_qb = T("qb")
        junk = A3.take([128, 128], F32); t_junk = T("junk")
        Ssm = A3.take([128, 24], F32); Psm = A3.take([128, 24], F32); t_S = T("Ssm"); t_P = T("Psm")
        Sown = A3.take([4, 4], F32); Pown = A3.take([4, 4], F32)
        den = A3.take([128, 4], F32); rd = A3.take([128, 4], F32); t_den = T("den")
        ptb = sb("ptb", [128, 128], I32); ptf = sb("ptf", [128, 128], F32); idxP = sb("idxP", [128, 128], I32); t_pt = T("pt")
        pio = sb("pio", [128, 2], F32); iota64 = sb("iota64", [128, 64], F32)
        onesf = sb("onesf", [128, 128], F32)
        esel = sb("esel", [4, 4, 128], F32); smask = sb("smask", [4, 4], F32)
        dve.op(lambda: V.memset(onesf[:, :], 1.0), writes=[t_const])
        c.dma(sp, esel[:, :, :], c_esel.ap(), writes=[t_const])
        c.dma(sp, smask[:, :], c_smask.ap(), writes=[t_const])
        c.dma(sp, pio[:, :], c_pio.ap(), writes=[t_const])
        c.dma(sp, iota64[:, :], c_iota64.ap(), writes=[t_const])
        c.dma(sp, ptb[:, :], ptab.ap().partition_broadcast(128), writes=[t_pt])
        dve.op(lambda: V.tensor_copy(ptf[:, :], ptb[:, :]), reads=[t_pt], writes=[t_pt])
        dve.op(lambda: V.tensor_scalar(idxP[:, :], ptf[:, :], 128.0, pio[:, 0:1], ALU.mult, ALU.add), reads=[t_pt, t_const], writes=[t_pt])
        ck_rows = cache_k.ap().rearrange("n t h d -> (n t) (h d)")
        ck_hrows = cache_k.ap().rearrange("n t h d -> (n t h) d")
        cv_hrows = cache_v.ap().rearrange("n t h d -> (n t h) d")
        for n in range(64):
            kb = n % 2
            for pg in range(2):
                c.swdma(4 + kb, kpg[kb][:, pg, :], ck_rows, reads=[t_pt], writes=[t_kpg[kb]],
                        indirect=idxP[:, 2 * n + pg:2 * n + pg + 1])
            for h in range(8):
                col = h * 64 + n
                pe.group([lambda kb=kb, h=h, col=col: PEn.matmul(PS[0][:, col:col + 1], kpg[kb][:, 0, h * 128:(h + 1) * 128], onesf[:, 0:1],
                                                                 start=True, stop=False),
                          lambda kb=kb, h=h, col=col: PEn.matmul(PS[0][:, col:col + 1], kpg[kb][:, 1, h * 128:(h + 1) * 128], onesf[:, 0:1],
                                                                 start=False, stop=True)],
                         reads=[t_kpg[kb], t_const], writes=[t_PS[0]])
        dve.op(lambda: V.tensor_scalar(kmS[:, :, :].rearrange("p a b -> p (a b)"), PS[0][:, :], 1.0 / 256.0, None, ALU.mult),
               reads=[t_PS[0]], writes=[t_kmS])
        for hq in range(32):
            h, q = hq // 4, hq % 4
            bank = 4 + hq // 8; col = (hq % 8) * 64
            k2 = hq % 2
            dve.op(lambda h=h, q=q, k2=k2: V.tensor_copy(Qrep[k2][:, :], QsT[:, h, q:q + 1].broadcast_to([128, 128])),
                   reads=[t_sm], writes=[t_Qrep[k2]])
            pe.group([lambda h=h, k2=k2, bank=bank, col=col: PEn.matmul(PS[bank][:, col:col + 64], Qrep[k2][:, :], kmS[:, h, :],
                                                                        start=True, stop=True)],
                     reads=[t_Qrep[k2], t_kmS], writes=[t_PS[bank]])
        for b4 in range(4):
            dve.op(lambda b4=b4: V.tensor_copy(G[:, b4 * 8:(b4 + 1) * 8, :].rearrange("p a b -> p (a b)"), PS[4 + b4][:, :]),
                   reads=[t_PS[4 + b4]], writes=[t_G])
        for hq in range(32):
            dve.op(lambda hq=hq: V.max(top8s[:, hq, :], G[:, hq, :]), reads=[t_G], writes=[t_G])
            dve.op(lambda hq=hq: V.max_index(idxs[:, hq, :], top8s[:, hq, :], G[:, hq, :]), reads=[t_G], writes=[t_G])
        dve.op(lambda: V.tensor_copy(nf[:, :, :], idxs[:, :, 0:3]), reads=[t_G], writes=[t_G])
        for h in range(8):
            dve.op(lambda h=h: V.tensor_tensor(OH[:, :, :], iota64[:, :].unsqueeze(1).broadcast_to([128, 12, 64]),
                                               nf[:, h * 4:(h + 1) * 4, :].rearrange("p a b -> p (a b)").unsqueeze(2).broadcast_to([128, 12, 64]),
                                               ALU.is_equal), reads=[t_G, t_const], writes=[t_OH])
            for pgi in range(2):
                dve.op(lambda pgi=pgi: V.tensor_tensor(PR[:, :, :], OH[:, :, :],
                                                       ptf[:, :].rearrange("p (n g) -> p n g", g=2)[:, :, pgi].unsqueeze(1).broadcast_to([128, 12, 64]),
                                                       ALU.mult), reads=[t_OH, t_pt], writes=[t_OH])
                dve.op(lambda pgi=pgi: V.tensor_reduce(physf[:, :, pgi], PR[:, :, :], AX.X, ALU.add), reads=[t_OH], writes=[t_OH])
            dve.op(lambda h=h: V.tensor_scalar(idxG[:, h, :], physf[:, :, :].rearrange("p a b -> p (a b)"), 1024.0, pio[:, 1:2], ALU.mult, ALU.add),
                   reads=[t_OH, t_const], writes=[t_idxG])
        for h in range(8):
            gb = h % 2
            for s_ in range(24):
                c.swdma(6 + gb, Kg[gb][:, s_, :], ck_hrows, reads=[t_idxG], writes=[t_Kg[gb]], indirect=idxG[:, h, s_:s_ + 1],
                        element_offset=h * 128)
            for s_ in range(24):
                c.swdma(8 + gb, Vg[gb][:, s_, :], cv_hrows, reads=[t_idxG], writes=[t_Vg[gb]], indirect=idxG[:, h, s_:s_ + 1],
                        element_offset=h * 128)
            pe.group([(lambda q=q, h=h: PEn.matmul(PS[1][:, q * 128:(q + 1) * 128], esel[0:4, q, :], Qs[0:4, h, :], start=True, stop=True))
                      for q in range(4)], reads=[t_sm, t_const], writes=[t_PS[1]])
            act.op(lambda: S.copy(qb[:, :, :].rearrange("p a b -> p (a b)"), PS[1][:, :]), reads=[t_PS[1]], writes=[t_qb])
            for s_ in range(24):
                dve.op(lambda s_=s_, gb=gb: V.scalar_tensor_tensor(junk[:, :], Kg[gb][:, s_, :], SCALE, qb[:, s_ // 6, :], ALU.mult, ALU.mult,
                                                                   accum_out=Ssm[:, s_:s_ + 1]),
                       reads=[t_Kg[gb], t_qb], writes=[t_junk, t_S])
            for q in range(4):
                dve.op(lambda q=q, h=h: V.scalar_tensor_tensor(junk[0:4, :], Ks[0:4, h, :], SCALE, qb[0:4, q, :], ALU.mult, ALU.mult,
                                                               accum_out=Sown[0:4, q:q + 1]),
                       reads=[t_sm, t_qb], writes=[t_junk, t_S])
            dve.op(lambda: V.tensor_tensor(Sown[0:4, :], Sown[0:4, :], smask[0:4, :], ALU.add), reads=[t_S, t_const], writes=[t_S])
            act.op(lambda: S.activation(Psm[:, :], Ssm[:, :], AF.Exp), reads=[t_S], writes=[t_P])
            act.op(lambda: S.activation(Pown[0:4, :], Sown[0:4, :], AF.Exp), reads=[t_S], writes=[t_P])
            pe.group([lambda: PEn.matmul(PS[2][:, 0:24], onesf[:, :], Psm[:, :], start=True, stop=True)],
                     reads=[t_P, t_const], writes=[t_PS[2]])
            dve.op(lambda: V.tensor_reduce(den[:, 0:4], PS[2][:, 0:24].rearrange("p (a b) -> p a b", a=4), AX.X, ALU.add),
                   reads=[t_PS[2]], writes=[t_den])
            pe.group([lambda: PEn.matmul(PS[2][:, 32:36], onesf[0:4, :], Pown[0:4, :], start=True, stop=True)],
                     reads=[t_P, t_const], writes=[t_PS[2]])
            dve.op(lambda: V.tensor_tensor(den[:, 0:4], den[:, 0:4], PS[2][:, 32:36], ALU.add), reads=[t_PS[2], t_den], writes=[t_den])
            dve.op(lambda: V.reciprocal(rd[:, 0:4], den[:, 0:4]), reads=[t_den], writes=[t_den])
            for q in range(4):
                fns = [(lambda q=q, i=i, gb=gb: PEn.matmul(PS[3][:, q:q + 1], Vg[gb][:, q * 6 + i, :], Psm[:, q * 6 + i:q * 6 + i + 1],
                                                           start=(i == 0), stop=False)) for i in range(6)]
                fns.append(lambda q=q, h=h: PEn.matmul(PS[3][:, q:q + 1], Vs[0:4, h, :], Pown[0:4, q:q + 1], start=False, stop=True))
                pe.group(fns, reads=[t_Vg[gb], t_P, t_sm], writes=[t_PS[3]])
            dve.op(lambda h=h: V.tensor_tensor(AT[:, h, NOWN:NOWN + NS], PS[3][:, 0:4], rd[:, 0:4], ALU.mult),
                   reads=[t_PS[3], t_den], writes=[t_AT])
    else:
        dve.op(lambda: V.memset(AT[:, 0:8, NOWN:NOWN + NS], 0.0), writes=[t_AT])
    c.barrier()

    c.dma(sp, R1[:, :], xT_scr.ap(), writes=t_xT)

    def resid_add(banks, oc, scale_ap=None):
        for (c0, n), b in zip(TOKT, banks):
            if scale_ap is None:
                dve.op(lambda c0=c0, n=n, b=b: V.tensor_tensor(xT[:, oc, c0:c0 + n], PS[b][:, 0:n], xT[:, oc, c0:c0 + n], ALU.add),
                       reads=[t_PS[b], t_xT[oc]], writes=[t_xT[oc]])
            else:
                dve.op(lambda c0=c0, n=n, b=b: V.scalar_tensor_tensor(xT[:, oc, c0:c0 + n], PS[b][:, 0:n], scale_ap, xT[:, oc, c0:c0 + n],
                                                                      ALU.mult, ALU.add),
                       reads=[t_PS[b], t_xT[oc], t_const], writes=[t_xT[oc]])

    for oc in range(NCH):
        wt, t_w = wload(w_o[:, oc * 128:(oc + 1) * 128], NCH)
        banks = bank_sets[bsi]; bsi ^= 1
        proj(wt, t_w, NCH, lambda ch, c0, n: (AT[:, ch, c0:c0 + n] if ch < 8 else YC[:, ch - 8, c0:c0 + n]), [t_AT, t_YC], banks)
        resid_add(banks, oc)
    c.barrier()

    def ffn(layer, gidx):
        norm_to_hT(gidx)
        A3.reset()
        NG = 4
        per = NFT // NG
        aT = A3.take([128, per, NTP], BF16); t_aT = T("aT")
        sg = [A3.take([128, 512], F32) for i in range(2)]; t_sg = [T("sg0"), T("sg1")]
        sgi = 0
        for g in range(NG):
            for fi in range(per):
                ft = g * per + fi
                wg, t_wg = wload(w_gate[layer, :, ft * 128:(ft + 1) * 128], NCH)
                wu, t_wu = wload(w_up[layer, :, ft * 128:(ft + 1) * 128], NCH)
                proj(wg, t_wg, NCH, hrhs, [t_hT], [0, 1, 2])
                proj(wu, t_wu, NCH, hrhs, [t_hT], [4, 5, 6])
                for (c0, n), bg, bu in zip(TOKT, [0, 1, 2], [4, 5, 6]):
                    k = sgi; sgi ^= 1
                    act.op(lambda k=k, bg=bg, n=n: S.activation(sg[k][:, 0:n], PS[bg][:, 0:n], AF.Silu), reads=[t_PS[bg]], writes=[t_sg[k]])
                    dve.op(lambda k=k, bu=bu, n=n, c0=c0, fi=fi: V.tensor_tensor(aT[:, fi, c0:c0 + n], PS[bu][:, 0:n], sg[k][:, 0:n], ALU.mult),
                           reads=[t_PS[bu], t_sg[k]], writes=[t_aT])
            for oc in range(NCH):
                wd, t_wd = wload(w_down[layer, g * per * 128:(g + 1) * per * 128, oc * 128:(oc + 1) * 128], per)
                banks = bank_sets[bsi_box[0]]; bsi_box[0] ^= 1
                proj(wd, t_wd, per, lambda ch, c0, n: aT[:, ch, c0:c0 + n], [t_aT], banks)
                resid_add(banks, oc)
        c.barrier()

    bsi_box = [bsi]
    if STAGE >= 4:
        ffn(0, 1)

    if STAGE >= 4:
        A3.reset()
        AR = A3
        dT = hT; t_dT = t_hT
        stpT = AR.take([128, NCH, NPH], F32)
        h1l = AR.take([128, NCH, NPH], F32); h1s = AR.take([128, NCH, NPH], F32); t_h1l = T("h1l")
        stp_sb = AR.take([NPH, D], F32); t_stp = T("stp")
        c.dma(sp, stp_sb[:, :], st_pool[:, :], writes=[t_stp])
        for q4 in range(4):
            pe.group([(lambda k=k: PEn.transpose(PS[7][:, k * 128:k * 128 + NPH], stp_sb[0:NPH, (q4 * 4 + k) * 128:(q4 * 4 + k + 1) * 128],
                                                  ident[0:NPH, 0:NPH])) for k in range(4)], reads=[t_stp, t_const], writes=[t_PS[7]])
            dve.op(lambda q4=q4: V.tensor_copy(stpT[:, q4 * 4:q4 * 4 + 4, :], PS[7][:, :].rearrange("p (k n) -> p k n", k=4)[:, :, 0:NPH]),
                   reads=[t_PS[7]], writes=[t_stp])
        rs_all = AR.take([128, NTP], F32); t_rsall = T("rsall")
        for (c0, n) in TOKT:
            rmsnorm_fm(xT, t_xT, (c0, n), 2, lambda ch, rs: None, sq2, t_sq2, rstd, t_rstd, 3)
            dve.op(lambda c0=c0, n=n: V.tensor_copy(rs_all[:, c0:c0 + n], rstd[:, 0:n]), reads=[t_rstd], writes=[t_rsall])
        hx = AR.take([128, NPH + NOWN], F32); hxs = AR.take([128, NPH + NS], F32); t_hx = T("hx")
        sA = AR.take([128, NPH + NOWN], F32); sB = AR.take([128, NPH + NOWN], F32); t_s = T("s")
        for ch in range(NCH):
            g = ch // 4
            gsc = gall[:, 2, ch:ch + 1]
            dve.op(lambda ch=ch: V.scalar_tensor_tensor(hx[:, NPH:NPH + NOWN], xT[:, ch, 0:NOWN], gsc, rs_all[:, 0:NOWN], ALU.mult, ALU.mult),
                   reads=[t_xT[ch], t_rsall, t_const], writes=[t_hx])
            dve.op(lambda ch=ch: V.scalar_tensor_tensor(hx[:, 0:NPH], xT[:, ch, NT - NPH:NT], gsc, rs_all[:, NT - NPH:NT], ALU.mult, ALU.mult),
                   reads=[t_xT[ch], t_rsall, t_const], writes=[t_hx])
            dve.op(lambda: V.tensor_tensor(hx[:, 0:NPH], hx[:, 0:NPH], hv[:, 0:NPH], ALU.mult), reads=[t_hx, t_const], writes=[t_hx])
            dve.op(lambda ch=ch: V.scalar_tensor_tensor(hxs[:, NPH:NPH + NS], xT[:, ch, NOWN:NOWN + NS], gsc, rs_all[:, NOWN:NOWN + NS], ALU.mult, ALU.mult),
                   reads=[t_xT[ch], t_rsall, t_const], writes=[t_hx])
            dve.op(lambda ch=ch: V.tensor_copy(hxs[:, 0:NPH], stpT[:, ch, :]), reads=[t_stp], writes=[t_hx])
            dve.op(lambda ch=ch: V.tensor_copy(h1l[:, ch, :], hx[:, NOWN:NOWN + NPH]), reads=[t_hx], writes=[t_h1l])
            dve.op(lambda ch=ch: V.tensor_copy(h1s[:, ch, :], hxs[:, NS:NS + NPH]), reads=[t_hx], writes=[t_h1l])
            w = 2 ** (g + 1)
            for (src, L, c0out, nout) in ((hx, NPH + NOWN, 0, NOWN), (hxs, NPH + NS, NOWN, NS)):
                cur = src
                sh = 1
                bufs = [sA, sB]
                bi = 0
                while sh < w:
                    nxt = bufs[bi]; bi ^= 1
                    dve.op(lambda cur=cur, nxt=nxt, sh=sh, L=L: V.tensor_tensor(nxt[:, sh:L], cur[:, sh:L], cur[:, 0:L - sh], ALU.add),
                           reads=[t_hx, t_s], writes=[t_s])
                    if sh > 1 or True:
                        dve.op(lambda cur=cur, nxt=nxt, sh=sh: V.tensor_copy(nxt[:, 0:sh], cur[:, 0:sh]), reads=[t_hx, t_s], writes=[t_s])
                    cur = nxt
                    sh *= 2
                dve.op(lambda cur=cur, src=src, c0out=c0out, nout=nout, ch=ch, w=w: V.scalar_tensor_tensor(
                    dT[:, ch, c0out:c0out + nout], cur[:, NPH:NPH + nout], 1.0 / w, src[:, NPH:NPH + nout], ALU.mult, ALU.subtract),
                    reads=[t_s, t_hx], writes=[t_dT])
                if nout == NOWN:
                    fx = sA if cur is sB else sB
                    dve.op(lambda cur=cur, fx=fx, g=g: V.tensor_tensor(fx[:, 0:NPH], cur[:, NPH:2 * NPH], invc[:, g, 0:NPH], ALU.mult),
                           reads=[t_s, t_const], writes=[t_s])
                    dve.op(lambda fx=fx, src=src, ch=ch: V.tensor_tensor(dT[:, ch, 0:NPH], fx[:, 0:NPH], src[:, NPH:2 * NPH], ALU.subtract),
                           reads=[t_s, t_hx], writes=[t_dT])
            dve.op(lambda ch=ch: V.memset(dT[:, ch, NOWN + NS:NT], 0.0), writes=[t_dT])
        for (src, dst) in ((h1l, pool_p), (h1s, pool_s)):
            po = stp_sb; t_po = t_stp
            for q4 in range(4):
                pe.group([(lambda k=k: PEn.transpose(PS[7][0:NPH, k * 128:(k + 1) * 128], src[:, q4 * 4 + k, :], ident[:, :])) for k in range(4)],
                         reads=[t_h1l, t_const], writes=[t_PS[7]])
                dve.op(lambda q4=q4, po=po: V.tensor_copy(po[:, q4 * 512:(q4 + 1) * 512], PS[7][0:NPH, :]), reads=[t_PS[7]], writes=[t_po])
            c.dma(sp, dst[:, :], po[:, :], reads=[t_po])
        for g in range(4):
            for oc4 in range(4):
                oc = g * 4 + oc4
                wt, t_w = wload(w_pool[g, :, oc4 * 128:(oc4 + 1) * 128], 4)
                banks = bank_sets[bsi_box[0]]; bsi_box[0] ^= 1
                proj(wt, t_w, 4, lambda ch, c0, n, g=g: dT[:, g * 4 + ch, c0:c0 + n], [t_dT], banks)
                resid_add(banks, oc, psc[:, oc:oc + 1])
        c.barrier()
        ffn(1, 3)

    A3.reset()
    stage2 = [A3.take([128, D], F32) for i in range(2)]; t_stage2 = [T("st0"), T("st1")]
    yT = A3.take([128, NCH, 128], F32); t_yT = T("yT")
    for (c0, n) in TOKT:
        rmsnorm_fm(xT, t_xT, (c0, n), 4, lambda ch, rs: None, sq2, t_sq2, rstd, t_rstd, 3)
        nsub = 4 if n == 512 else 1
        for sub in range(nsub):
            nr = 128 if n == 512 else NS
            k2 = sub % 2
            for ch in range(NCH):
                dve.op(lambda ch=ch, sub=sub, nr=nr, c0=c0: V.scalar_tensor_tensor(
                    yT[:, ch, 0:nr], xT[:, ch, c0 + sub * 128:c0 + sub * 128 + nr], gall[:, 4, ch:ch + 1],
                    rstd[:, sub * 128:sub * 128 + nr], ALU.mult, ALU.mult),
                    reads=[t_xT[ch], t_rstd, t_const], writes=[t_yT])
            for q4 in range(4):
                b = 4 + q4
                pe.group([(lambda k=k: PEn.transpose(PS[b][0:nr, k * 128:(k + 1) * 128], yT[:, q4 * 4 + k, 0:nr], ident[:, :]))
                          for k in range(4)], reads=[t_yT, t_const], writes=[t_PS[b]])
                evac(stage2[k2][0:nr, q4 * 512:(q4 + 1) * 512], PS[b][0:nr, :], [t_PS[b]], [t_stage2[k2]])
            if n == 512:
                c.dma(sp, y_own[c0 + sub * 128:c0 + (sub + 1) * 128, :], stage2[k2][:, :], reads=[t_stage2[k2]])
            else:
                c.dma(sp, y_smp[:, :], stage2[k2][0:NS, :], reads=[t_stage2[k2]])
    c.finish()
    return nc


_CACHE = {}


def _consts(j):
    bf = ml_dtypes.bfloat16
    half = 64
    inv = (np.float32(10000.0) ** (-np.arange(half, dtype=np.float32) / np.float32(half))).astype(np.float32)

    def cs(pos):
        ang = pos.astype(np.float32)[:, None] * inv[None, :]
        co, si = np.cos(ang).astype(np.float32), np.sin(ang).astype(np.float32)
        return np.stack([np.concatenate([co, co], 1).T, np.concatenate([si, si], 1).T], axis=1)
    T0 = j * 1024
    pos = np.zeros(NTP, np.float32)
    pos[0:NOWN] = T0 + np.arange(NOWN)
    pos[NOWN:NOWN + NS] = 16384 + np.arange(NS)
    pos[NOWN + NS:NT] = np.maximum(T0 - NH + np.arange(NH), 0)
    d = {}
    d["cs_own"] = np.ascontiguousarray(cs(pos))
    d["cs_past"] = np.ascontiguousarray(cs(np.arange(NPAST, dtype=np.float32)))
    d["c_ident"] = np.eye(128, dtype=np.float32)
    pr = np.zeros((128, 128), np.float32)
    for dd in range(64):
        pr[dd + 64, dd] = -1.0
        pr[dd, dd + 64] = 1.0
    d["c_prot"] = pr.astype(bf)
    ea = np.zeros((16, 16, 128), np.float32)
    for r in range(16):
        ea[r, r, :] = 1.0
    d["c_eall"] = ea.astype(bf)
    k = np.arange(128)[:, None]; q = np.arange(512)[None, :]
    cmm = np.stack([np.where(off * 128 + k <= q, 0.0, NEGB) for off in range(4)], axis=1)
    d["c_cm"] = cmm.astype(bf)
    hbm = np.zeros((128, 24, NH), np.float32)
    for kt in range(24):
        for qq in range(NH):
            p = T0 - NH + qq
            s = kt * 128 + np.arange(128)
            vis = (s <= p) if j > 0 else np.full(128, kt == 0)
            hbm[:, kt, qq] = np.where(vis, 0.0, NEGB)
    d["c_hb"] = hbm.astype(bf)
    adc = np.zeros((128, 8, 16), np.float32)
    for qt in range(8):
        ob = (qt * 128) // 256
        for sbk in range(16):
            if sbk < 12:
                v = 0.0 if sbk < 4 * j else -2e30
            else:
                o = sbk - 12
                v = 0.0 if o < ob else (1e30 if o == ob else -2e30)
            adc[:, qt, sbk] = v
    d["c_addc"] = adc
    adh = np.full((NH, 16), -2e30, np.float32)
    if j > 0:
        adh[:, :4 * j - 1] = 0.0
        adh[:, 4 * j - 1] = 1e30
    else:
        adh[:, 0] = 1e30
    d["c_addh"] = adh
    d["c_hv"] = np.full((128, NH), 1.0 if j > 0 else 0.0, np.float32)
    ic = np.zeros((128, 4, 16), np.float32)
    for g in range(4):
        w = 2 ** (g + 1)
        for i in range(16):
            ic[:, g, i] = 1.0 / min(T0 + i + 1, w)
    d["c_invc"] = ic
    sm = np.zeros((4, 4), np.float32)
    for kk in range(4):
        for qq in range(4):
            sm[kk, qq] = 0.0 if kk <= qq else NEGB
    d["c_smask"] = sm
    es = np.zeros((4, 4, 128), np.float32)
    for r in range(4):
        es[r, r, :] = 1.0
    d["c_esel"] = es
    d["c_pio"] = np.stack([np.arange(128, dtype=np.float32), 8.0 * np.arange(128, dtype=np.float32)], axis=1)
    d["c_iota64"] = np.ascontiguousarray(np.broadcast_to(np.arange(64, dtype=np.float32)[None, :], (128, 64)))
    return d


def kernel(x_prompt, x_sample, cache_k, cache_v, page_table, state_conv, state_pool, norm_mix, norm_ffn, norm_final,
           w_in, conv_w, w_o, w_pool, pool_scale, w_gate, w_up, w_down):
    f = lambda a: np.ascontiguousarray(np.asarray(a, dtype=np.float32))
    x_prompt, x_sample = f(x_prompt), f(x_sample)
    if "nc" not in _CACHE:
        _CACHE["nc"] = build_program()
    nc = _CACHE["nc"]
    fm = lambda v: np.asarray(v, np.float32).reshape(16, 128).T
    g_all = np.ascontiguousarray(np.stack([fm(norm_mix[0]), fm(norm_ffn[0]), fm(norm_mix[1]), fm(norm_ffn[1]), fm(norm_final)], axis=1))
    shared = {
        "g_all": g_all, "w_in": f(w_in)[0], "conv_w": np.ascontiguousarray(f(conv_w)[0].reshape(3, 8, 128).transpose(2, 1, 0)),
        "w_o": f(w_o)[0], "w_pool": f(w_pool)[0], "pscale": np.ascontiguousarray(fm(pool_scale[0])),
        "w_gate": f(w_gate), "w_up": f(w_up), "w_down": f(w_down),
    }
    if STAGE >= 5:
        shared["cache_k"] = f(cache_k)[0]; shared["cache_v"] = f(cache_v)[0]
    in_maps = []
    for c in range(8):
        b, j = c // 4, c % 4
        T0 = j * 1024
        m = dict(shared)
        m["x_own"] = np.ascontiguousarray(x_prompt[b, T0:T0 + 1024])
        halo = np.zeros((NH, D), np.float32)
        if j > 0:
            halo[:] = x_prompt[b, T0 - NH:T0]
        m["x_c"] = np.ascontiguousarray(np.concatenate([x_sample[c], halo], axis=0))
        m["x_past"] = np.ascontiguousarray(x_prompt[b, 0:NPAST])
        m["ptab"] = np.ascontiguousarray(np.asarray(page_table, np.int32)[c:c + 1])
        m["st_conv"] = np.ascontiguousarray(f(state_conv)[0, c])
        m["st_pool"] = np.ascontiguousarray(f(state_pool)[0, c])
        m.update(_consts(j))
        in_maps.append(m)
    res = run_bass_kernel_spmd(nc, in_maps, core_ids=list(range(8)))
    R = res.results
    y_prompt = np.stack([np.concatenate([R[b * 4 + j]["y_own"] for j in range(4)], 0) for b in range(2)], 0)
    y_sample = np.stack([R[c]["y_smp"] for c in range(8)], 0)
    k_prompt = np.stack([np.concatenate([R[b * 4 + j]["k_own"] for j in range(4)], 0) for b in range(2)], 0)[None]
    v_prompt = np.stack([np.concatenate([R[b * 4 + j]["v_own"] for j in range(4)], 0) for b in range(2)], 0)[None]
    k_sample = np.stack([R[c]["k_smp"] for c in range(8)], 0)[None]
    v_sample = np.stack([R[c]["v_smp"] for c in range(8)], 0)[None]
    conv_prompt = np.stack([R[3]["conv_p"], R[7]["conv_p"]], 0)[None]
    conv_sample = np.stack([R[c]["conv_s"] for c in range(8)], 0)[None]
    pool_prompt = np.stack([R[3]["pool_p"], R[7]["pool_p"]], 0)[None]
    pool_sample = np.stack([R[c]["pool_s"] for c in range(8)], 0)[None]
    outs = (y_prompt, y_sample, k_prompt, v_prompt, k_sample, v_sample, conv_prompt, conv_sample, pool_prompt, pool_sample)
    return tuple(np.ascontiguousarray(o.astype(np.float32)) for o in outs)
```

```python
import os
import numpy as np
import ml_dtypes
import concourse.bass as bass
import concourse.mybir as mybir
from concourse.bass_utils import run_bass_kernel_spmd

F32 = mybir.dt.float32
BF16 = mybir.dt.bfloat16
I32 = mybir.dt.int32
U32 = mybir.dt.uint32
AF = mybir.ActivationFunctionType
ALU = mybir.AluOpType
AX = mybir.AxisListType

D = 2048
NCH = 16
DFF = 5632
NFT = 44
NPAST = 3072
NOWN = 1024
NS = 4
NH = 17
NPH = 15
NC_ = NS + NH
NT = NOWN + NC_
NTP = 1048
SCALE = 128 ** -0.5
NEGB = -30000.0
TOKT = [(0, 512), (512, 512), (1024, NC_)]
STAGE = int(os.environ.get("MK_STAGE", "5"))


class T:
    __slots__ = ("name", "w", "rd", "excl")

    def __init__(self, name="", excl=False):
        self.name = name
        self.w = None
        self.rd = []
        self.excl = excl


def _prune(rd):
    best = {}
    for tok in rd:
        if tok[0] in ("dma", "sw"):
            k = (tok[0], tok[1])
            if k not in best or best[k][2] < tok[2]:
                best[k] = tok
        else:
            k = tok[0].name
            if k not in best or best[k][1] < tok[1]:
                best[k] = tok
    return list(best.values())


class Eng:
    def __init__(self, ctx, name, e, sem):
        self.ctx, self.name, self.e, self.sem = ctx, name, e, sem
        self.count = 0
        self.seen = {}
        self.seen_dma = {}
        self.seen_sw = {}
        self.is_pe = name == "pe"

    def need(self, tok, war=False):
        if tok is None:
            return
        if tok[0] == "sw":
            _, s, g = tok
            if self.seen_sw.get(s, 0) >= g:
                return
            self.e.wait_ge(self.ctx.sw_sems[s], 16 * g)
            self.seen_sw[s] = g
            return
        if tok[0] == "dma":
            _, s, v = tok
            if self.seen_dma.get(s, 0) >= v:
                return
            self.e.wait_ge(self.ctx.dma_sems[s], v)
            self.seen_dma[s] = v
            return
        src, o = tok
        if src is self:
            if self.is_pe:
                return
            if self.seen.get(self.name, 0) >= o:
                return
            self.e.wait_ge(self.sem, o)
            self.seen[self.name] = o
            return
        if self.seen.get(src.name, 0) >= o:
            return
        self.e.wait_ge(src.sem, o)
        self.seen[src.name] = o

    def deps(self, reads, writes):
        for t in reads:
            self.need(t.w)
            if t.excl:
                for r in t.rd:
                    if r[0] is not self:
                        self.need(r)
        for t in writes:
            self.need(t.w)
            for r in t.rd:
                self.need(r, war=True)

    def commit(self, tok, reads, writes):
        for t in reads:
            t.rd.append(tok)
            if len(t.rd) > 16:
                t.rd = _prune(t.rd)
        for t in writes:
            t.w = tok
            t.rd = []

    def op(self, fn, reads=(), writes=()):
        self.deps(reads, writes)
        ins = fn()
        self.count += 1
        ins.then_inc(self.sem, 1)
        tok = (self, self.count)
        self.commit(tok, reads, writes)
        return tok

    def group(self, fns, reads=(), writes=()):
        self.deps(reads, writes)
        ins = None
        for fn in fns:
            ins = fn()
        self.count += 1
        ins.then_inc(self.sem, 1)
        tok = (self, self.count)
        self.commit(tok, reads, writes)
        return tok


class Ctx:
    def __init__(self, nc, n_dma_sems=40):
        self.nc = nc
        self.pe = Eng(self, "pe", nc.tensor, nc.alloc_semaphore("s_pe"))
        self.dve = Eng(self, "dve", nc.vector, nc.alloc_semaphore("s_dve"))
        self.act = Eng(self, "act", nc.scalar, nc.alloc_semaphore("s_act"))
        self.pool = Eng(self, "pool", nc.gpsimd, nc.alloc_semaphore("s_pool"))
        self.sp = Eng(self, "sp", nc.sync, nc.alloc_semaphore("s_sp"))
        self.dma_sems = [nc.alloc_semaphore(f"s_dma{i}") for i in range(n_dma_sems)]
        self.dma_val = [0] * n_dma_sems
        self.dma_next = 0
        self.sw_sems = [nc.alloc_semaphore(f"s_sw{i}") for i in range(10)]
        self.sw_gen = [0] * 10

    def swdma(self, i, out, in_, reads=(), writes=(), indirect=None, element_offset=0, **kw):
        q = self.pool
        for t in reads:
            q.need(t.w)
        for t in writes:
            if not (t.w is not None and t.w[0] == "sw" and t.w[1] == i):
                q.need(t.w)
            for r in t.rd:
                q.need(r, war=True)
        if indirect is not None:
            ins = q.e.indirect_dma_start(out=out, out_offset=None, in_=in_,
                                         in_offset=bass.IndirectOffsetOnAxis(ap=indirect, axis=0), element_offset=element_offset)
        else:
            ins = q.e.dma_start(out=out, in_=in_, **kw)
        self.sw_gen[i] += 1
        ins.then_inc(self.sw_sems[i], 16)
        tok = ("sw", i, self.sw_gen[i])
        q.commit(tok, reads, writes)
        return tok

    def dma(self, q, out, in_, reads=(), writes=(), **kw):
        q.deps(reads, writes)
        s = self.dma_next
        self.dma_next = (self.dma_next + 1) % len(self.dma_sems)
        if self.dma_val[s] > 0:
            q.need(("dma", s, self.dma_val[s]))
        ins = q.e.dma_start(out=out, in_=in_, **kw)
        self.dma_val[s] += 16
        ins.then_inc(self.dma_sems[s], 16)
        tok = ("dma", s, self.dma_val[s])
        q.commit(tok, reads, writes)
        return tok

    def barrier(self):
        engs = (self.pe, self.dve, self.act, self.pool, self.sp)
        for q in engs:
            for s, g in enumerate(self.sw_gen):
                if g > 0:
                    q.need(("sw", s, g))
            for s, v in enumerate(self.dma_val):
                if v > 0:
                    q.need(("dma", s, v))
            for e in engs:
                if e is not q and e.count > 0:
                    q.need((e, e.count))

    def finish(self):
        q = self.sp
        for s, v in enumerate(self.dma_val):
            if v > 0:
                q.need(("dma", s, v))
        for e in (self.pe, self.dve, self.act, self.pool):
            if e.count > 0:
                q.need((e, e.count))


def build_program():
    nc = bass.Bass("TRN2", target_bir_lowering=False)
    c = Ctx(nc)
    pe, dve, act, pool, sp = c.pe, c.dve, c.act, c.pool, c.sp
    V, S, PEn = nc.vector, nc.scalar, nc.tensor

    def din(name, shape, dt=F32):
        return nc.dram_tensor(name, list(shape), dt, kind="ExternalInput")

    def dout(name, shape, dt=F32):
        return nc.dram_tensor(name, list(shape), dt, kind="ExternalOutput")

    x_own = din("x_own", [NOWN, D]); x_c = din("x_c", [NC_, D]); x_past = din("x_past", [NPAST, D])
    if STAGE >= 5:
        cache_k = din("cache_k", [1280, 128, 8, 128]); cache_v = din("cache_v", [1280, 128, 8, 128])
    ptab = din("ptab", [1, 128], I32)
    st_conv = din("st_conv", [2, 1024]); st_pool = din("st_pool", [15, D])
    g_all = din("g_all", [128, 5, 16])
    w_in = din("w_in", [D, 6144]); conv_w = din("conv_w", [128, 8, 3]); w_o = din("w_o", [D, D])
    w_pool = din("w_pool", [4, 512, 512]); pscale = din("pscale", [128, 16])
    w_gate = din("w_gate", [2, D, DFF]); w_up = din("w_up", [2, D, DFF]); w_down = din("w_down", [2, DFF, D])
    cs_own = din("cs_own", [128, 2, NTP]); cs_past = din("cs_past", [128, 2, NPAST])
    c_ident = din("c_ident", [128, 128]); c_prot = din("c_prot", [128, 128], BF16)
    c_eall = din("c_eall", [16, 16, 128], BF16); c_cm = din("c_cm", [128, 4, 512], BF16)
    c_hb = din("c_hb", [128, 24, NH], BF16); c_addc = din("c_addc", [128, 8, 16]); c_addh = din("c_addh", [NH, 16])
    c_hv = din("c_hv", [128, NH]); c_invc = din("c_invc", [128, 4, 16])
    c_smask = din("c_smask", [4, 4]); c_esel = din("c_esel", [4, 4, 128])
    c_pio = din("c_pio", [128, 2]); c_iota64 = din("c_iota64", [128, 64])
    y_own = dout("y_own", [NOWN, D]); y_smp = dout("y_smp", [NS, D])
    k_own = dout("k_own", [NOWN, 8, 128]); v_own = dout("v_own", [NOWN, 8, 128])
    k_smp = dout("k_smp", [NS, 8, 128]); v_smp = dout("v_smp", [NS, 8, 128])
    conv_p = dout("conv_p", [2, 1024]); conv_s = dout("conv_s", [2, 1024])
    pool_p = dout("pool_p", [15, D]); pool_s = dout("pool_s", [15, D])
    kt_scr = nc.dram_tensor("kt_scr", [8, 128, 4096], BF16)
    v_scr = nc.dram_tensor("v_scr", [8, 32, 128, 128], BF16)

    sbuf_used = [0]

    def sb(name, shape, dt):
        return nc.alloc_sbuf_tensor(name, list(shape), dt)

    R1 = sb("R1", [128, NCH * NTP], F32)
    R2 = sb("R2", [128, NCH * NTP // 2], F32)
    R3 = sb("R3", [128, 11264], F32)
    xT = R1[:, :].rearrange("p (a b) -> p a b", a=NCH); t_xT = [T(f"xT{i}") for i in range(NCH)]
    hT = R2[:, :].bitcast(BF16).rearrange("p (a b) -> p a b", a=NCH); t_hT = T("hT")
    xT_scr = nc.dram_tensor("xT_scr", [128, NCH * NTP], F32)
    ident = sb("ident", [128, 128], F32); identb = sb("identb", [128, 128], BF16)
    onesb = sb("onesb", [128, 128], BF16); prot = sb("prot", [128, 128], BF16)
    eall = sb("eall", [16, 16, 128], BF16); cm = sb("cm", [128, 4, 512], BF16); hb = sb("hb", [128, 24, NH], BF16)
    addc = sb("addc", [128, 8, 16], F32); addh = sb("addh", [NH, 16], F32)
    hv = sb("hv", [128, NH], F32); invc = sb("invc", [128, 4, 16], F32)
    gall = sb("gall", [128, 5, 16], F32); convw = sb("convw", [128, 8, 3], F32); psc = sb("psc", [128, 16], F32)
    csown = sb("csown", [128, 2, NTP], F32)
    t_const = T("const")
    for dst, src in [(ident, c_ident), (prot, c_prot), (eall, c_eall), (cm, c_cm), (hb, c_hb), (addc, c_addc),
                     (addh, c_addh), (hv, c_hv), (invc, c_invc), (gall, g_all), (convw, conv_w), (psc, pscale),
                     (csown, cs_own)]:
        c.dma(sp, dst.ap(), src.ap(), writes=[t_const])
    dve.op(lambda: V.tensor_copy(identb[:, :], ident[:, :]), reads=[t_const], writes=[t_const])
    dve.op(lambda: V.memset(onesb[:, :], 1.0), writes=[t_const])
    c.barrier()

    PS = [nc.alloc_psum_tensor(f"ps{i}", [128, 512], F32) for i in range(8)]
    t_PS = [T(f"ps{i}", excl=True) for i in range(8)]

    class Arena:
        def __init__(self, base, nbytes):
            self.base = base
            self.nbytes = nbytes
            self.off = 0

        def reset(self):
            self.off = 0

        def take(self, shape, dt):
            esz = 4 if dt in (F32, I32, U32) else 2
            n = int(np.prod(shape[1:]))
            nbytes = (n * esz + 31) // 32 * 32
            assert self.off + nbytes <= self.nbytes, ("arena overflow", self.off, nbytes, self.nbytes)
            a = self.base[:, self.off // 4:(self.off + nbytes) // 4]
            self.off += nbytes
            if esz == 2:
                a = a.bitcast(BF16)[:, 0:n]
            elif dt != F32:
                a = a.bitcast(dt)[:, 0:n]
            else:
                a = a[:, 0:n]
            if len(shape) == 3:
                a = a.rearrange("p (a b) -> p a b", a=shape[1])
            if len(shape) == 4:
                a = a.rearrange("p (a b c) -> p a b c", a=shape[1], b=shape[2])
            if shape[0] < 128:
                a = a[0:shape[0]]
            return a

    A1 = Arena(R1, NCH * NTP * 4)
    A2 = Arena(R2, NCH * NTP * 2)
    A3 = Arena(R3, 11264 * 4)

    def take(ar, name, shape, dt):
        return ar.take(shape, dt)

    WB = [sb(f"wb{i}", [128, 16, 128], BF16) for i in range(4)]
    t_WB = [T(f"wb{i}") for i in range(4)]
    wb_next = [0]

    def wload(src_ap, nchunk):
        i = wb_next[0]
        wb_next[0] = (i + 1) % 4
        c.swdma(i, WB[i][:, 0:nchunk, :], src_ap.rearrange("(c p) n -> p c n", p=128), writes=[t_WB[i]])
        return WB[i], t_WB[i]

    evac_flip = [0]

    def evac(out_ap, in_ap, reads, writes):
        evac_flip[0] ^= 1
        if evac_flip[0]:
            return act.op(lambda: S.copy(out_ap, in_ap), reads=reads, writes=writes)
        return dve.op(lambda: V.tensor_copy(out_ap, in_ap), reads=reads, writes=writes)

    def load_transpose(src_rows_ap, nrows, dstT, t_dst, col0, stage, t_stage, psb):
        c.dma(sp, stage[0:nrows, :], src_rows_ap, writes=[t_stage])
        for q4 in range(4):
            b = psb[q4 % len(psb)]
            pe.group([(lambda k=k: PEn.transpose(PS[b][:, k * 128:k * 128 + nrows],
                                                  stage[0:nrows, (q4 * 4 + k) * 128:(q4 * 4 + k + 1) * 128],
                                                  ident[0:nrows, 0:nrows])) for k in range(4)],
                     reads=[t_stage, t_const], writes=[t_PS[b]])
            wr = t_dst[q4 * 4:q4 * 4 + 4] if isinstance(t_dst, list) else [t_dst]
            evac(dstT[:, q4 * 4:q4 * 4 + 4, col0:col0 + nrows],
                 PS[b][:, :].rearrange("p (k n) -> p k n", k=4)[:, :, 0:nrows], [t_PS[b]], wr)

    def rmsnorm_fm(srcT, t_src, cols, gidx, out_fn, tmp_sq, t_sq, rstd, t_rstd, psb):
        c0, n = cols
        rd = t_src if isinstance(t_src, list) else [t_src]
        fns = []
        for ch in range(NCH):
            k = ch % 2
            act.op(lambda ch=ch, k=k: S.activation(tmp_sq[k][:, 0:n], srcT[:, ch, c0:c0 + n], AF.Square),
                   reads=[rd[ch] if len(rd) > 1 else rd[0]], writes=[t_sq[k]])
            pe.deps([t_sq[k], t_const], [t_PS[psb]] if ch == 0 else [])
            ins = PEn.matmul(PS[psb][:, 0:n], onesb[:, :], tmp_sq[k][:, 0:n], start=(ch == 0), stop=(ch == NCH - 1))
            pe.count += 1
            ins.then_inc(pe.sem, 1)
            tok = (pe, pe.count)
            pe.commit(tok, [t_sq[k]], [t_PS[psb]] if ch == NCH - 1 else [])
            if ch != NCH - 1:
                t_PS[psb].w = tok
        act.op(lambda: S.activation(rstd[:, 0:n], PS[psb][:, 0:n], AF.Sqrt, bias=eps_ap[:, 0:1], scale=1.0 / D),
               reads=[t_PS[psb], t_const], writes=[t_rstd])
        dve.op(lambda: V.reciprocal(rstd[:, 0:n], rstd[:, 0:n]), reads=[t_rstd], writes=[t_rstd])
        for ch in range(NCH):
            out_fn(ch, rstd[:, 0:n])

    eps_ap = sb("eps", [128, 1], F32)
    dve.op(lambda: V.memset(eps_ap[:, :], 1e-6), writes=[t_const])

    def proj(wt, t_w, nchunk, rhs_fn, t_rhs, banks, tiles=TOKT):
        for (c0, n), b in zip(tiles, banks):
            pe.group([(lambda ch=ch, c0=c0, n=n, b=b: PEn.matmul(PS[b][:, 0:n], wt[:, ch, :], rhs_fn(ch, c0, n),
                                                              start=(ch == 0), stop=(ch == nchunk - 1)))
                      for ch in range(nchunk)], reads=[t_w] + list(t_rhs), writes=[t_PS[b]])

    def rope(psb, n, cos_ap, sin_ap, out_ap, t_out, tmp, t_tmp, rotb, t_cs=None):
        t_cs = t_cs or t_const
        act.op(lambda: S.copy(tmp["qb"][:, 0:n], PS[psb][:, 0:n]), reads=[t_PS[psb]], writes=[t_tmp["qb"]])
        pe.group([lambda: PEn.matmul(PS[rotb][:, 0:n], prot[:, :], tmp["qb"][:, 0:n], start=True, stop=True)],
                 reads=[t_tmp["qb"], t_const], writes=[t_PS[rotb]])
        dve.op(lambda: V.tensor_tensor(tmp["t1"][:, 0:n], PS[psb][:, 0:n], cos_ap, ALU.mult),
               reads=[t_PS[psb], t_cs], writes=[t_tmp["t1"]])
        dve.op(lambda: V.tensor_tensor(tmp["t2"][:, 0:n], PS[rotb][:, 0:n], sin_ap, ALU.mult),
               reads=[t_PS[rotb], t_cs], writes=[t_tmp["t2"]])
        dve.op(lambda: V.tensor_tensor(out_ap, tmp["t1"][:, 0:n], tmp["t2"][:, 0:n], ALU.add),
               reads=[t_tmp["t1"], t_tmp["t2"]], writes=[t_out])

    ksum = sb("ksum", [128, 8, 16], F32); t_ksum = T("ksum")
    sq2 = [sb(f"sq{i}", [128, 512], BF16) for i in range(2)]; t_sq2 = [T("sq0"), T("sq1")]
    rstd = sb("rstd", [128, 512], F32); t_rstd = T("rstd")
    rt = {"qb": sb("r_qb", [128, 512], BF16), "t1": sb("r_t1", [128, 512], F32), "t2": sb("r_t2", [128, 512], F32)}
    t_rt = {k: T(k) for k in rt}

    if STAGE >= 2:
        A1.reset(); A2.reset(); A3.reset()
        hpT = A1.take([128, NCH, 1536], BF16); t_hpT = T("hpT")
        xpT = A2.take([128, NCH, 512], F32); t_xpT = T("xpT")
        stage2 = [A3.take([128, D], F32) for i in range(2)]; t_stage2 = [T("st0"), T("st1")]
        cspast = A3.take([128, 3, 2, 512], F32); t_csp = T("csp")
        cntb = [0]
        kf = A3.take([128, 512], F32); t_kf = T("kf")
        kst = [A3.take([128, 512], BF16) for i in range(2)]; t_kst = [T("kst0"), T("kst1")]
        vst = [A3.take([128, 4, 128], BF16) for i in range(2)]; t_vst = [T("vst0"), T("vst1")]
        vtb = A3.take([128, 512], BF16); t_vtb = T("vtb")
        cnt = 0
        for grp in range(2):
            for tl in range(3):
                tok0 = grp * 1536 + tl * 512
                for sub in range(4):
                    k = cnt % 2; cnt += 1
                    load_transpose(x_past[tok0 + sub * 128: tok0 + (sub + 1) * 128, :], 128, xpT, t_xpT, sub * 128,
                                   stage2[k], t_stage2[k], [4, 5, 6, 7])

                def out_fn(ch, rs, tl=tl):
                    dve.op(lambda: V.scalar_tensor_tensor(hpT[:, ch, tl * 512:(tl + 1) * 512], xpT[:, ch, :],
                                                          gall[:, 0, ch:ch + 1], rs, ALU.mult, ALU.mult),
                           reads=[t_xpT, t_rstd, t_const], writes=[t_hpT])
                rmsnorm_fm(xpT, t_xpT, (0, 512), 0, out_fn, sq2, t_sq2, rstd, t_rstd, 3)
            for tl in range(3):
                tok0 = grp * 1536 + tl * 512
                c.dma(sp, cspast[:, tl, :, :], cs_past[:, :, tok0:tok0 + 512], writes=[t_csp])
            ptiles = [(tl * 512, 512) for tl in range(3)]
            pend = None
            psi = 0
            for which in (1, 2):
                for h in range(8):
                    wt, t_w = wload(w_in[:, which * 1024 + h * 128: which * 1024 + (h + 1) * 128], NCH)
                    banks = [[0, 1, 2], [4, 5, 6]][psi]; tb = [3, 7][psi]; psi ^= 1
                    proj(wt, t_w, NCH, lambda ch, c0, n: hpT[:, ch, c0:c0 + n], [t_hpT], banks, tiles=ptiles)
                    if pend is not None:
                        pend()

                    def post(which=which, h=h, banks=banks, tb=tb, grp=grp):
                        for tl in range(3):
                            tok0 = grp * 1536 + tl * 512
                            b = banks[tl]
                            kk = cntb[0] % 2; cntb[0] += 1
                            if which == 1:
                                rope(b, 512, cspast[:, tl, 0, :], cspast[:, tl, 1, :], kf[:, :], t_kf, rt, t_rt, tb, t_cs=t_csp)
                                act.op(lambda kk=kk: S.copy(kst[kk][:, :], kf[:, :]), reads=[t_kf], writes=[t_kst[kk]])
                                c.dma(sp, kt_scr[h, :, tok0:tok0 + 512], kst[kk][:, :], reads=[t_kst[kk]])
                                sb0 = tok0 // 256
                                dve.op(lambda sb0=sb0, h=h: V.tensor_reduce(ksum[:, h, sb0:sb0 + 2],
                                                                            kf[:, :].rearrange("p (a b) -> p a b", a=2), AX.X, ALU.add),
                                       reads=[t_kf], writes=[t_ksum])
                            else:
                                act.op(lambda b=b: S.copy(vtb[:, :], PS[b][:, :]), reads=[t_PS[b]], writes=[t_vtb])
                                pb16 = PS[tb][:, :].bitcast(BF16)
                                pe.group([(lambda s4=s4, pb16=pb16: PEn.transpose(pb16[:, s4 * 128:(s4 + 1) * 128],
                                                                                   vtb[:, s4 * 128:(s4 + 1) * 128], identb[:, :]))
                                          for s4 in range(4)], reads=[t_vtb, t_const], writes=[t_PS[tb]])
                                dve.op(lambda kk=kk, pb16=pb16: V.tensor_copy(vst[kk][:, :, :].rearrange("p a b -> p (a b)"), pb16[:, 0:512]),
                                       reads=[t_PS[tb]], writes=[t_vst[kk]])
                                c.dma(sp, v_scr[h, tok0 // 128: tok0 // 128 + 4, :, :].rearrange("s p d -> p s d"), vst[kk][:, :, :],
                                      reads=[t_vst[kk]])
                    pend = post
            pend()
        c.barrier()

    A3.reset()
    stage2 = [A3.take([128, D], F32) for i in range(2)]; t_stage2 = [T("st0"), T("st1")]
    dve.op(lambda: V.memset(xT[:, :, NT:NTP], 0.0), writes=t_xT)
    dve.op(lambda: V.memset(hT[:, :, NT:NTP], 0.0), writes=[t_hT])
    for sub in range(8):
        load_transpose(x_own[sub * 128:(sub + 1) * 128, :], 128, xT, t_xT, sub * 128, stage2[sub % 2], t_stage2[sub % 2],
                       [4, 5, 6, 7])
    load_transpose(x_c[:, :], NC_, xT, t_xT, NOWN, stage2[0], t_stage2[0], [4, 5, 6, 7])

    def norm_to_hT(gidx):
        for (c0, n) in TOKT:
            def out_fn(ch, rs, c0=c0, n=n):
                dve.op(lambda: V.scalar_tensor_tensor(hT[:, ch, c0:c0 + n], xT[:, ch, c0:c0 + n],
                                                      gall[:, gidx, ch:ch + 1], rs, ALU.mult, ALU.mult),
                       reads=[t_xT[ch], t_rstd, t_const], writes=[t_hT])
            rmsnorm_fm(xT, t_xT, (c0, n), gidx, out_fn, sq2, t_sq2, rstd, t_rstd, 3)

    norm_to_hT(0)
    c.dma(sp, xT_scr.ap(), R1[:, :], reads=t_xT)
    c.barrier()
    A1.reset(); A3.reset()
    QT = A1.take([128, 8, NTP], BF16); t_QT = T("QT")
    YC = A3.take([128, 8, NTP], BF16); t_YC = T("YC")
    kf = A3.take([128, 512], F32); t_kf = T("kf")
    kst = [A3.take([128, 512], BF16) for i in range(2)]; t_kst = [T("kst0"), T("kst1")]
    vst = [A3.take([128, 4, 128], BF16) for i in range(2)]; t_vst = [T("vst0"), T("vst1")]

    hrhs = lambda ch, c0, n: hT[:, ch, c0:c0 + n]
    QsT = sb("QsT", [128, 8, NS], F32); KsT = sb("KsT", [128, 8, NS], F32); t_sm = T("smp")
    Vs = sb("Vs", [NS, 8, 128], F32); Ks = sb("Ks", [NS, 8, 128], F32); Qs = sb("Qs", [NS, 8, 128], F32)
    ost = [A3.take([128, 4, 128], F32) for i in range(2)]; t_ost = [T("ost0"), T("ost1")]
    vf = A3.take([128, 512], F32); t_vf = T("vf")
    ulast = sb("ulast", [128, 8, 2], F32); uslast = sb("uslast", [128, 8, 2], F32); t_ulast = T("ulast")
    stcT = sb("stcT", [128, 8, 2], F32)
    for t_ in range(2):
        c.dma(sp, stcT[:, :, t_], st_conv[t_, :].rearrange("(c p) -> p c", p=128), writes=[t_const],
              allow_slow_non_contiguous=True)
    gcs = A3.take([128, NTP], F32); t_gcs = T("gcs")
    uext = A3.take([128, NOWN + 2], F32); usx = sb("usx", [128, NS + 2], F32); t_u = T("u")
    cva = A3.take([128, NOWN], F32); cvs = sb("cvs", [128, NS], F32); t_cv = T("cv")
    uh = sb("uh", [128, NH], F32); cvh = sb("cvh", [128, NH], F32)
    cnt = 0
    bank_sets = [[0, 1, 2], [4, 5, 6]]
    bsi = 0
    cnto = [0]
    pend = None
    for h in range(8):
        for which in range(3):
            wt, t_w = wload(w_in[:, which * 1024 + h * 128: which * 1024 + (h + 1) * 128], NCH)
            banks = bank_sets[bsi]; bsi ^= 1
            tb = 3 if banks[0] == 0 else 7
            proj(wt, t_w, NCH, hrhs, [t_hT], banks)
            if pend is not None:
                pend()

            def post(h=h, which=which, banks=banks, tb=tb):
                for (c0, n), b in zip(TOKT, banks):
                    cosap, sinap = csown[:, 0, c0:c0 + n], csown[:, 1, c0:c0 + n]
                    if which == 0:
                        rope(b, n, cosap, sinap, QT[:, h, c0:c0 + n], t_QT, rt, t_rt, tb)
                        if n == NC_:
                            dve.op(lambda h=h: V.tensor_tensor(QsT[:, h, :], rt["t1"][:, 0:NS], rt["t2"][:, 0:NS], ALU.add),
                                   reads=[t_rt["t1"], t_rt["t2"]], writes=[t_sm])
                            pe.group([lambda h=h: PEn.transpose(PS[tb][0:NS, 0:128], QsT[:, h, :], ident[:, :])],
                                     reads=[t_sm, t_const], writes=[t_PS[tb]])
                            dve.op(lambda h=h: V.tensor_copy(Qs[:, h, :], PS[tb][0:NS, 0:128]), reads=[t_PS[tb]], writes=[t_sm])
                    elif which == 1:
                        rope(b, n, cosap, sinap, kf[:, 0:n], t_kf, rt, t_rt, tb)
                        if n == 512:
                            kk = cnto[0] % 2; cnto[0] += 1
                            act.op(lambda kk=kk: S.copy(kst[kk][:, :], kf[:, :]), reads=[t_kf], writes=[t_kst[kk]])
                            c.dma(sp, kt_scr[h, :, NPAST + c0:NPAST + c0 + 512], kst[kk][:, :], reads=[t_kst[kk]])
                            sb0 = 12 + c0 // 256
                            dve.op(lambda sb0=sb0, h=h: V.tensor_reduce(ksum[:, h, sb0:sb0 + 2],
                                                                        kf[:, :].rearrange("p (a b) -> p a b", a=2), AX.X, ALU.add),
                                   reads=[t_kf], writes=[t_ksum])
                            pe.group([(lambda s4=s4: PEn.transpose(PS[tb][:, s4 * 128:(s4 + 1) * 128], kf[:, s4 * 128:(s4 + 1) * 128],
                                                                    ident[:, :])) for s4 in range(4)],
                                     reads=[t_kf, t_const], writes=[t_PS[tb]])
                            kk = cnto[0] % 2; cnto[0] += 1
                            evac(ost[kk][:, :, :].rearrange("p a b -> p (a b)"), PS[tb][:, :], [t_PS[tb]], [t_ost[kk]])
                            c.dma(sp, k_own[c0:c0 + 512, h, :].rearrange("(s p) d -> p s d", p=128), ost[kk][:, :, :], reads=[t_ost[kk]])
                        else:
                            dve.op(lambda h=h: V.tensor_copy(KsT[:, h, :], kf[:, 0:NS]), reads=[t_kf], writes=[t_sm])
                            pe.group([lambda h=h: PEn.transpose(PS[tb][0:NS, 0:128], KsT[:, h, :], ident[:, :])],
                                     reads=[t_sm, t_const], writes=[t_PS[tb]])
                            dve.op(lambda h=h: V.tensor_copy(Ks[:, h, :], PS[tb][0:NS, 0:128]), reads=[t_PS[tb]], writes=[t_sm])
                    else:
                        evac(vf[:, 0:n], PS[b][:, 0:n], [t_PS[b]], [t_vf])
                        if n == 512:
                            pe.group([(lambda s4=s4: PEn.transpose(PS[tb][:, s4 * 128:(s4 + 1) * 128], vf[:, s4 * 128:(s4 + 1) * 128],
                                                                    ident[:, :])) for s4 in range(4)],
                                     reads=[t_vf, t_const], writes=[t_PS[tb]])
                            kk = cnto[0] % 2; cnto[0] += 1
                            evac(ost[kk][:, :, :].rearrange("p a b -> p (a b)"), PS[tb][:, :], [t_PS[tb]], [t_ost[kk]])
                            c.dma(sp, v_own[c0:c0 + 512, h, :].rearrange("(s p) d -> p s d", p=128), ost[kk][:, :, :], reads=[t_ost[kk]])
                            dve.op(lambda kk=kk: V.tensor_copy(vst[kk][:, :, :].rearrange("p a b -> p (a b)"), PS[tb][:, :]),
                                   reads=[t_PS[tb]], writes=[t_vst[kk]])
                            st = (NPAST + c0) // 128
                            c.dma(sp, v_scr[h, st:st + 4, :, :].rearrange("s p d -> p s d"), vst[kk][:, :, :], reads=[t_vst[kk]])
                        else:
                            pe.group([lambda: PEn.transpose(PS[tb][0:NS, 0:128], vf[:, 0:NS], ident[:, :])],
                                     reads=[t_vf, t_const], writes=[t_PS[tb]])
                            dve.op(lambda h=h: V.tensor_copy(Vs[:, h, :], PS[tb][0:NS, 0:128]), reads=[t_PS[tb]], writes=[t_sm])

            pend = post
    pend()
    c.dma(sp, k_smp[:, :, :], Ks[:, :, :], reads=[t_sm])
    c.dma(sp, v_smp[:, :, :], Vs[:, :, :], reads=[t_sm])

    for cc in range(8):
        wts = [wload(w_in[:, 3072 + which * 1024 + cc * 128: 3072 + which * 1024 + (cc + 1) * 128], NCH) for which in (1, 2, 0)]
        banks = bank_sets[bsi]; bsi ^= 1
        proj(wts[0][0], wts[0][1], NCH, hrhs, [t_hT], banks)
        for (c0, n), b in zip(TOKT, banks):
            evac(gcs[:, c0:c0 + n], PS[b][:, 0:n], [t_PS[b]], [t_gcs])
        banks = bank_sets[bsi]; bsi ^= 1
        proj(wts[1][0], wts[1][1], NCH, hrhs, [t_hT], banks)
        for (c0, n), b in zip(TOKT, banks):
            if n == 512:
                dve.op(lambda c0=c0, b=b: V.tensor_tensor(uext[:, 2 + c0:2 + c0 + 512], PS[b][:, 0:512], gcs[:, c0:c0 + 512], ALU.mult),
                       reads=[t_PS[b], t_gcs], writes=[t_u])
            else:
                dve.op(lambda b=b: V.tensor_tensor(usx[:, 2:2 + NS], PS[b][:, 0:NS], gcs[:, NOWN:NOWN + NS], ALU.mult),
                       reads=[t_PS[b], t_gcs], writes=[t_u])
                dve.op(lambda b=b: V.tensor_tensor(uext[:, 0:2], PS[b][:, NC_ - 2:NC_], gcs[:, NT - 2:NT], ALU.mult),
                       reads=[t_PS[b], t_gcs], writes=[t_u])
                dve.op(lambda b=b: V.tensor_tensor(uh[:, 0:NH], PS[b][:, NS:NC_], gcs[:, NOWN + NS:NT], ALU.mult),
                       reads=[t_PS[b], t_gcs], writes=[t_u])
                dve.op(lambda cc=cc: V.tensor_copy(usx[:, 0:2], stcT[:, cc, :]), reads=[t_const], writes=[t_u])
        dve.op(lambda cc=cc: V.tensor_copy(ulast[:, cc, :], uext[:, NOWN:NOWN + 2]), reads=[t_u], writes=[t_ulast])
        dve.op(lambda cc=cc: V.tensor_copy(uslast[:, cc, :], usx[:, NS:NS + 2]), reads=[t_u], writes=[t_ulast])
        for (ux, cv, n) in ((uext, cva, NOWN), (usx, cvs, NS), (uh, cvh, NH - 2)):
            dve.op(lambda ux=ux, cv=cv, n=n, cc=cc: V.tensor_scalar(cv[:, 0:n], ux[:, 0:n], convw[:, cc, 0:1], None, ALU.mult),
                   reads=[t_u, t_const], writes=[t_cv])
            for j in (1, 2):
                dve.op(lambda ux=ux, cv=cv, n=n, cc=cc, j=j: V.scalar_tensor_tensor(cv[:, 0:n], ux[:, j:j + n], convw[:, cc, j:j + 1],
                                                                                 cv[:, 0:n], ALU.mult, ALU.add),
                       reads=[t_u, t_cv, t_const], writes=[t_cv])
        banks = bank_sets[bsi]; bsi ^= 1
        proj(wts[2][0], wts[2][1], NCH, hrhs, [t_hT], banks)
        for (c0, n), b in zip(TOKT, banks):
            if n == 512:
                dve.op(lambda c0=c0, b=b, cc=cc: V.tensor_tensor(YC[:, cc, c0:c0 + 512], PS[b][:, 0:512], cva[:, c0:c0 + 512], ALU.mult),
                       reads=[t_PS[b], t_cv], writes=[t_YC])
            else:
                dve.op(lambda b=b, cc=cc: V.tensor_tensor(YC[:, cc, NOWN:NOWN + NS], PS[b][:, 0:NS], cvs[:, 0:NS], ALU.mult),
                       reads=[t_PS[b], t_cv], writes=[t_YC])
                dve.op(lambda b=b, cc=cc: V.tensor_tensor(YC[:, cc, NOWN + NS + 2:NT], PS[b][:, NS + 2:NC_], cvh[:, 0:NH - 2], ALU.mult),
                       reads=[t_PS[b], t_cv], writes=[t_YC])
                dve.op(lambda cc=cc: V.memset(YC[:, cc, NOWN + NS:NOWN + NS + 2], 0.0), writes=[t_YC])
    for t_ in range(2):
        c.dma(sp, conv_p[t_, :].rearrange("(c p) -> p c", p=128), ulast[:, :, t_], reads=[t_ulast], allow_slow_non_contiguous=True)
        c.dma(sp, conv_s[t_, :].rearrange("(c p) -> p c", p=128), uslast[:, :, t_], reads=[t_ulast], allow_slow_non_contiguous=True)
    c.barrier()

    AT = hT
    t_AT = T("AT")
    if STAGE >= 3:
        kmT = sb("kmT", [128, 8, 16], BF16); t_km = T("km")
        dve.op(lambda: V.tensor_scalar(kmT[:, :, :], ksum[:, :, :], 1.0 / 256.0, None, ALU.mult), reads=[t_ksum], writes=[t_km])
        BT = A1.take([16, 8, NTP], BF16); t_BT = T("BT")
        g2 = sb("g2", [128, 8, 16], F32); top8 = sb("top8", [128, 8, 8], F32); thr = sb("thr", [128, 8], F32)
        selb = sb("selb", [128, 8, 16], BF16); t_g = T("g")
        qtiles = [(i * 128, 128, addc[:, i, :]) for i in range(8)] + [(NOWN + NS, NH, addh[:, :])]
        for (q0, qn, adc) in qtiles:
            gb = 3
            for h in range(8):
                pe.group([lambda h=h: PEn.matmul(PS[gb][0:qn, h * 16:(h + 1) * 16], QT[:, h, q0:q0 + qn], kmT[:, h, :], start=True, stop=True)],
                         reads=[t_QT, t_km], writes=[t_PS[gb]])
            dve.op(lambda: V.tensor_tensor(g2[0:qn, :, :], PS[gb][0:qn, 0:128].rearrange("p (a b) -> p a b", a=8),
                                           adc.unsqueeze(1).broadcast_to([qn, 8, 16]), ALU.add),
                   reads=[t_PS[gb], t_const], writes=[t_g])
            for h in range(8):
                dve.op(lambda h=h: V.max(top8[0:qn, h, :], g2[0:qn, h, :]), reads=[t_g], writes=[t_g])
            dve.op(lambda: V.tensor_scalar(thr[0:qn, :], top8[0:qn, :, 3], -1e29, None, ALU.max), reads=[t_g], writes=[t_g])
            dve.op(lambda: V.tensor_tensor(g2[0:qn, :, :], g2[0:qn, :, :], thr[0:qn, :].unsqueeze(2).broadcast_to([qn, 8, 16]), ALU.is_ge),
                   reads=[t_g], writes=[t_g])
            dve.op(lambda: V.tensor_scalar(selb[0:qn, :, :], g2[0:qn, :, :], -NEGB, NEGB, ALU.mult, ALU.add), reads=[t_g], writes=[t_g])
            pb16 = PS[7][:, :].bitcast(BF16)
            pe.group([(lambda h=h: PEn.transpose(pb16[0:16, h * 128:h * 128 + qn], selb[0:qn, h, :], identb[0:qn, 0:qn])) for h in range(8)],
                     reads=[t_g, t_const], writes=[t_PS[7]])
            dve.op(lambda: V.tensor_copy(BT[:, :, q0:q0 + qn], pb16[0:16, :].rearrange("p (a b) -> p a b", a=8)[:, :, 0:qn]),
                   reads=[t_PS[7]], writes=[t_BT])

        KT2 = [A1.take([128, 4096], BF16) for _ in range(2)]; t_KT2 = [T("KTa"), T("KTb")]
        VV2 = [A1.take([128, 32, 128], BF16) for _ in range(2)]; t_VV2 = [T("Va"), T("Vb")]
        A3.off = (8 * NTP * 2 + 31) // 32 * 32
        PT = [A3.take([128, 512], BF16) for i in range(3)]; t_PT = [T(f"PT{i}") for i in range(3)]
        rden = A3.take([128, 512], F32); t_rden = T("rden")
        pti = 0
        sctr = [0]
        for h in range(8):
            kb = h % 2
            c.dma(sp, KT2[kb][:, :], kt_scr[h, :, :], writes=[t_KT2[kb]])
            c.dma(sp, VV2[kb][:, :, :], v_scr[h, :, :, :].rearrange("s p d -> p s d"), writes=[t_VV2[kb]])
            for qi, (q0, qn, nkt) in enumerate([(0, 512, 28), (512, 512, 32), (NOWN + NS, NH, 24)]):
                ob, db = 4, 5

                def emit_S(kt, h=h, kb=kb, qi=qi, q0=q0, qn=qn):
                    sbk = sctr[0] % 3; sctr[0] += 1
                    fns = [lambda: PEn.matmul(PS[sbk][:, 0:qn], KT2[kb][:, kt * 128:(kt + 1) * 128], QT[:, h, q0:q0 + qn],
                                              start=True, stop=False)]
                    extra = None
                    if qi == 2:
                        extra = hb[:, kt, :]
                    elif kt >= 24:
                        off = (kt - 24) * 128 - q0
                        if off >= 0:
                            extra = cm[:, off // 128, :]
                    fns.append(lambda: PEn.matmul(PS[sbk][:, 0:qn], eall[:, kt // 2, :], BT[:, h, q0:q0 + qn],
                                                  start=False, stop=(extra is None)))
                    if extra is not None:
                        fns.append(lambda: PEn.matmul(PS[sbk][:, 0:qn], identb[:, :], extra, start=False, stop=True))
                    pe.group(fns, reads=[t_KT2[kb], t_QT, t_BT, t_const], writes=[t_PS[sbk]])
                    return sbk

                LOOK = 2
                sb_of = {}
                for kt in range(min(LOOK, nkt)):
                    sb_of[kt] = emit_S(kt)
                for kt in range(nkt):
                    if kt + LOOK < nkt:
                        sb_of[kt + LOOK] = emit_S(kt + LOOK)
                    sbk = sb_of.pop(kt)
                    p = pti; pti = (pti + 1) % 3
                    act.op(lambda sbk=sbk, p=p: S.activation(PT[p][:, 0:qn], PS[sbk][:, 0:qn], AF.Exp, scale=SCALE),
                           reads=[t_PS[sbk]], writes=[t_PT[p]])
                    pe.deps([t_PT[p], t_VV2[kb], t_const], [t_PS[ob], t_PS[db]] if kt == 0 else [])
                    PEn.matmul(PS[ob][:, 0:qn], VV2[kb][:, kt, :], PT[p][:, 0:qn], start=(kt == 0), stop=(kt == nkt - 1))
                    ins = PEn.matmul(PS[db][:, 0:qn], onesb[:, :], PT[p][:, 0:qn], start=(kt == 0), stop=(kt == nkt - 1))
                    pe.count += 1
                    ins.then_inc(pe.sem, 1)
                    tok = (pe, pe.count)
                    pe.commit(tok, [t_PT[p], t_VV2[kb]], [])
                    t_PS[ob].w = tok; t_PS[db].w = tok
                    if kt == 0:
                        t_PS[ob].rd = []; t_PS[db].rd = []
                dve.op(lambda: V.reciprocal(rden[:, 0:qn], PS[db][:, 0:qn]), reads=[t_PS[db]], writes=[t_rden])
                dve.op(lambda h=h: V.tensor_tensor(AT[:, h, q0:q0 + qn], PS[ob][:, 0:qn], rden[:, 0:qn], ALU.mult),
                       reads=[t_PS[ob], t_rden], writes=[t_AT])
    else:
        dve.op(lambda: V.memset(AT[:, 0:8, :], 0.0), writes=[t_AT])
    if STAGE >= 5:
        c.barrier()
        A1.reset()
        kpg = [A1.take([128, 2, 1024], BF16) for _ in range(2)]; t_kpg = [T("kpg0"), T("kpg1")]
        Kg = [A1.take([128, 24, 128], F32) for _ in range(2)]; t_Kg = [T("Kg0"), T("Kg1")]
        Vg = [A1.take([128, 24, 128], F32) for _ in range(2)]; t_Vg = [T("Vg0"), T("Vg1")]
        A3.off = (8 * NTP * 2 + 31) // 32 * 32
        kmS = A3.take([128, 8, 64], F32); t_kmS = T("kmS")
        G = A3.take([128, 32, 64], F32); top8s = A3.take([128, 32, 8], F32); idxs = A3.take([128, 32, 8], U32); t_G = T("G")
        nf = A3.take([128, 32, 3], F32)
        OH = A3.take([128, 12, 64], F32); PR = A3.take([128, 12, 64], F32); physf = A3.take([128, 12, 2], F32); t_OH = T("OH")
        idxG = A3.take([128, 8, 24], I32); t_idxG = T("idxG")
        Qrep = [A3.take([128, 128], F32) for _ in range(2)]; t_Qrep = [T("qr0"), T("qr1")]
        qb = A3.take([128, 4, 128], F32); t_qb = T("qb")
        junk = A3.take([128, 128], F32); t_junk = T("junk")
        Ssm = A3.take([128, 24], F32); Psm = A3.take([128, 24], F32); t_S = T("Ssm"); t_P = T("Psm")
        Sown = A3.take([4, 4], F32); Pown = A3.take([4, 4], F32)
        den = A3.take([128, 4], F32); rd = A3.take([128, 4], F32); t_den = T("den")
        ptb = sb("ptb", [128, 128], I32); ptf = sb("ptf", [128, 128], F32); idxP = sb("idxP", [128, 128], I32); t_pt = T("pt")
        pio = sb("pio", [128, 2], F32); iota64 = sb("iota64", [128, 64], F32)
        onesf = sb("onesf", [128, 128], F32)
        esel = sb("esel", [4, 4, 128], F32); smask = sb("smask", [4, 4], F32)
        dve.op(lambda: V.memset(onesf[:, :], 1.0), writes=[t_const])
        c.dma(sp, esel[:, :, :], c_esel.ap(), writes=[t_const])
        c.dma(sp, smask[:, :], c_smask.ap(), writes=[t_const])
        c.dma(sp, pio[:, :], c_pio.ap(), writes=[t_const])
        c.dma(sp, iota64[:, :], c_iota64.ap(), writes=[t_const])
        c.dma(sp, ptb[:, :], ptab.ap().partition_broadcast(128), writes=[t_pt])
        dve.op(lambda: V.tensor_copy(ptf[:, :], ptb[:, :]), reads=[t_pt], writes=[t_pt])
        dve.op(lambda: V.tensor_scalar(idxP[:, :], ptf[:, :], 128.0, pio[:, 0:1], ALU.mult, ALU.add), reads=[t_pt, t_const], writes=[t_pt])
        ck_rows = cache_k.ap().rearrange("n t h d -> (n t) (h d)")
        ck_hrows = cache_k.ap().rearrange("n t h d -> (n t h) d")
        cv_hrows = cache_v.ap().rearrange("n t h d -> (n t h) d")
        for n in range(64):
            kb = n % 2
            for pg in range(2):
                c.swdma(4 + kb, kpg[kb][:, pg, :], ck_rows, reads=[t_pt], writes=[t_kpg[kb]],
                        indirect=idxP[:, 2 * n + pg:2 * n + pg + 1])
            for h in range(8):
                col = h * 64 + n
                pe.group([lambda kb=kb, h=h, col=col: PEn.matmul(PS[0][:, col:col + 1], kpg[kb][:, 0, h * 128:(h + 1) * 128], onesb[:, 0:1],
                                                                 start=True, stop=False),
                          lambda kb=kb, h=h, col=col: PEn.matmul(PS[0][:, col:col + 1], kpg[kb][:, 1, h * 128:(h + 1) * 128], onesb[:, 0:1],
                                                                 start=False, stop=True)],
                         reads=[t_kpg[kb], t_const], writes=[t_PS[0]])
        dve.op(lambda: V.tensor_scalar(kmS[:, :, :].rearrange("p a b -> p (a b)"), PS[0][:, :], 1.0 / 256.0, None, ALU.mult),
               reads=[t_PS[0]], writes=[t_kmS])
        for hq in range(32):
            h, q = hq // 4, hq % 4
            bank = 4 + hq // 8; col = (hq % 8) * 64
            k2 = hq % 2
            dve.op(lambda h=h, q=q, k2=k2: V.tensor_copy(Qrep[k2][:, :], QsT[:, h, q:q + 1].broadcast_to([128, 128])),
                   reads=[t_sm], writes=[t_Qrep[k2]])
            pe.group([lambda h=h, k2=k2, bank=bank, col=col: PEn.matmul(PS[bank][:, col:col + 64], Qrep[k2][:, :], kmS[:, h, :],
                                                                        start=True, stop=True)],
                     reads=[t_Qrep[k2], t_kmS], writes=[t_PS[bank]])
        for b4 in range(4):
            dve.op(lambda b4=b4: V.tensor_copy(G[:, b4 * 8:(b4 + 1) * 8, :].rearrange("p a b -> p (a b)"), PS[4 + b4][:, :]),
                   reads=[t_PS[4 + b4]], writes=[t_G])
        for hq in range(32):
            dve.op(lambda hq=hq: V.max(top8s[:, hq, :], G[:, hq, :]), reads=[t_G], writes=[t_G])
            dve.op(lambda hq=hq: V.max_index(idxs[:, hq, :], top8s[:, hq, :], G[:, hq, :]), reads=[t_G], writes=[t_G])
        dve.op(lambda: V.tensor_copy(nf[:, :, :], idxs[:, :, 0:3]), reads=[t_G], writes=[t_G])
        for h in range(8):
            dve.op(lambda h=h: V.tensor_tensor(OH[:, :, :], iota64[:, :].unsqueeze(1).broadcast_to([128, 12, 64]),
                                               nf[:, h * 4:(h + 1) * 4, :].rearrange("p a b -> p (a b)").unsqueeze(2).broadcast_to([128, 12, 64]),
                                               ALU.is_equal), reads=[t_G, t_const], writes=[t_OH])
            for pgi in range(2):
                dve.op(lambda pgi=pgi: V.tensor_tensor(PR[:, :, :], OH[:, :, :],
                                                       ptf[:, :].rearrange("p (n g) -> p n g", g=2)[:, :, pgi].unsqueeze(1).broadcast_to([128, 12, 64]),
                                                       ALU.mult), reads=[t_OH, t_pt], writes=[t_OH])
                dve.op(lambda pgi=pgi: V.tensor_reduce(physf[:, :, pgi], PR[:, :, :], AX.X, ALU.add), reads=[t_OH], writes=[t_OH])
            dve.op(lambda h=h: V.tensor_scalar(idxG[:, h, :], physf[:, :, :].rearrange("p a b -> p (a b)"), 1024.0, pio[:, 1:2], ALU.mult, ALU.add),
                   reads=[t_OH, t_const], writes=[t_idxG])
        for h in range(8):
            gb = h % 2
            for s_ in range(24):
                c.swdma(6 + gb, Kg[gb][:, s_, :], ck_hrows, reads=[t_idxG], writes=[t_Kg[gb]], indirect=idxG[:, h, s_:s_ + 1],
                        element_offset=h * 128)
            for s_ in range(24):
                c.swdma(8 + gb, Vg[gb][:, s_, :], cv_hrows, reads=[t_idxG], writes=[t_Vg[gb]], indirect=idxG[:, h, s_:s_ + 1],
                        element_offset=h * 128)
            pe.group([(lambda q=q, h=h: PEn.matmul(PS[1][:, q * 128:(q + 1) * 128], esel[0:4, q, :], Qs[0:4, h, :], start=True, stop=True))
                      for q in range(4)], reads=[t_sm, t_const], writes=[t_PS[1]])
            act.op(lambda: S.copy(qb[:, :, :].rearrange("p a b -> p (a b)"), PS[1][:, :]), reads=[t_PS[1]], writes=[t_qb])
            for s_ in range(24):
                dve.op(lambda s_=s_, gb=gb: V.scalar_tensor_tensor(junk[:, :], Kg[gb][:, s_, :], SCALE, qb[:, s_ // 6, :], ALU.mult, ALU.mult,
                                                                   accum_out=Ssm[:, s_:s_ + 1]),
                       reads=[t_Kg[gb], t_qb], writes=[t_junk, t_S])
            for q in range(4):
                dve.op(lambda q=q, h=h: V.scalar_tensor_tensor(junk[0:4, :], Ks[0:4, h, :], SCALE, qb[0:4, q, :], ALU.mult, ALU.mult,
                                                               accum_out=Sown[0:4, q:q + 1]),
                       reads=[t_sm, t_qb], writes=[t_junk, t_S])
            dve.op(lambda: V.tensor_tensor(Sown[0:4, :], Sown[0:4, :], smask[0:4, :], ALU.add), reads=[t_S, t_const], writes=[t_S])
            act.op(lambda: S.activation(Psm[:, :], Ssm[:, :], AF.Exp), reads=[t_S], writes=[t_P])
            act.op(lambda: S.activation(Pown[0:4, :], Sown[0:4, :], AF.Exp), reads=[t_S], writes=[t_P])
            pe.group([lambda: PEn.matmul(PS[2][:, 0:24], onesf[:, :], Psm[:, :], start=True, stop=True)],
                     reads=[t_P, t_const], writes=[t_PS[2]])
            dve.op(lambda: V.tensor_reduce(den[:, 0:4], PS[2][:, 0:24].rearrange("p (a b) -> p a b", a=4), AX.X, ALU.add),
                   reads=[t_PS[2]], writes=[t_den])
            pe.group([lambda: PEn.matmul(PS[2][:, 32:36], onesf[0:4, :], Pown[0:4, :], start=True, stop=True)],
                     reads=[t_P, t_const], writes=[t_PS[2]])
            dve.op(lambda: V.tensor_tensor(den[:, 0:4], den[:, 0:4], PS[2][:, 32:36], ALU.add), reads=[t_PS[2], t_den], writes=[t_den])
            dve.op(lambda: V.reciprocal(rd[:, 0:4], den[:, 0:4]), reads=[t_den], writes=[t_den])
            for q in range(4):
                fns = [(lambda q=q, i=i, gb=gb: PEn.matmul(PS[3][:, q:q + 1], Vg[gb][:, q * 6 + i, :], Psm[:, q * 6 + i:q * 6 + i + 1],
                                                           start=(i == 0), stop=False)) for i in range(6)]
                fns.append(lambda q=q, h=h: PEn.matmul(PS[3][:, q:q + 1], Vs[0:4, h, :], Pown[0:4, q:q + 1], start=False, stop=True))
                pe.group(fns, reads=[t_Vg[gb], t_P, t_sm], writes=[t_PS[3]])
            dve.op(lambda h=h: V.tensor_tensor(AT[:, h, NOWN:NOWN + NS], PS[3][:, 0:4], rd[:, 0:4], ALU.mult),
                   reads=[t_PS[3], t_den], writes=[t_AT])
    else:
        dve.op(lambda: V.memset(AT[:, 0:8, NOWN:NOWN + NS], 0.0), writes=[t_AT])
    c.barrier()

    c.dma(sp, R1[:, :], xT_scr.ap(), writes=t_xT)

    def resid_add(banks, oc, scale_ap=None):
        for (c0, n), b in zip(TOKT, banks):
            if scale_ap is None:
                dve.op(lambda c0=c0, n=n, b=b: V.tensor_tensor(xT[:, oc, c0:c0 + n], PS[b][:, 0:n], xT[:, oc, c0:c0 + n], ALU.add),
                       reads=[t_PS[b], t_xT[oc]], writes=[t_xT[oc]])
            else:
                dve.op(lambda c0=c0, n=n, b=b: V.scalar_tensor_tensor(xT[:, oc, c0:c0 + n], PS[b][:, 0:n], scale_ap, xT[:, oc, c0:c0 + n],
                                                                      ALU.mult, ALU.add),
                       reads=[t_PS[b], t_xT[oc], t_const], writes=[t_xT[oc]])

    for oc in range(NCH):
        wt, t_w = wload(w_o[:, oc * 128:(oc + 1) * 128], NCH)
        banks = bank_sets[bsi]; bsi ^= 1
        proj(wt, t_w, NCH, lambda ch, c0, n: (AT[:, ch, c0:c0 + n] if ch < 8 else YC[:, ch - 8, c0:c0 + n]), [t_AT, t_YC], banks)
        resid_add(banks, oc)
    c.barrier()

    def ffn(layer, gidx):
        norm_to_hT(gidx)
        A3.reset()
        NG = 4
        per = NFT // NG
        aT = A3.take([128, per, NTP], BF16); t_aT = T("aT")
        sg = [A3.take([128, 512], F32) for i in range(2)]; t_sg = [T("sg0"), T("sg1")]
        sgi = 0
        for g in range(NG):
            for fi in range(per):
                ft = g * per + fi
                wg, t_wg = wload(w_gate[layer, :, ft * 128:(ft + 1) * 128], NCH)
                wu, t_wu = wload(w_up[layer, :, ft * 128:(ft + 1) * 128], NCH)
                proj(wg, t_wg, NCH, hrhs, [t_hT], [0, 1, 2])
                proj(wu, t_wu, NCH, hrhs, [t_hT], [4, 5, 6])
                for (c0, n), bg, bu in zip(TOKT, [0, 1, 2], [4, 5, 6]):
                    k = sgi; sgi ^= 1
                    act.op(lambda k=k, bg=bg, n=n: S.activation(sg[k][:, 0:n], PS[bg][:, 0:n], AF.Silu), reads=[t_PS[bg]], writes=[t_sg[k]])
                    dve.op(lambda k=k, bu=bu, n=n, c0=c0, fi=fi: V.tensor_tensor(aT[:, fi, c0:c0 + n], PS[bu][:, 0:n], sg[k][:, 0:n], ALU.mult),
                           reads=[t_PS[bu], t_sg[k]], writes=[t_aT])
            for oc in range(NCH):
                wd, t_wd = wload(w_down[layer, g * per * 128:(g + 1) * per * 128, oc * 128:(oc + 1) * 128], per)
                banks = bank_sets[bsi_box[0]]; bsi_box[0] ^= 1
                proj(wd, t_wd, per, lambda ch, c0, n: aT[:, ch, c0:c0 + n], [t_aT], banks)
                resid_add(banks, oc)
        c.barrier()

    bsi_box = [bsi]
    if STAGE >= 4:
        ffn(0, 1)

    if STAGE >= 4:
        A3.reset()
        AR = A3
        dT = hT; t_dT = t_hT
        stpT = AR.take([128, NCH, NPH], F32)
        h1l = AR.take([128, NCH, NPH], F32); h1s = AR.take([128, NCH, NPH], F32); t_h1l = T("h1l")
        stp_sb = AR.take([NPH, D], F32); t_stp = T("stp")
        c.dma(sp, stp_sb[:, :], st_pool[:, :], writes=[t_stp])
        for q4 in range(4):
            pe.group([(lambda k=k: PEn.transpose(PS[7][:, k * 128:k * 128 + NPH], stp_sb[0:NPH, (q4 * 4 + k) * 128:(q4 * 4 + k + 1) * 128],
                                                  ident[0:NPH, 0:NPH])) for k in range(4)], reads=[t_stp, t_const], writes=[t_PS[7]])
            dve.op(lambda q4=q4: V.tensor_copy(stpT[:, q4 * 4:q4 * 4 + 4, :], PS[7][:, :].rearrange("p (k n) -> p k n", k=4)[:, :, 0:NPH]),
                   reads=[t_PS[7]], writes=[t_stp])
        rs_all = AR.take([128, NTP], F32); t_rsall = T("rsall")
        for (c0, n) in TOKT:
            rmsnorm_fm(xT, t_xT, (c0, n), 2, lambda ch, rs: None, sq2, t_sq2, rstd, t_rstd, 3)
            dve.op(lambda c0=c0, n=n: V.tensor_copy(rs_all[:, c0:c0 + n], rstd[:, 0:n]), reads=[t_rstd], writes=[t_rsall])
        hx = AR.take([128, NPH + NOWN], F32); hxs = AR.take([128, NPH + NS], F32); t_hx = T("hx")
        sA = AR.take([128, NPH + NOWN], F32); sB = AR.take([128, NPH + NOWN], F32); t_s = T("s")
        for ch in range(NCH):
            g = ch // 4
            gsc = gall[:, 2, ch:ch + 1]
            dve.op(lambda ch=ch: V.scalar_tensor_tensor(hx[:, NPH:NPH + NOWN], xT[:, ch, 0:NOWN], gsc, rs_all[:, 0:NOWN], ALU.mult, ALU.mult),
                   reads=[t_xT[ch], t_rsall, t_const], writes=[t_hx])
            dve.op(lambda ch=ch: V.scalar_tensor_tensor(hx[:, 0:NPH], xT[:, ch, NT - NPH:NT], gsc, rs_all[:, NT - NPH:NT], ALU.mult, ALU.mult),
                   reads=[t_xT[ch], t_rsall, t_const], writes=[t_hx])
            dve.op(lambda: V.tensor_tensor(hx[:, 0:NPH], hx[:, 0:NPH], hv[:, 0:NPH], ALU.mult), reads=[t_hx, t_const], writes=[t_hx])
            dve.op(lambda ch=ch: V.scalar_tensor_tensor(hxs[:, NPH:NPH + NS], xT[:, ch, NOWN:NOWN + NS], gsc, rs_all[:, NOWN:NOWN + NS], ALU.mult, ALU.mult),
                   reads=[t_xT[ch], t_rsall, t_const], writes=[t_hx])
            dve.op(lambda ch=ch: V.tensor_copy(hxs[:, 0:NPH], stpT[:, ch, :]), reads=[t_stp], writes=[t_hx])
            dve.op(lambda ch=ch: V.tensor_copy(h1l[:, ch, :], hx[:, NOWN:NOWN + NPH]), reads=[t_hx], writes=[t_h1l])
            dve.op(lambda ch=ch: V.tensor_copy(h1s[:, ch, :], hxs[:, NS:NS + NPH]), reads=[t_hx], writes=[t_h1l])
            w = 2 ** (g + 1)
            for (src, L, c0out, nout) in ((hx, NPH + NOWN, 0, NOWN), (hxs, NPH + NS, NOWN, NS)):
                cur = src
                sh = 1
                bufs = [sA, sB]
                bi = 0
                while sh < w:
                    nxt = bufs[bi]; bi ^= 1
                    dve.op(lambda cur=cur, nxt=nxt, sh=sh, L=L: V.tensor_tensor(nxt[:, sh:L], cur[:, sh:L], cur[:, 0:L - sh], ALU.add),
                           reads=[t_hx, t_s], writes=[t_s])
                    if sh > 1 or True:
                        dve.op(lambda cur=cur, nxt=nxt, sh=sh: V.tensor_copy(nxt[:, 0:sh], cur[:, 0:sh]), reads=[t_hx, t_s], writes=[t_s])
                    cur = nxt
                    sh *= 2
                dve.op(lambda cur=cur, src=src, c0out=c0out, nout=nout, ch=ch, w=w: V.scalar_tensor_tensor(
                    dT[:, ch, c0out:c0out + nout], cur[:, NPH:NPH + nout], 1.0 / w, src[:, NPH:NPH + nout], ALU.mult, ALU.subtract),
                    reads=[t_s, t_hx], writes=[t_dT])
                if nout == NOWN:
                    fx = sA if cur is sB else sB
                    dve.op(lambda cur=cur, fx=fx, g=g: V.tensor_tensor(fx[:, 0:NPH], cur[:, NPH:2 * NPH], invc[:, g, 0:NPH], ALU.mult),
                           reads=[t_s, t_const], writes=[t_s])
                    dve.op(lambda fx=fx, src=src, ch=ch: V.tensor_tensor(dT[:, ch, 0:NPH], fx[:, 0:NPH], src[:, NPH:2 * NPH], ALU.subtract),
                           reads=[t_s, t_hx], writes=[t_dT])
            dve.op(lambda ch=ch: V.memset(dT[:, ch, NOWN + NS:NT], 0.0), writes=[t_dT])
        for (src, dst) in ((h1l, pool_p), (h1s, pool_s)):
            po = stp_sb; t_po = t_stp
            for q4 in range(4):
                pe.group([(lambda k=k: PEn.transpose(PS[7][0:NPH, k * 128:(k + 1) * 128], src[:, q4 * 4 + k, :], ident[:, :])) for k in range(4)],
                         reads=[t_h1l, t_const], writes=[t_PS[7]])
                dve.op(lambda q4=q4, po=po: V.tensor_copy(po[:, q4 * 512:(q4 + 1) * 512], PS[7][0:NPH, :]), reads=[t_PS[7]], writes=[t_po])
            c.dma(sp, dst[:, :], po[:, :], reads=[t_po])
        for g in range(4):
            for oc4 in range(4):
                oc = g * 4 + oc4
                wt, t_w = wload(w_pool[g, :, oc4 * 128:(oc4 + 1) * 128], 4)
                banks = bank_sets[bsi_box[0]]; bsi_box[0] ^= 1
                proj(wt, t_w, 4, lambda ch, c0, n, g=g: dT[:, g * 4 + ch, c0:c0 + n], [t_dT], banks)
                resid_add(banks, oc, psc[:, oc:oc + 1])
        c.barrier()
        ffn(1, 3)

    A3.reset()
    stage2 = [A3.take([128, D], F32) for i in range(2)]; t_stage2 = [T("st0"), T("st1")]
    yT = A3.take([128, NCH, 128], F32); t_yT = T("yT")
    for (c0, n) in TOKT:
        rmsnorm_fm(xT, t_xT, (c0, n), 4, lambda ch, rs: None, sq2, t_sq2, rstd, t_rstd, 3)
        nsub = 4 if n == 512 else 1
        for sub in range(nsub):
            nr = 128 if n == 512 else NS
            k2 = sub % 2
            for ch in range(NCH):
                dve.op(lambda ch=ch, sub=sub, nr=nr, c0=c0: V.scalar_tensor_tensor(
                    yT[:, ch, 0:nr], xT[:, ch, c0 + sub * 128:c0 + sub * 128 + nr], gall[:, 4, ch:ch + 1],
                    rstd[:, sub * 128:sub * 128 + nr], ALU.mult, ALU.mult),
                    reads=[t_xT[ch], t_rstd, t_const], writes=[t_yT])
            for q4 in range(4):
                b = 4 + q4
                pe.group([(lambda k=k: PEn.transpose(PS[b][0:nr, k * 128:(k + 1) * 128], yT[:, q4 * 4 + k, 0:nr], ident[:, :]))
                          for k in range(4)], reads=[t_yT, t_const], writes=[t_PS[b]])
                evac(stage2[k2][0:nr, q4 * 512:(q4 + 1) * 512], PS[b][0:nr, :], [t_PS[b]], [t_stage2[k2]])
            if n == 512:
                c.dma(sp, y_own[c0 + sub * 128:c0 + (sub + 1) * 128, :], stage2[k2][:, :], reads=[t_stage2[k2]])
            else:
                c.dma(sp, y_smp[:, :], stage2[k2][0:NS, :], reads=[t_stage2[k2]])
    c.finish()
    return nc


_CACHE = {}


def _consts(j):
    bf = ml_dtypes.bfloat16
    half = 64
    inv = (np.float32(10000.0) ** (-np.arange(half, dtype=np.float32) / np.float32(half))).astype(np.float32)

    def cs(pos):
        ang = pos.astype(np.float32)[:, None] * inv[None, :]
        co, si = np.cos(ang).astype(np.float32), np.sin(ang).astype(np.float32)
        return np.stack([np.concatenate([co, co], 1).T, np.concatenate([si, si], 1).T], axis=1)
    T0 = j * 1024
    pos = np.zeros(NTP, np.float32)
    pos[0:NOWN] = T0 + np.arange(NOWN)
    pos[NOWN:NOWN + NS] = 16384 + np.arange(NS)
    pos[NOWN + NS:NT] = np.maximum(T0 - NH + np.arange(NH), 0)
    d = {}
    d["cs_own"] = np.ascontiguousarray(cs(pos))
    d["cs_past"] = np.ascontiguousarray(cs(np.arange(NPAST, dtype=np.float32)))
    d["c_ident"] = np.eye(128, dtype=np.float32)
    pr = np.zeros((128, 128), np.float32)
    for dd in range(64):
        pr[dd + 64, dd] = -1.0
        pr[dd, dd + 64] = 1.0
    d["c_prot"] = pr.astype(bf)
    ea = np.zeros((16, 16, 128), np.float32)
    for r in range(16):
        ea[r, r, :] = 1.0
    d["c_eall"] = ea.astype(bf)
    k = np.arange(128)[:, None]; q = np.arange(512)[None, :]
    cmm = np.stack([np.where(off * 128 + k <= q, 0.0, NEGB) for off in range(4)], axis=1)
    d["c_cm"] = cmm.astype(bf)
    hbm = np.zeros((128, 24, NH), np.float32)
    for kt in range(24):
        for qq in range(NH):
            p = T0 - NH + qq
            s = kt * 128 + np.arange(128)
            vis = (s <= p) if j > 0 else np.full(128, kt == 0)
            hbm[:, kt, qq] = np.where(vis, 0.0, NEGB)
    d["c_hb"] = hbm.astype(bf)
    adc = np.zeros((128, 8, 16), np.float32)
    for qt in range(8):
        ob = (qt * 128) // 256
        for sbk in range(16):
            if sbk < 12:
                v = 0.0 if sbk < 4 * j else -2e30
            else:
                o = sbk - 12
                v = 0.0 if o < ob else (1e30 if o == ob else -2e30)
            adc[:, qt, sbk] = v
    d["c_addc"] = adc
    adh = np.full((NH, 16), -2e30, np.float32)
    if j > 0:
        adh[:, :4 * j - 1] = 0.0
        adh[:, 4 * j - 1] = 1e30
    else:
        adh[:, 0] = 1e30
    d["c_addh"] = adh
    d["c_hv"] = np.full((128, NH), 1.0 if j > 0 else 0.0, np.float32)
    ic = np.zeros((128, 4, 16), np.float32)
    for g in range(4):
        w = 2 ** (g + 1)
        for i in range(16):
            ic[:, g, i] = 1.0 / min(T0 + i + 1, w)
    d["c_invc"] = ic
    sm = np.zeros((4, 4), np.float32)
    for kk in range(4):
        for qq in range(4):
            sm[kk, qq] = 0.0 if kk <= qq else NEGB
    d["c_smask"] = sm
    es = np.zeros((4, 4, 128), np.float32)
    for r in range(4):
        es[r, r, :] = 1.0
    d["c_esel"] = es
    d["c_pio"] = np.stack([np.arange(128, dtype=np.float32), 8.0 * np.arange(128, dtype=np.float32)], axis=1)
    d["c_iota64"] = np.ascontiguousarray(np.broadcast_to(np.arange(64, dtype=np.float32)[None, :], (128, 64)))
    return d


def kernel(x_prompt, x_sample, cache_k, cache_v, page_table, state_conv, state_pool, norm_mix, norm_ffn, norm_final,
           w_in, conv_w, w_o, w_pool, pool_scale, w_gate, w_up, w_down):
    f = lambda a: np.ascontiguousarray(np.asarray(a, dtype=np.float32))
    x_prompt, x_sample = f(x_prompt), f(x_sample)
    if "nc" not in _CACHE:
        _CACHE["nc"] = build_program()
    nc = _CACHE["nc"]
    fm = lambda v: np.asarray(v, np.float32).reshape(16, 128).T
    g_all = np.ascontiguousarray(np.stack([fm(norm_mix[0]), fm(norm_ffn[0]), fm(norm_mix[1]), fm(norm_ffn[1]), fm(norm_final)], axis=1))
    shared = {
        "g_all": g_all, "w_in": f(w_in)[0], "conv_w": np.ascontiguousarray(f(conv_w)[0].reshape(3, 8, 128).transpose(2, 1, 0)),
        "w_o": f(w_o)[0], "w_pool": f(w_pool)[0], "pscale": np.ascontiguousarray(fm(pool_scale[0])),
        "w_gate": f(w_gate), "w_up": f(w_up), "w_down": f(w_down),
    }
    if STAGE >= 5:
        shared["cache_k"] = f(cache_k)[0]; shared["cache_v"] = f(cache_v)[0]
    in_maps = []
    for c in range(8):
        b, j = c // 4, c % 4
        T0 = j * 1024
        m = dict(shared)
        m["x_own"] = np.ascontiguousarray(x_prompt[b, T0:T0 + 1024])
        halo = np.zeros((NH, D), np.float32)
        if j > 0:
            halo[:] = x_prompt[b, T0 - NH:T0]
        m["x_c"] = np.ascontiguousarray(np.concatenate([x_sample[c], halo], axis=0))
        m["x_past"] = np.ascontiguousarray(x_prompt[b, 0:NPAST])
        m["ptab"] = np.ascontiguousarray(np.asarray(page_table, np.int32)[c:c + 1])
        m["st_conv"] = np.ascontiguousarray(f(state_conv)[0, c])
        m["st_pool"] = np.ascontiguousarray(f(state_pool)[0, c])
        m.update(_consts(j))
        in_maps.append(m)
    res = run_bass_kernel_spmd(nc, in_maps, core_ids=list(range(8)))
    R = res.results
    y_prompt = np.stack([np.concatenate([R[b * 4 + j]["y_own"] for j in range(4)], 0) for b in range(2)], 0)
    y_sample = np.stack([R[c]["y_smp"] for c in range(8)], 0)
    k_prompt = np.stack([np.concatenate([R[b * 4 + j]["k_own"] for j in range(4)], 0) for b in range(2)], 0)[None]
    v_prompt = np.stack([np.concatenate([R[b * 4 + j]["v_own"] for j in range(4)], 0) for b in range(2)], 0)[None]
    k_sample = np.stack([R[c]["k_smp"] for c in range(8)], 0)[None]
    v_sample = np.stack([R[c]["v_smp"] for c in range(8)], 0)[None]
    conv_prompt = np.stack([R[3]["conv_p"], R[7]["conv_p"]], 0)[None]
    conv_sample = np.stack([R[c]["conv_s"] for c in range(8)], 0)[None]
    pool_prompt = np.stack([R[3]["pool_p"], R[7]["pool_p"]], 0)[None]
    pool_sample = np.stack([R[c]["pool_s"] for c in range(8)], 0)[None]
    outs = (y_prompt, y_sample, k_prompt, v_prompt, k_sample, v_sample, conv_prompt, conv_sample, pool_prompt, pool_sample)
    return tuple(np.ascontiguousarray(o.astype(np.float32)) for o in outs)
```

```python
import os
import numpy as np
import ml_dtypes
import concourse.bass as bass
import concourse.mybir as mybir
from concourse.bass_utils import run_bass_kernel_spmd

F32 = mybir.dt.float32
BF16 = mybir.dt.bfloat16
I32 = mybir.dt.int32
U32 = mybir.dt.uint32
AF = mybir.ActivationFunctionType
ALU = mybir.AluOpType
AX = mybir.AxisListType

D = 2048
NCH = 16
DFF = 5632
NFT = 44
NPAST = 3072
NOWN = 1024
NS = 4
NH = 17
NPH = 15
NC_ = NS + NH
NT = NOWN + NC_
NTP = 1048
SCALE = 128 ** -0.5
NEGB = -30000.0
TOKT = [(0, 512), (512, 512), (1024, NC_)]
STAGE = int(os.environ.get("MK_STAGE", "5"))


class T:
    __slots__ = ("name", "w", "rd", "excl")

    def __init__(self, name="", excl=False):
        self.name = name
        self.w = None
        self.rd = []
        self.excl = excl


def _prune(rd):
    best = {}
    for tok in rd:
        if tok[0] in ("dma", "sw"):
            k = (tok[0], tok[1])
            if k not in best or best[k][2] < tok[2]:
                best[k] = tok
        else:
            k = tok[0].name
            if k not in best or best[k][1] < tok[1]:
                best[k] = tok
    return list(best.values())


class Eng:
    def __init__(self, ctx, name, e, sem):
        self.ctx, self.name, self.e, self.sem = ctx, name, e, sem
        self.count = 0
        self.seen = {}
        self.seen_dma = {}
        self.seen_sw = {}
        self.is_pe = name == "pe"

    def need(self, tok, war=False):
        if tok is None:
            return
        if tok[0] == "sw":
            _, s, g = tok
            if self.seen_sw.get(s, 0) >= g:
                return
            self.e.wait_ge(self.ctx.sw_sems[s], 16 * g)
            self.seen_sw[s] = g
            return
        if tok[0] == "dma":
            _, s, v = tok
            if self.seen_dma.get(s, 0) >= v:
                return
            self.e.wait_ge(self.ctx.dma_sems[s], v)
            self.seen_dma[s] = v
            return
        src, o = tok
        if src is self:
            if self.is_pe:
                return
            if self.seen.get(self.name, 0) >= o:
                return
            self.e.wait_ge(self.sem, o)
            self.seen[self.name] = o
            return
        if self.seen.get(src.name, 0) >= o:
            return
        self.e.wait_ge(src.sem, o)
        self.seen[src.name] = o

    def deps(self, reads, writes):
        for t in reads:
            self.need(t.w)
            if t.excl:
                for r in t.rd:
                    if r[0] is not self:
                        self.need(r)
        for t in writes:
            self.need(t.w)
            for r in t.rd:
                self.need(r, war=True)

    def commit(self, tok, reads, writes):
        for t in reads:
            t.rd.append(tok)
            if len(t.rd) > 16:
                t.rd = _prune(t.rd)
        for t in writes:
            t.w = tok
            t.rd = []

    def op(self, fn, reads=(), writes=()):
        self.deps(reads, writes)
        ins = fn()
        self.count += 1
        ins.then_inc(self.sem, 1)
        tok = (self, self.count)
        self.commit(tok, reads, writes)
        return tok

    def group(self, fns, reads=(), writes=()):
        self.deps(reads, writes)
        ins = None
        for fn in fns:
            ins = fn()
        self.count += 1
        ins.then_inc(self.sem, 1)
        tok = (self, self.count)
        self.commit(tok, reads, writes)
        return tok


class Ctx:
    def __init__(self, nc, n_dma_sems=40):
        self.nc = nc
        self.pe = Eng(self, "pe", nc.tensor, nc.alloc_semaphore("s_pe"))
        self.dve = Eng(self, "dve", nc.vector, nc.alloc_semaphore("s_dve"))
        self.act = Eng(self, "act", nc.scalar, nc.alloc_semaphore("s_act"))
        self.pool = Eng(self, "pool", nc.gpsimd, nc.alloc_semaphore("s_pool"))
        self.sp = Eng(self, "sp", nc.sync, nc.alloc_semaphore("s_sp"))
        self.dma_sems = [nc.alloc_semaphore(f"s_dma{i}") for i in range(n_dma_sems)]
        self.dma_val = [0] * n_dma_sems
        self.dma_next = 0
        self.sw_sems = [nc.alloc_semaphore(f"s_sw{i}") for i in range(10)]
        self.sw_gen = [0] * 10

    def swdma(self, i, out, in_, reads=(), writes=(), indirect=None, element_offset=0, **kw):
        q = self.pool
        for t in reads:
            q.need(t.w)
        for t in writes:
            if not (t.w is not None and t.w[0] == "sw" and t.w[1] == i):
                q.need(t.w)
            for r in t.rd:
                q.need(r, war=True)
        if indirect is not None:
            ins = q.e.indirect_dma_start(out=out, out_offset=None, in_=in_,
                                         in_offset=bass.IndirectOffsetOnAxis(ap=indirect, axis=0), element_offset=element_offset)
        else:
            ins = q.e.dma_start(out=out, in_=in_, **kw)
        self.sw_gen[i] += 1
        ins.then_inc(self.sw_sems[i], 16)
        tok = ("sw", i, self.sw_gen[i])
        q.commit(tok, reads, writes)
        return tok

    def dma(self, q, out, in_, reads=(), writes=(), **kw):
        q.deps(reads, writes)
        s = self.dma_next
        self.dma_next = (self.dma_next + 1) % len(self.dma_sems)
        if self.dma_val[s] > 0:
            q.need(("dma", s, self.dma_val[s]))
        ins = q.e.dma_start(out=out, in_=in_, **kw)
        self.dma_val[s] += 16
        ins.then_inc(self.dma_sems[s], 16)
        tok = ("dma", s, self.dma_val[s])
        q.commit(tok, reads, writes)
        return tok

    def barrier(self):
        engs = (self.pe, self.dve, self.act, self.pool, self.sp)
        for q in engs:
            for s, g in enumerate(self.sw_gen):
                if g > 0:
                    q.need(("sw", s, g))
            for s, v in enumerate(self.dma_val):
                if v > 0:
                    q.need(("dma", s, v))
            for e in engs:
                if e is not q and e.count > 0:
                    q.need((e, e.count))

    def finish(self):
        q = self.sp
        for s, v in enumerate(self.dma_val):
            if v > 0:
                q.need(("dma", s, v))
        for e in (self.pe, self.dve, self.act, self.pool):
            if e.count > 0:
                q.need((e, e.count))


def build_program():
    nc = bass.Bass("TRN2", target_bir_lowering=False)
    c = Ctx(nc)
    pe, dve, act, pool, sp = c.pe, c.dve, c.act, c.pool, c.sp
    V, S, PEn = nc.vector, nc.scalar, nc.tensor

    def din(name, shape, dt=F32):
        return nc.dram_tensor(name, list(shape), dt, kind="ExternalInput")

    def dout(name, shape, dt=F32):
        return nc.dram_tensor(name, list(shape), dt, kind="ExternalOutput")

    x_own = din("x_own", [NOWN, D]); x_c = din("x_c", [NC_, D]); x_past = din("x_past", [NPAST, D])
    if STAGE >= 5:
        cache_k = din("cache_k", [1280, 128, 8, 128]); cache_v = din("cache_v", [1280, 128, 8, 128])
    ptab = din("ptab", [1, 128], I32)
    st_conv = din("st_conv", [2, 1024]); st_pool = din("st_pool", [15, D])
    g_all = din("g_all", [128, 5, 16])
    w_in = din("w_in", [D, 6144]); conv_w = din("conv_w", [128, 8, 3]); w_o = din("w_o", [D, D])
    w_pool = din("w_pool", [4, 512, 512]); pscale = din("pscale", [128, 16])
    w_gate = din("w_gate", [2, D, DFF]); w_up = din("w_up", [2, D, DFF]); w_down = din("w_down", [2, DFF, D])
    cs_own = din("cs_own", [128, 2, NTP]); cs_past = din("cs_past", [128, 2, NPAST])
    c_ident = din("c_ident", [128, 128]); c_prot = din("c_prot", [128, 128], BF16)
    c_eall = din("c_eall", [16, 16, 128], BF16); c_cm = din("c_cm", [128, 4, 512], BF16)
    c_hb = din("c_hb", [128, 24, NH], BF16); c_addc = din("c_addc", [128, 8, 16]); c_addh = din("c_addh", [NH, 16])
    c_hv = din("c_hv", [128, NH]); c_invc = din("c_invc", [128, 4, 16])
    c_smask = din("c_smask", [4, 4]); c_esel = din("c_esel", [4, 4, 128])
    c_pio = din("c_pio", [128, 2]); c_iota64 = din("c_iota64", [128, 64])
    y_own = dout("y_own", [NOWN, D]); y_smp = dout("y_smp", [NS, D])
    k_own = dout("k_own", [NOWN, 8, 128]); v_own = dout("v_own", [NOWN, 8, 128])
    k_smp = dout("k_smp", [NS, 8, 128]); v_smp = dout("v_smp", [NS, 8, 128])
    conv_p = dout("conv_p", [2, 1024]); conv_s = dout("conv_s", [2, 1024])
    pool_p = dout("pool_p", [15, D]); pool_s = dout("pool_s", [15, D])
    kt_scr = nc.dram_tensor("kt_scr", [8, 128, 4096], BF16)
    v_scr = nc.dram_tensor("v_scr", [8, 32, 128, 128], BF16)

    sbuf_used = [0]

    def sb(name, shape, dt):
        return nc.alloc_sbuf_tensor(name, list(shape), dt)

    R1 = sb("R1", [128, NCH * NTP], F32)
    R2 = sb("R2", [128, NCH * NTP // 2], F32)
    R3 = sb("R3", [128, 11264], F32)
    xT = R1[:, :].rearrange("p (a b) -> p a b", a=NCH); t_xT = [T(f"xT{i}") for i in range(NCH)]
    hT = R2[:, :].bitcast(BF16).rearrange("p (a b) -> p a b", a=NCH); t_hT = T("hT")
    xT_scr = nc.dram_tensor("xT_scr", [128, NCH * NTP], F32)
    ident = sb("ident", [128, 128], F32); identb = sb("identb", [128, 128], BF16)
    onesb = sb("onesb", [128, 128], BF16); prot = sb("prot", [128, 128], BF16)
    eall = sb("eall", [16, 16, 128], BF16); cm = sb("cm", [128, 4, 512], BF16); hb = sb("hb", [128, 24, NH], BF16)
    addc = sb("addc", [128, 8, 16], F32); addh = sb("addh", [NH, 16], F32)
    hv = sb("hv", [128, NH], F32); invc = sb("invc", [128, 4, 16], F32)
    gall = sb("gall", [128, 5, 16], F32); convw = sb("convw", [128, 8, 3], F32); psc = sb("psc", [128, 16], F32)
    csown = sb("csown", [128, 2, NTP], F32)
    t_const = T("const")
    for dst, src in [(ident, c_ident), (prot, c_prot), (eall, c_eall), (cm, c_cm), (hb, c_hb), (addc, c_addc),
                     (addh, c_addh), (hv, c_hv), (invc, c_invc), (gall, g_all), (convw, conv_w), (psc, pscale),
                     (csown, cs_own)]:
        c.dma(sp, dst.ap(), src.ap(), writes=[t_const])
    dve.op(lambda: V.tensor_copy(identb[:, :], ident[:, :]), reads=[t_const], writes=[t_const])
    dve.op(lambda: V.memset(onesb[:, :], 1.0), writes=[t_const])
    c.barrier()

    PS = [nc.alloc_psum_tensor(f"ps{i}", [128, 512], F32) for i in range(8)]
    t_PS = [T(f"ps{i}", excl=True) for i in range(8)]

    class Arena:
        def __init__(self, base, nbytes):
            self.base = base
            self.nbytes = nbytes
            self.off = 0

        def reset(self):
            self.off = 0

        def take(self, shape, dt):
            esz = 4 if dt in (F32, I32, U32) else 2
            n = int(np.prod(shape[1:]))
            nbytes = (n * esz + 31) // 32 * 32
            assert self.off + nbytes <= self.nbytes, ("arena overflow", self.off, nbytes, self.nbytes)
            a = self.base[:, self.off // 4:(self.off + nbytes) // 4]
            self.off += nbytes
            if esz == 2:
                a = a.bitcast(BF16)[:, 0:n]
            elif dt != F32:
                a = a.bitcast(dt)[:, 0:n]
            else:
                a = a[:, 0:n]
            if len(shape) == 3:
                a = a.rearrange("p (a b) -> p a b", a=shape[1])
            if len(shape) == 4:
                a = a.rearrange("p (a b c) -> p a b c", a=shape[1], b=shape[2])
            if shape[0] < 128:
                a = a[0:shape[0]]
            return a

    A1 = Arena(R1, NCH * NTP * 4)
    A2 = Arena(R2, NCH * NTP * 2)
    A3 = Arena(R3, 11264 * 4)

    def take(ar, name, shape, dt):
        return ar.take(shape, dt)

    WB = [sb(f"wb{i}", [128, 16, 128], BF16) for i in range(4)]
    t_WB = [T(f"wb{i}") for i in range(4)]
    wb_next = [0]

    def wload(src_ap, nchunk):
        i = wb_next[0]
        wb_next[0] = (i + 1) % 4
        c.swdma(i, WB[i][:, 0:nchunk, :], src_ap.rearrange("(c p) n -> p c n", p=128), writes=[t_WB[i]])
        return WB[i], t_WB[i]

    evac_flip = [0]

    def evac(out_ap, in_ap, reads, writes):
        evac_flip[0] ^= 1
        if evac_flip[0]:
            return act.op(lambda: S.copy(out_ap, in_ap), reads=reads, writes=writes)
        return dve.op(lambda: V.tensor_copy(out_ap, in_ap), reads=reads, writes=writes)

    def load_transpose(src_rows_ap, nrows, dstT, t_dst, col0, stage, t_stage, psb):
        c.dma(sp, stage[0:nrows, :], src_rows_ap, writes=[t_stage])
        for q4 in range(4):
            b = psb[q4 % len(psb)]
            pe.group([(lambda k=k: PEn.transpose(PS[b][:, k * 128:k * 128 + nrows],
                                                  stage[0:nrows, (q4 * 4 + k) * 128:(q4 * 4 + k + 1) * 128],
                                                  ident[0:nrows, 0:nrows])) for k in range(4)],
                     reads=[t_stage, t_const], writes=[t_PS[b]])
            wr = t_dst[q4 * 4:q4 * 4 + 4] if isinstance(t_dst, list) else [t_dst]
            evac(dstT[:, q4 * 4:q4 * 4 + 4, col0:col0 + nrows],
                 PS[b][:, :].rearrange("p (k n) -> p k n", k=4)[:, :, 0:nrows], [t_PS[b]], wr)

    def rmsnorm_fm(srcT, t_src, cols, gidx, out_fn, tmp_sq, t_sq, rstd, t_rstd, psb):
        c0, n = cols
        rd = t_src if isinstance(t_src, list) else [t_src]
        fns = []
        for ch in range(NCH):
            k = ch % 2
            act.op(lambda ch=ch, k=k: S.activation(tmp_sq[k][:, 0:n], srcT[:, ch, c0:c0 + n], AF.Square),
                   reads=[rd[ch] if len(rd) > 1 else rd[0]], writes=[t_sq[k]])
            pe.deps([t_sq[k], t_const], [t_PS[psb]] if ch == 0 else [])
            ins = PEn.matmul(PS[psb][:, 0:n], onesb[:, :], tmp_sq[k][:, 0:n], start=(ch == 0), stop=(ch == NCH - 1))
            pe.count += 1
            ins.then_inc(pe.sem, 1)
            tok = (pe, pe.count)
            pe.commit(tok, [t_sq[k]], [t_PS[psb]] if ch == NCH - 1 else [])
            if ch != NCH - 1:
                t_PS[psb].w = tok
        act.op(lambda: S.activation(rstd[:, 0:n], PS[psb][:, 0:n], AF.Sqrt, bias=eps_ap[:, 0:1], scale=1.0 / D),
               reads=[t_PS[psb], t_const], writes=[t_rstd])
        dve.op(lambda: V.reciprocal(rstd[:, 0:n], rstd[:, 0:n]), reads=[t_rstd], writes=[t_rstd])
        for ch in range(NCH):
            out_fn(ch, rstd[:, 0:n])

    eps_ap = sb("eps", [128, 1], F32)
    dve.op(lambda: V.memset(eps_ap[:, :], 1e-6), writes=[t_const])

    def proj(wt, t_w, nchunk, rhs_fn, t_rhs, banks, tiles=TOKT):
        for (c0, n), b in zip(tiles, banks):
            pe.group([(lambda ch=ch, c0=c0, n=n, b=b: PEn.matmul(PS[b][:, 0:n], wt[:, ch, :], rhs_fn(ch, c0, n),
                                                              start=(ch == 0), stop=(ch == nchunk - 1)))
                      for ch in range(nchunk)], reads=[t_w] + list(t_rhs), writes=[t_PS[b]])

    def rope(psb, n, cos_ap, sin_ap, out_ap, t_out, tmp, t_tmp, rotb, t_cs=None):
        t_cs = t_cs or t_const
        act.op(lambda: S.copy(tmp["qb"][:, 0:n], PS[psb][:, 0:n]), reads=[t_PS[psb]], writes=[t_tmp["qb"]])
        pe.group([lambda: PEn.matmul(PS[rotb][:, 0:n], prot[:, :], tmp["qb"][:, 0:n], start=True, stop=True)],
                 reads=[t_tmp["qb"], t_const], writes=[t_PS[rotb]])
        dve.op(lambda: V.tensor_tensor(tmp["t1"][:, 0:n], PS[psb][:, 0:n], cos_ap, ALU.mult),
               reads=[t_PS[psb], t_cs], writes=[t_tmp["t1"]])
        dve.op(lambda: V.tensor_tensor(tmp["t2"][:, 0:n], PS[rotb][:, 0:n], sin_ap, ALU.mult),
               reads=[t_PS[rotb], t_cs], writes=[t_tmp["t2"]])
        dve.op(lambda: V.tensor_tensor(out_ap, tmp["t1"][:, 0:n], tmp["t2"][:, 0:n], ALU.add),
               reads=[t_tmp["t1"], t_tmp["t2"]], writes=[t_out])

    ksum = sb("ksum", [128, 8, 16], F32); t_ksum = T("ksum")
    sq2 = [sb(f"sq{i}", [128, 512], BF16) for i in range(2)]; t_sq2 = [T("sq0"), T("sq1")]
    rstd = sb("rstd", [128, 512], F32); t_rstd = T("rstd")
    rt = {"qb": sb("r_qb", [128, 512], BF16), "t1": sb("r_t1", [128, 512], F32), "t2": sb("r_t2", [128, 512], F32)}
    t_rt = {k: T(k) for k in rt}

    if STAGE >= 2:
        A1.reset(); A2.reset(); A3.reset()
        hpT = A1.take([128, NCH, 1536], BF16); t_hpT = T("hpT")
        xpT = A2.take([128, NCH, 512], F32); t_xpT = T("xpT")
        stage2 = [A3.take([128, D], F32) for i in range(2)]; t_stage2 = [T("st0"), T("st1")]
        cspast = A3.take([128, 3, 2, 512], F32); t_csp = T("csp")
        cntb = [0]
        kf = A3.take([128, 512], F32); t_kf = T("kf")
        kst = [A3.take([128, 512], BF16) for i in range(2)]; t_kst = [T("kst0"), T("kst1")]
        vst = [A3.take([128, 4, 128], BF16) for i in range(2)]; t_vst = [T("vst0"), T("vst1")]
        vtb = A3.take([128, 512], BF16); t_vtb = T("vtb")
        cnt = 0
        for grp in range(2):
            for tl in range(3):
                tok0 = grp * 1536 + tl * 512
                for sub in range(4):
                    k = cnt % 2; cnt += 1
                    load_transpose(x_past[tok0 + sub * 128: tok0 + (sub + 1) * 128, :], 128, xpT, t_xpT, sub * 128,
                                   stage2[k], t_stage2[k], [4, 5, 6, 7])

                def out_fn(ch, rs, tl=tl):
                    dve.op(lambda: V.scalar_tensor_tensor(hpT[:, ch, tl * 512:(tl + 1) * 512], xpT[:, ch, :],
                                                          gall[:, 0, ch:ch + 1], rs, ALU.mult, ALU.mult),
                           reads=[t_xpT, t_rstd, t_const], writes=[t_hpT])
                rmsnorm_fm(xpT, t_xpT, (0, 512), 0, out_fn, sq2, t_sq2, rstd, t_rstd, 3)
            for tl in range(3):
                tok0 = grp * 1536 + tl * 512
                c.dma(sp, cspast[:, tl, :, :], cs_past[:, :, tok0:tok0 + 512], writes=[t_csp])
            ptiles = [(tl * 512, 512) for tl in range(3)]
            pend = None
            psi = 0
            for which in (1, 2):
                for h in range(8):
                    wt, t_w = wload(w_in[:, which * 1024 + h * 128: which * 1024 + (h + 1) * 128], NCH)
                    banks = [[0, 1, 2], [4, 5, 6]][psi]; tb = [3, 7][psi]; psi ^= 1
                    proj(wt, t_w, NCH, lambda ch, c0, n: hpT[:, ch, c0:c0 + n], [t_hpT], banks, tiles=ptiles)
                    if pend is not None:
                        pend()

                    def post(which=which, h=h, banks=banks, tb=tb, grp=grp):
                        for tl in range(3):
                            tok0 = grp * 1536 + tl * 512
                            b = banks[tl]
                            kk = cntb[0] % 2; cntb[0] += 1
                            if which == 1:
                                rope(b, 512, cspast[:, tl, 0, :], cspast[:, tl, 1, :], kf[:, :], t_kf, rt, t_rt, tb, t_cs=t_csp)
                                act.op(lambda kk=kk: S.copy(kst[kk][:, :], kf[:, :]), reads=[t_kf], writes=[t_kst[kk]])
                                c.dma(sp, kt_scr[h, :, tok0:tok0 + 512], kst[kk][:, :], reads=[t_kst[kk]])
                                sb0 = tok0 // 256
                                dve.op(lambda sb0=sb0, h=h: V.tensor_reduce(ksum[:, h, sb0:sb0 + 2],
                                                                            kf[:, :].rearrange("p (a b) -> p a b", a=2), AX.X, ALU.add),
                                       reads=[t_kf], writes=[t_ksum])
                            else:
                                act.op(lambda b=b: S.copy(vtb[:, :], PS[b][:, :]), reads=[t_PS[b]], writes=[t_vtb])
                                pb16 = PS[tb][:, :].bitcast(BF16)
                                pe.group([(lambda s4=s4, pb16=pb16: PEn.transpose(pb16[:, s4 * 128:(s4 + 1) * 128],
                                                                                   vtb[:, s4 * 128:(s4 + 1) * 128], identb[:, :]))
                                          for s4 in range(4)], reads=[t_vtb, t_const], writes=[t_PS[tb]])
                                dve.op(lambda kk=kk, pb16=pb16: V.tensor_copy(vst[kk][:, :, :].rearrange("p a b -> p (a b)"), pb16[:, 0:512]),
                                       reads=[t_PS[tb]], writes=[t_vst[kk]])
                                c.dma(sp, v_scr[h, tok0 // 128: tok0 // 128 + 4, :, :].rearrange("s p d -> p s d"), vst[kk][:, :, :],
                                      reads=[t_vst[kk]])
                    pend = post
            pend()
        c.barrier()

    A3.reset()
    stage2 = [A3.take([128, D], F32) for i in range(2)]; t_stage2 = [T("st0"), T("st1")]
    dve.op(lambda: V.memset(xT[:, :, NT:NTP], 0.0), writes=t_xT)
    dve.op(lambda: V.memset(hT[:, :, NT:NTP], 0.0), writes=[t_hT])
    for sub in range(8):
        load_transpose(x_own[sub * 128:(sub + 1) * 128, :], 128, xT, t_xT, sub * 128, stage2[sub % 2], t_stage2[sub % 2],
                       [4, 5, 6, 7])
    load_transpose(x_c[:, :], NC_, xT, t_xT, NOWN, stage2[0], t_stage2[0], [4, 5, 6, 7])

    def norm_to_hT(gidx):
        for (c0, n) in TOKT:
            def out_fn(ch, rs, c0=c0, n=n):
                dve.op(lambda: V.scalar_tensor_tensor(hT[:, ch, c0:c0 + n], xT[:, ch, c0:c0 + n],
                                                      gall[:, gidx, ch:ch + 1], rs, ALU.mult, ALU.mult),
                       reads=[t_xT[ch], t_rstd, t_const], writes=[t_hT])
            rmsnorm_fm(xT, t_xT, (c0, n), gidx, out_fn, sq2, t_sq2, rstd, t_rstd, 3)

    norm_to_hT(0)
    c.dma(sp, xT_scr.ap(), R1[:, :], reads=t_xT)
    c.barrier()
    A1.reset(); A3.reset()
    QT = A1.take([128, 8, NTP], BF16); t_QT = T("QT")
    YC = A3.take([128, 8, NTP], BF16); t_YC = T("YC")
    kf = A3.take([128, 512], F32); t_kf = T("kf")
    kst = [A3.take([128, 512], BF16) for i in range(2)]; t_kst = [T("kst0"), T("kst1")]
    vst = [A3.take([128, 4, 128], BF16) for i in range(2)]; t_vst = [T("vst0"), T("vst1")]

    hrhs = lambda ch, c0, n: hT[:, ch, c0:c0 + n]
    QsT = sb("QsT", [128, 8, NS], F32); KsT = sb("KsT", [128, 8, NS], F32); t_sm = T("smp")
    Vs = sb("Vs", [NS, 8, 128], F32); Ks = sb("Ks", [NS, 8, 128], F32); Qs = sb("Qs", [NS, 8, 128], F32)
    ost = [A3.take([128, 4, 128], F32) for i in range(2)]; t_ost = [T("ost0"), T("ost1")]
    vf = A3.take([128, 512], F32); t_vf = T("vf")
    ulast = sb("ulast", [128, 8, 2], F32); uslast = sb("uslast", [128, 8, 2], F32); t_ulast = T("ulast")
    stcT = sb("stcT", [128, 8, 2], F32)
    for t_ in range(2):
        c.dma(sp, stcT[:, :, t_], st_conv[t_, :].rearrange("(c p) -> p c", p=128), writes=[t_const],
              allow_slow_non_contiguous=True)
    gcs = A3.take([128, NTP], F32); t_gcs = T("gcs")
    uext = A3.take([128, NOWN + 2], F32); usx = sb("usx", [128, NS + 2], F32); t_u = T("u")
    cva = A3.take([128, NOWN], F32); cvs = sb("cvs", [128, NS], F32); t_cv = T("cv")
    uh = sb("uh", [128, NH], F32); cvh = sb("cvh", [128, NH], F32)
    cnt = 0
    bank_sets = [[0, 1, 2], [4, 5, 6]]
    bsi = 0
    cnto = [0]
    pend = None
    for h in range(8):
        for which in range(3):
            wt, t_w = wload(w_in[:, which * 1024 + h * 128: which * 1024 + (h + 1) * 128], NCH)
            banks = bank_sets[bsi]; bsi ^= 1
            tb = 3 if banks[0] == 0 else 7
            proj(wt, t_w, NCH, hrhs, [t_hT], banks)
            if pend is not None:
                pend()

            def post(h=h, which=which, banks=banks, tb=tb):
                for (c0, n), b in zip(TOKT, banks):
                    cosap, sinap = csown[:, 0, c0:c0 + n], csown[:, 1, c0:c0 + n]
                    if which == 0:
                        rope(b, n, cosap, sinap, QT[:, h, c0:c0 + n], t_QT, rt, t_rt, tb)
                        if n == NC_:
                            dve.op(lambda h=h: V.tensor_tensor(QsT[:, h, :], rt["t1"][:, 0:NS], rt["t2"][:, 0:NS], ALU.add),
                                   reads=[t_rt["t1"], t_rt["t2"]], writes=[t_sm])
                            pe.group([lambda h=h: PEn.transpose(PS[tb][0:NS, 0:128], QsT[:, h, :], ident[:, :])],
                                     reads=[t_sm, t_const], writes=[t_PS[tb]])
                            dve.op(lambda h=h: V.tensor_copy(Qs[:, h, :], PS[tb][0:NS, 0:128]), reads=[t_PS[tb]], writes=[t_sm])
                    elif which == 1:
                        rope(b, n, cosap, sinap, kf[:, 0:n], t_kf, rt, t_rt, tb)
                        if n == 512:
                            kk = cnto[0] % 2; cnto[0] += 1
                            act.op(lambda kk=kk: S.copy(kst[kk][:, :], kf[:, :]), reads=[t_kf], writes=[t_kst[kk]])
                            c.dma(sp, kt_scr[h, :, NPAST + c0:NPAST + c0 + 512], kst[kk][:, :], reads=[t_kst[kk]])
                            sb0 = 12 + c0 // 256
                            dve.op(lambda sb0=sb0, h=h: V.tensor_reduce(ksum[:, h, sb0:sb0 + 2],
                                                                        kf[:, :].rearrange("p (a b) -> p a b", a=2), AX.X, ALU.add),
                                   reads=[t_kf], writes=[t_ksum])
                            pe.group([(lambda s4=s4: PEn.transpose(PS[tb][:, s4 * 128:(s4 + 1) * 128], kf[:, s4 * 128:(s4 + 1) * 128],
                                                                    ident[:, :])) for s4 in range(4)],
                                     reads=[t_kf, t_const], writes=[t_PS[tb]])
                            kk = cnto[0] % 2; cnto[0] += 1
                            evac(ost[kk][:, :, :].rearrange("p a b -> p (a b)"), PS[tb][:, :], [t_PS[tb]], [t_ost[kk]])
                            c.dma(sp, k_own[c0:c0 + 512, h, :].rearrange("(s p) d -> p s d", p=128), ost[kk][:, :, :], reads=[t_ost[kk]])
                        else:
                            dve.op(lambda h=h: V.tensor_copy(KsT[:, h, :], kf[:, 0:NS]), reads=[t_kf], writes=[t_sm])
                            pe.group([lambda h=h: PEn.transpose(PS[tb][0:NS, 0:128], KsT[:, h, :], ident[:, :])],
                                     reads=[t_sm, t_const], writes=[t_PS[tb]])
                            dve.op(lambda h=h: V.tensor_copy(Ks[:, h, :], PS[tb][0:NS, 0:128]), reads=[t_PS[tb]], writes=[t_sm])
                    else:
                        evac(vf[:, 0:n], PS[b][:, 0:n], [t_PS[b]], [t_vf])
                        if n == 512:
                            pe.group([(lambda s4=s4: PEn.transpose(PS[tb][:, s4 * 128:(s4 + 1) * 128], vf[:, s4 * 128:(s4 + 1) * 128],
                                                                    ident[:, :])) for s4 in range(4)],
                                     reads=[t_vf, t_const], writes=[t_PS[tb]])
                            kk = cnto[0] % 2; cnto[0] += 1
                            evac(ost[kk][:, :, :].rearrange("p a b -> p (a b)"), PS[tb][:, :], [t_PS[tb]], [t_ost[kk]])
                            c.dma(sp, v_own[c0:c0 + 512, h, :].rearrange("(s p) d -> p s d", p=128), ost[kk][:, :, :], reads=[t_ost[kk]])
                            dve.op(lambda kk=kk: V.tensor_copy(vst[kk][:, :, :].rearrange("p a b -> p (a b)"), PS[tb][:, :]),
                                   reads=[t_PS[tb]], writes=[t_vst[kk]])
                            st = (NPAST + c0) // 128
                            c.dma(sp, v_scr[h, st:st + 4, :, :].rearrange("s p d -> p s d"), vst[kk][:, :, :], reads=[t_vst[kk]])
                        else:
                            pe.group([lambda: PEn.transpose(PS[tb][0:NS, 0:128], vf[:, 0:NS], ident[:, :])],
                                     reads=[t_vf, t_const], writes=[t_PS[tb]])
                            dve.op(lambda h=h: V.tensor_copy(Vs[:, h, :], PS[tb][0:NS, 0:128]), reads=[t_PS[tb]], writes=[t_sm])

            pend = post
    pend()
    c.dma(sp, k_smp[:, :, :], Ks[:, :, :], reads=[t_sm])
    c.dma(sp, v_smp[:, :, :], Vs[:, :, :], reads=[t_sm])

    for cc in range(8):
        wts = [wload(w_in[:, 3072 + which * 1024 + cc * 128: 3072 + which * 1024 + (cc + 1) * 128], NCH) for which in (1, 2, 0)]
        banks = bank_sets[bsi]; bsi ^= 1
        proj(wts[0][0], wts[0][1], NCH, hrhs, [t_hT], banks)
        for (c0, n), b in zip(TOKT, banks):
            evac(gcs[:, c0:c0 + n], PS[b][:, 0:n], [t_PS[b]], [t_gcs])
        banks = bank_sets[bsi]; bsi ^= 1
        proj(wts[1][0], wts[1][1], NCH, hrhs, [t_hT], banks)
        for (c0, n), b in zip(TOKT, banks):
            if n == 512:
                dve.op(lambda c0=c0, b=b: V.tensor_tensor(uext[:, 2 + c0:2 + c0 + 512], PS[b][:, 0:512], gcs[:, c0:c0 + 512], ALU.mult),
                       reads=[t_PS[b], t_gcs], writes=[t_u])
            else:
                dve.op(lambda b=b: V.tensor_tensor(usx[:, 2:2 + NS], PS[b][:, 0:NS], gcs[:, NOWN:NOWN + NS], ALU.mult),
                       reads=[t_PS[b], t_gcs], writes=[t_u])
                dve.op(lambda b=b: V.tensor_tensor(uext[:, 0:2], PS[b][:, NC_ - 2:NC_], gcs[:, NT - 2:NT], ALU.mult),
                       reads=[t_PS[b], t_gcs], writes=[t_u])
                dve.op(lambda b=b: V.tensor_tensor(uh[:, 0:NH], PS[b][:, NS:NC_], gcs[:, NOWN + NS:NT], ALU.mult),
                       reads=[t_PS[b], t_gcs], writes=[t_u])
                dve.op(lambda cc=cc: V.tensor_copy(usx[:, 0:2], stcT[:, cc, :]), reads=[t_const], writes=[t_u])
        dve.op(lambda cc=cc: V.tensor_copy(ulast[:, cc, :], uext[:, NOWN:NOWN + 2]), reads=[t_u], writes=[t_ulast])
        dve.op(lambda cc=cc: V.tensor_copy(uslast[:, cc, :], usx[:, NS:NS + 2]), reads=[t_u], writes=[t_ulast])
        for (ux, cv, n) in ((uext, cva, NOWN), (usx, cvs, NS), (uh, cvh, NH - 2)):
            dve.op(lambda ux=ux, cv=cv, n=n, cc=cc: V.tensor_scalar(cv[:, 0:n], ux[:, 0:n], convw[:, cc, 0:1], None, ALU.mult),
                   reads=[t_u, t_const], writes=[t_cv])
            for j in (1, 2):
                dve.op(lambda ux=ux, cv=cv, n=n, cc=cc, j=j: V.scalar_tensor_tensor(cv[:, 0:n], ux[:, j:j + n], convw[:, cc, j:j + 1],
                                                                                 cv[:, 0:n], ALU.mult, ALU.add),
                       reads=[t_u, t_cv, t_const], writes=[t_cv])
        banks = bank_sets[bsi]; bsi ^= 1
        proj(wts[2][0], wts[2][1], NCH, hrhs, [t_hT], banks)
        for (c0, n), b in zip(TOKT, banks):
            if n == 512:
                dve.op(lambda c0=c0, b=b, cc=cc: V.tensor_tensor(YC[:, cc, c0:c0 + 512], PS[b][:, 0:512], cva[:, c0:c0 + 512], ALU.mult),
                       reads=[t_PS[b], t_cv], writes=[t_YC])
            else:
                dve.op(lambda b=b, cc=cc: V.tensor_tensor(YC[:, cc, NOWN:NOWN + NS], PS[b][:, 0:NS], cvs[:, 0:NS], ALU.mult),
                       reads=[t_PS[b], t_cv], writes=[t_YC])
                dve.op(lambda b=b, cc=cc: V.tensor_tensor(YC[:, cc, NOWN + NS + 2:NT], PS[b][:, NS + 2:NC_], cvh[:, 0:NH - 2], ALU.mult),
                       reads=[t_PS[b], t_cv], writes=[t_YC])
                dve.op(lambda cc=cc: V.memset(YC[:, cc, NOWN + NS:NOWN + NS + 2], 0.0), writes=[t_YC])
    for t_ in range(2):
        c.dma(sp, conv_p[t_, :].rearrange("(c p) -> p c", p=128), ulast[:, :, t_], reads=[t_ulast], allow_slow_non_contiguous=True)
        c.dma(sp, conv_s[t_, :].rearrange("(c p) -> p c", p=128), uslast[:, :, t_], reads=[t_ulast], allow_slow_non_contiguous=True)
    c.barrier()

    AT = hT
    t_AT = T("AT")
    if STAGE >= 3:
        kmT = sb("kmT", [128, 8, 16], BF16); t_km = T("km")
        dve.op(lambda: V.tensor_scalar(kmT[:, :, :], ksum[:, :, :], 1.0 / 256.0, None, ALU.mult), reads=[t_ksum], writes=[t_km])
        BT = A1.take([16, 8, NTP], BF16); t_BT = T("BT")
        g2 = sb("g2", [128, 8, 16], F32); top8 = sb("top8", [128, 8, 8], F32); thr = sb("thr", [128, 8], F32)
        selb = sb("selb", [128, 8, 16], BF16); t_g = T("g")
        qtiles = [(i * 128, 128, addc[:, i, :]) for i in range(8)] + [(NOWN + NS, NH, addh[:, :])]
        for (q0, qn, adc) in qtiles:
            gb = 3
            for h in range(8):
                pe.group([lambda h=h: PEn.matmul(PS[gb][0:qn, h * 16:(h + 1) * 16], QT[:, h, q0:q0 + qn], kmT[:, h, :], start=True, stop=True)],
                         reads=[t_QT, t_km], writes=[t_PS[gb]])
            dve.op(lambda: V.tensor_tensor(g2[0:qn, :, :], PS[gb][0:qn, 0:128].rearrange("p (a b) -> p a b", a=8),
                                           adc.unsqueeze(1).broadcast_to([qn, 8, 16]), ALU.add),
                   reads=[t_PS[gb], t_const], writes=[t_g])
            for h in range(8):
                dve.op(lambda h=h: V.max(top8[0:qn, h, :], g2[0:qn, h, :]), reads=[t_g], writes=[t_g])
            dve.op(lambda: V.tensor_scalar(thr[0:qn, :], top8[0:qn, :, 3], -1e29, None, ALU.max), reads=[t_g], writes=[t_g])
            dve.op(lambda: V.tensor_tensor(g2[0:qn, :, :], g2[0:qn, :, :], thr[0:qn, :].unsqueeze(2).broadcast_to([qn, 8, 16]), ALU.is_ge),
                   reads=[t_g], writes=[t_g])
            dve.op(lambda: V.tensor_scalar(selb[0:qn, :, :], g2[0:qn, :, :], -NEGB, NEGB, ALU.mult, ALU.add), reads=[t_g], writes=[t_g])
            pb16 = PS[7][:, :].bitcast(BF16)
            pe.group([(lambda h=h: PEn.transpose(pb16[0:16, h * 128:h * 128 + qn], selb[0:qn, h, :], identb[0:qn, 0:qn])) for h in range(8)],
                     reads=[t_g, t_const], writes=[t_PS[7]])
            dve.op(lambda: V.tensor_copy(BT[:, :, q0:q0 + qn], pb16[0:16, :].rearrange("p (a b) -> p a b", a=8)[:, :, 0:qn]),
                   reads=[t_PS[7]], writes=[t_BT])

        KT2 = [A1.take([128, 4096], BF16) for _ in range(2)]; t_KT2 = [T("KTa"), T("KTb")]
        VV2 = [A1.take([128, 32, 128], BF16) for _ in range(2)]; t_VV2 = [T("Va"), T("Vb")]
        A3.off = (8 * NTP * 2 + 31) // 32 * 32
        PT = [A3.take([128, 512], BF16) for i in range(3)]; t_PT = [T(f"PT{i}") for i in range(3)]
        rden = A3.take([128, 512], F32); t_rden = T("rden")
        pti = 0
        sctr = [0]
        km_state = {"n": 0}
        if STAGE >= 5:
            kpg = [A3.take([128, 2, 1024], BF16) for _ in range(2)]; t_kpg = [T("kpg0"), T("kpg1")]
            ptb = sb("ptb", [128, 128], I32); ptf = sb("ptf", [128, 128], F32); idxP = sb("idxP", [128, 128], I32); t_pt = T("pt")
            pio = sb("pio", [128, 2], F32); iota64 = sb("iota64", [128, 64], F32)
            c.dma(sp, pio[:, :], c_pio.ap(), writes=[t_const])
            c.dma(sp, iota64[:, :], c_iota64.ap(), writes=[t_const])
            c.dma(sp, ptb[:, :], ptab.ap().partition_broadcast(128), writes=[t_pt])
            dve.op(lambda: V.tensor_copy(ptf[:, :], ptb[:, :]), reads=[t_pt], writes=[t_pt])
            dve.op(lambda: V.tensor_scalar(idxP[:, :], ptf[:, :], 128.0, pio[:, 0:1], ALU.mult, ALU.add), reads=[t_pt, t_const], writes=[t_pt])
            ck_rows = cache_k.ap().rearrange("n t h d -> (n t) (h d)")
            KMB = 6

            def kmean_step():
                n = km_state["n"]
                if n >= 64:
                    return
                km_state["n"] = n + 1
                kb = n % 2
                for pg in range(2):
                    c.swdma(4 + kb, kpg[kb][:, pg, :], ck_rows, reads=[t_pt], writes=[t_kpg[kb]],
                            indirect=idxP[:, 2 * n + pg:2 * n + pg + 1])
                for hh in range(8):
                    col = hh * 64 + n
                    pe.group([lambda hh=hh, col=col: PEn.matmul(PS[KMB][:, col:col + 1], kpg[kb][:, 0, hh * 128:(hh + 1) * 128], onesb[:, 0:1],
                                                                start=True, stop=False),
                              lambda hh=hh, col=col: PEn.matmul(PS[KMB][:, col:col + 1], kpg[kb][:, 1, hh * 128:(hh + 1) * 128], onesb[:, 0:1],
                                                                start=False, stop=True)],
                             reads=[t_kpg[kb], t_const], writes=[t_PS[KMB]])
        ktc = [0]
        for h in range(8):
            kb = h % 2
            c.dma(sp, KT2[kb][:, :], kt_scr[h, :, :], writes=[t_KT2[kb]])
            c.dma(sp, VV2[kb][:, :, :], v_scr[h, :, :, :].rearrange("s p d -> p s d"), writes=[t_VV2[kb]])
            for qi, (q0, qn, nkt) in enumerate([(0, 512, 28), (512, 512, 32), (NOWN + NS, NH, 24)]):
                ob, db = 4, 5

                def emit_S(kt, h=h, kb=kb, qi=qi, q0=q0, qn=qn):
                    sbk = sctr[0] % 3; sctr[0] += 1
                    fns = [lambda: PEn.matmul(PS[sbk][:, 0:qn], KT2[kb][:, kt * 128:(kt + 1) * 128], QT[:, h, q0:q0 + qn],
                                              start=True, stop=False)]
                    extra = None
                    if qi == 2:
                        extra = hb[:, kt, :]
                    elif kt >= 24:
                        off = (kt - 24) * 128 - q0
                        if off >= 0:
                            extra = cm[:, off // 128, :]
                    fns.append(lambda: PEn.matmul(PS[sbk][:, 0:qn], eall[:, kt // 2, :], BT[:, h, q0:q0 + qn],
                                                  start=False, stop=(extra is None)))
                    if extra is not None:
                        fns.append(lambda: PEn.matmul(PS[sbk][:, 0:qn], identb[:, :], extra, start=False, stop=True))
                    pe.group(fns, reads=[t_KT2[kb], t_QT, t_BT, t_const], writes=[t_PS[sbk]])
                    return sbk

                LOOK = 2
                sb_of = {}
                for kt in range(min(LOOK, nkt)):
                    sb_of[kt] = emit_S(kt)
                for kt in range(nkt):
                    if kt + LOOK < nkt:
                        sb_of[kt + LOOK] = emit_S(kt + LOOK)
                    sbk = sb_of.pop(kt)
                    p = pti; pti = (pti + 1) % 3
                    act.op(lambda sbk=sbk, p=p: S.activation(PT[p][:, 0:qn], PS[sbk][:, 0:qn], AF.Exp, scale=SCALE),
                           reads=[t_PS[sbk]], writes=[t_PT[p]])
                    pe.deps([t_PT[p], t_VV2[kb], t_const], [t_PS[ob], t_PS[db]] if kt == 0 else [])
                    PEn.matmul(PS[ob][:, 0:qn], VV2[kb][:, kt, :], PT[p][:, 0:qn], start=(kt == 0), stop=(kt == nkt - 1))
                    ins = PEn.matmul(PS[db][:, 0:qn], onesb[:, :], PT[p][:, 0:qn], start=(kt == 0), stop=(kt == nkt - 1))
                    pe.count += 1
                    ins.then_inc(pe.sem, 1)
                    tok = (pe, pe.count)
                    pe.commit(tok, [t_PT[p], t_VV2[kb]], [])
                    t_PS[ob].w = tok; t_PS[db].w = tok
                    if kt == 0:
                        t_PS[ob].rd = []; t_PS[db].rd = []
                    ktc[0] += 1
                    if STAGE >= 5 and ktc[0] % 7 == 0:
                        kmean_step()
                dve.op(lambda: V.reciprocal(rden[:, 0:qn], PS[db][:, 0:qn]), reads=[t_PS[db]], writes=[t_rden])
                dve.op(lambda h=h: V.tensor_tensor(AT[:, h, q0:q0 + qn], PS[ob][:, 0:qn], rden[:, 0:qn], ALU.mult),
                       reads=[t_PS[ob], t_rden], writes=[t_AT])
        if STAGE >= 5:
            while km_state["n"] < 64:
                kmean_step()
    else:
        dve.op(lambda: V.memset(AT[:, 0:8, :], 0.0), writes=[t_AT])
    if STAGE >= 5:
        c.barrier()
        A1.reset()
        Kg = [A1.take([128, 24, 128], F32) for _ in range(2)]; t_Kg = [T("Kg0"), T("Kg1")]
        Vg = [A1.take([128, 24, 128], F32) for _ in range(2)]; t_Vg = [T("Vg0"), T("Vg1")]
        A3.off = (8 * NTP * 2 + 31) // 32 * 32
        kmS = A3.take([128, 8, 64], F32); t_kmS = T("kmS")
        G = A3.take([128, 32, 64], F32); top8s = A3.take([128, 32, 8], F32); idxs = A3.take([128, 32, 8], U32); t_G = T("G")
        nf = A3.take([128, 32, 3], F32)
        OH = A3.take([128, 12, 64], F32); PR = A3.take([128, 12, 64], F32); physf = A3.take([128, 12, 2], F32); t_OH = T("OH")
        idxG = A3.take([128, 8, 24], I32); t_idxG = T("idxG")
        Qrep = [A3.take([128, 128], F32) for _ in range(2)]; t_Qrep = [T("qr0"), T("qr1")]
        qb = A3.take([128, 4, 128], F32); t_qb = T("qb")
        junk = A3.take([128, 128], F32); t_junk = T("junk")
        Ssm = A3.take([128, 24], F32); Psm = A3.take([128, 24], F32); t_S = T("Ssm"); t_P = T("Psm")
        Sown = A3.take([4, 4], F32); Pown = A3.take([4, 4], F32)
        den = A3.take([128, 4], F32); rd = A3.take([128, 4], F32); t_den = T("den")
        onesf = sb("onesf", [128, 128], F32)
        esel = sb("esel", [4, 4, 128], F32); smask = sb("smask", [4, 4], F32)
        dve.op(lambda: V.memset(onesf[:, :], 1.0), writes=[t_const])
        c.dma(sp, esel[:, :, :], c_esel.ap(), writes=[t_const])
        c.dma(sp, smask[:, :], c_smask.ap(), writes=[t_const])
        ck_hrows = cache_k.ap().rearrange("n t h d -> (n t h) d")
        cv_hrows = cache_v.ap().rearrange("n t h d -> (n t h) d")
        dve.op(lambda: V.tensor_scalar(kmS[:, :, :].rearrange("p a b -> p (a b)"), PS[KMB][:, :], 1.0 / 256.0, None, ALU.mult),
               reads=[t_PS[KMB]], writes=[t_kmS])
        for hq in range(32):
            h, q = hq // 4, hq % 4
            bank = 4 + hq // 8; col = (hq % 8) * 64
            k2 = hq % 2
            dve.op(lambda h=h, q=q, k2=k2: V.tensor_copy(Qrep[k2][:, :], QsT[:, h, q:q + 1].broadcast_to([128, 128])),
                   reads=[t_sm], writes=[t_Qrep[k2]])
            pe.group([lambda h=h, k2=k2, bank=bank, col=col: PEn.matmul(PS[bank][:, col:col + 64], Qrep[k2][:, :], kmS[:, h, :],
                                                                        start=True, stop=True)],
                     reads=[t_Qrep[k2], t_kmS], writes=[t_PS[bank]])
        for b4 in range(4):
            dve.op(lambda b4=b4: V.tensor_copy(G[:, b4 * 8:(b4 + 1) * 8, :].rearrange("p a b -> p (a b)"), PS[4 + b4][:, :]),
                   reads=[t_PS[4 + b4]], writes=[t_G])
        for hq in range(32):
            dve.op(lambda hq=hq: V.max(top8s[:, hq, :], G[:, hq, :]), reads=[t_G], writes=[t_G])
            dve.op(lambda hq=hq: V.max_index(idxs[:, hq, :], top8s[:, hq, :], G[:, hq, :]), reads=[t_G], writes=[t_G])
        dve.op(lambda: V.tensor_copy(nf[:, :, :], idxs[:, :, 0:3]), reads=[t_G], writes=[t_G])
        for h in range(8):
            dve.op(lambda h=h: V.tensor_tensor(OH[:, :, :], iota64[:, :].unsqueeze(1).broadcast_to([128, 12, 64]),
                                               nf[:, h * 4:(h + 1) * 4, :].rearrange("p a b -> p (a b)").unsqueeze(2).broadcast_to([128, 12, 64]),
                                               ALU.is_equal), reads=[t_G, t_const], writes=[t_OH])
            for pgi in range(2):
                dve.op(lambda pgi=pgi: V.tensor_tensor(PR[:, :, :], OH[:, :, :],
                                                       ptf[:, :].rearrange("p (n g) -> p n g", g=2)[:, :, pgi].unsqueeze(1).broadcast_to([128, 12, 64]),
                                                       ALU.mult), reads=[t_OH, t_pt], writes=[t_OH])
                dve.op(lambda pgi=pgi: V.tensor_reduce(physf[:, :, pgi], PR[:, :, :], AX.X, ALU.add), reads=[t_OH], writes=[t_OH])
            dve.op(lambda h=h: V.tensor_scalar(idxG[:, h, :], physf[:, :, :].rearrange("p a b -> p (a b)"), 1024.0, pio[:, 1:2], ALU.mult, ALU.add),
                   reads=[t_OH, t_const], writes=[t_idxG])
        for h in range(8):
            gb = h % 2
            for s_ in range(24):
                c.swdma(6 + gb, Kg[gb][:, s_, :], ck_hrows, reads=[t_idxG], writes=[t_Kg[gb]], indirect=idxG[:, h, s_:s_ + 1],
                        element_offset=h * 128)
            for s_ in range(24):
                c.swdma(8 + gb, Vg[gb][:, s_, :], cv_hrows, reads=[t_idxG], writes=[t_Vg[gb]], indirect=idxG[:, h, s_:s_ + 1],
                        element_offset=h * 128)
            pe.group([(lambda q=q, h=h: PEn.matmul(PS[1][:, q * 128:(q + 1) * 128], esel[0:4, q, :], Qs[0:4, h, :], start=True, stop=True))
                      for q in range(4)], reads=[t_sm, t_const], writes=[t_PS[1]])
            act.op(lambda: S.copy(qb[:, :, :].rearrange("p a b -> p (a b)"), PS[1][:, :]), reads=[t_PS[1]], writes=[t_qb])
            for s_ in range(24):
                dve.op(lambda s_=s_, gb=gb: V.scalar_tensor_tensor(junk[:, :], Kg[gb][:, s_, :], SCALE, qb[:, s_ // 6, :], ALU.mult, ALU.mult,
                                                                   accum_out=Ssm[:, s_:s_ + 1]),
                       reads=[t_Kg[gb], t_qb], writes=[t_junk, t_S])
            for q in range(4):
                dve.op(lambda q=q, h=h: V.scalar_tensor_tensor(junk[0:4, :], Ks[0:4, h, :], SCALE, qb[0:4, q, :], ALU.mult, ALU.mult,
                                                               accum_out=Sown[0:4, q:q + 1]),
                       reads=[t_sm, t_qb], writes=[t_junk, t_S])
            dve.op(lambda: V.tensor_tensor(Sown[0:4, :], Sown[0:4, :], smask[0:4, :], ALU.add), reads=[t_S, t_const], writes=[t_S])
            act.op(lambda: S.activation(Psm[:, :], Ssm[:, :], AF.Exp), reads=[t_S], writes=[t_P])
            act.op(lambda: S.activation(Pown[0:4, :], Sown[0:4, :], AF.Exp), reads=[t_S], writes=[t_P])
            pe.group([lambda: PEn.matmul(PS[2][:, 0:24], onesf[:, :], Psm[:, :], start=True, stop=True)],
                     reads=[t_P, t_const], writes=[t_PS[2]])
            dve.op(lambda: V.tensor_reduce(den[:, 0:4], PS[2][:, 0:24].rearrange("p (a b) -> p a b", a=4), AX.X, ALU.add),
                   reads=[t_PS[2]], writes=[t_den])
            pe.group([lambda: PEn.matmul(PS[2][:, 32:36], onesf[0:4, :], Pown[0:4, :], start=True, stop=True)],
                     reads=[t_P, t_const], writes=[t_PS[2]])
            dve.op(lambda: V.tensor_tensor(den[:, 0:4], den[:, 0:4], PS[2][:, 32:36], ALU.add), reads=[t_PS[2], t_den], writes=[t_den])
            dve.op(lambda: V.reciprocal(rd[:, 0:4], den[:, 0:4]), reads=[t_den], writes=[t_den])
            for q in range(4):
                fns = [(lambda q=q, i=i, gb=gb: PEn.matmul(PS[3][:, q:q + 1], Vg[gb][:, q * 6 + i, :], Psm[:, q * 6 + i:q * 6 + i + 1],
                                                           start=(i == 0), stop=False)) for i in range(6)]
                fns.append(lambda q=q, h=h: PEn.matmul(PS[3][:, q:q + 1], Vs[0:4, h, :], Pown[0:4, q:q + 1], start=False, stop=True))
                pe.group(fns, reads=[t_Vg[gb], t_P, t_sm], writes=[t_PS[3]])
            dve.op(lambda h=h: V.tensor_tensor(AT[:, h, NOWN:NOWN + NS], PS[3][:, 0:4], rd[:, 0:4], ALU.mult),
                   reads=[t_PS[3], t_den], writes=[t_AT])
    else:
        dve.op(lambda: V.memset(AT[:, 0:8, NOWN:NOWN + NS], 0.0), writes=[t_AT])
    c.barrier()

    c.dma(sp, R1[:, :], xT_scr.ap(), writes=t_xT)

    def resid_add(banks, oc, scale_ap=None):
        for (c0, n), b in zip(TOKT, banks):
            if scale_ap is None:
                dve.op(lambda c0=c0, n=n, b=b: V.tensor_tensor(xT[:, oc, c0:c0 + n], PS[b][:, 0:n], xT[:, oc, c0:c0 + n], ALU.add),
                       reads=[t_PS[b], t_xT[oc]], writes=[t_xT[oc]])
            else:
                dve.op(lambda c0=c0, n=n, b=b: V.scalar_tensor_tensor(xT[:, oc, c0:c0 + n], PS[b][:, 0:n], scale_ap, xT[:, oc, c0:c0 + n],
                                                                      ALU.mult, ALU.add),
                       reads=[t_PS[b], t_xT[oc], t_const], writes=[t_xT[oc]])

    for oc in range(NCH):
        wt, t_w = wload(w_o[:, oc * 128:(oc + 1) * 128], NCH)
        banks = bank_sets[bsi]; bsi ^= 1
        proj(wt, t_w, NCH, lambda ch, c0, n: (AT[:, ch, c0:c0 + n] if ch < 8 else YC[:, ch - 8, c0:c0 + n]), [t_AT, t_YC], banks)
        resid_add(banks, oc)
    c.barrier()

    def ffn(layer, gidx):
        norm_to_hT(gidx)
        A3.reset()
        NG = 4
        per = NFT // NG
        aT = A3.take([128, per, NTP], BF16); t_aT = T("aT")
        sg = [A3.take([128, 512], F32) for i in range(2)]; t_sg = [T("sg0"), T("sg1")]
        sgi = 0
        for g in range(NG):
            for fi in range(per):
                ft = g * per + fi
                wg, t_wg = wload(w_gate[layer, :, ft * 128:(ft + 1) * 128], NCH)
                wu, t_wu = wload(w_up[layer, :, ft * 128:(ft + 1) * 128], NCH)
                proj(wg, t_wg, NCH, hrhs, [t_hT], [0, 1, 2])
                proj(wu, t_wu, NCH, hrhs, [t_hT], [4, 5, 6])
                for (c0, n), bg, bu in zip(TOKT, [0, 1, 2], [4, 5, 6]):
                    k = sgi; sgi ^= 1
                    act.op(lambda k=k, bg=bg, n=n: S.activation(sg[k][:, 0:n], PS[bg][:, 0:n], AF.Silu), reads=[t_PS[bg]], writes=[t_sg[k]])
                    dve.op(lambda k=k, bu=bu, n=n, c0=c0, fi=fi: V.tensor_tensor(aT[:, fi, c0:c0 + n], PS[bu][:, 0:n], sg[k][:, 0:n], ALU.mult),
                           reads=[t_PS[bu], t_sg[k]], writes=[t_aT])
            for oc in range(NCH):
                wd, t_wd = wload(w_down[layer, g * per * 128:(g + 1) * per * 128, oc * 128:(oc + 1) * 128], per)
                banks = bank_sets[bsi_box[0]]; bsi_box[0] ^= 1
                proj(wd, t_wd, per, lambda ch, c0, n: aT[:, ch, c0:c0 + n], [t_aT], banks)
                resid_add(banks, oc)
        c.barrier()

    bsi_box = [bsi]
    if STAGE >= 4:
        ffn(0, 1)

    if STAGE >= 4:
        A3.reset()
        AR = A3
        dT = hT; t_dT = t_hT
        stpT = AR.take([128, NCH, NPH], F32)
        h1l = AR.take([128, NCH, NPH], F32); h1s = AR.take([128, NCH, NPH], F32); t_h1l = T("h1l")
        stp_sb = AR.take([NPH, D], F32); t_stp = T("stp")
        c.dma(sp, stp_sb[:, :], st_pool[:, :], writes=[t_stp])
        for q4 in range(4):
            pe.group([(lambda k=k: PEn.transpose(PS[7][:, k * 128:k * 128 + NPH], stp_sb[0:NPH, (q4 * 4 + k) * 128:(q4 * 4 + k + 1) * 128],
                                                  ident[0:NPH, 0:NPH])) for k in range(4)], reads=[t_stp, t_const], writes=[t_PS[7]])
            dve.op(lambda q4=q4: V.tensor_copy(stpT[:, q4 * 4:q4 * 4 + 4, :], PS[7][:, :].rearrange("p (k n) -> p k n", k=4)[:, :, 0:NPH]),
                   reads=[t_PS[7]], writes=[t_stp])
        rs_all = AR.take([128, NTP], F32); t_rsall = T("rsall")
        for (c0, n) in TOKT:
            rmsnorm_fm(xT, t_xT, (c0, n), 2, lambda ch, rs: None, sq2, t_sq2, rstd, t_rstd, 3)
            dve.op(lambda c0=c0, n=n: V.tensor_copy(rs_all[:, c0:c0 + n], rstd[:, 0:n]), reads=[t_rstd], writes=[t_rsall])
        hx = AR.take([128, NPH + NOWN], F32); hxs = AR.take([128, NPH + NS], F32); t_hx = T("hx")
        sA = AR.take([128, NPH + NOWN], F32); sB = AR.take([128, NPH + NOWN], F32); t_s = T("s")
        for ch in range(NCH):
            g = ch // 4
            gsc = gall[:, 2, ch:ch + 1]
            dve.op(lambda ch=ch: V.scalar_tensor_tensor(hx[:, NPH:NPH + NOWN], xT[:, ch, 0:NOWN], gsc, rs_all[:, 0:NOWN], ALU.mult, ALU.mult),
                   reads=[t_xT[ch], t_rsall, t_const], writes=[t_hx])
            dve.op(lambda ch=ch: V.scalar_tensor_tensor(hx[:, 0:NPH], xT[:, ch, NT - NPH:NT], gsc, rs_all[:, NT - NPH:NT], ALU.mult, ALU.mult),
                   reads=[t_xT[ch], t_rsall, t_const], writes=[t_hx])
            dve.op(lambda: V.tensor_tensor(hx[:, 0:NPH], hx[:, 0:NPH], hv[:, 0:NPH], ALU.mult), reads=[t_hx, t_const], writes=[t_hx])
            dve.op(lambda ch=ch: V.scalar_tensor_tensor(hxs[:, NPH:NPH + NS], xT[:, ch, NOWN:NOWN + NS], gsc, rs_all[:, NOWN:NOWN + NS], ALU.mult, ALU.mult),
                   reads=[t_xT[ch], t_rsall, t_const], writes=[t_hx])
            dve.op(lambda ch=ch: V.tensor_copy(hxs[:, 0:NPH], stpT[:, ch, :]), reads=[t_stp], writes=[t_hx])
            dve.op(lambda ch=ch: V.tensor_copy(h1l[:, ch, :], hx[:, NOWN:NOWN + NPH]), reads=[t_hx], writes=[t_h1l])
            dve.op(lambda ch=ch: V.tensor_copy(h1s[:, ch, :], hxs[:, NS:NS + NPH]), reads=[t_hx], writes=[t_h1l])
            w = 2 ** (g + 1)
            for (src, L, c0out, nout) in ((hx, NPH + NOWN, 0, NOWN), (hxs, NPH + NS, NOWN, NS)):
                cur = src
                sh = 1
                bufs = [sA, sB]
                bi = 0
                while sh < w:
                    nxt = bufs[bi]; bi ^= 1
                    dve.op(lambda cur=cur, nxt=nxt, sh=sh, L=L: V.tensor_tensor(nxt[:, sh:L], cur[:, sh:L], cur[:, 0:L - sh], ALU.add),
                           reads=[t_hx, t_s], writes=[t_s])
                    if sh > 1 or True:
                        dve.op(lambda cur=cur, nxt=nxt, sh=sh: V.tensor_copy(nxt[:, 0:sh], cur[:, 0:sh]), reads=[t_hx, t_s], writes=[t_s])
                    cur = nxt
                    sh *= 2
                dve.op(lambda cur=cur, src=src, c0out=c0out, nout=nout, ch=ch, w=w: V.scalar_tensor_tensor(
                    dT[:, ch, c0out:c0out + nout], cur[:, NPH:NPH + nout], 1.0 / w, src[:, NPH:NPH + nout], ALU.mult, ALU.subtract),
                    reads=[t_s, t_hx], writes=[t_dT])
                if nout == NOWN:
                    fx = sA if cur is sB else sB
                    dve.op(lambda cur=cur, fx=fx, g=g: V.tensor_tensor(fx[:, 0:NPH], cur[:, NPH:2 * NPH], invc[:, g, 0:NPH], ALU.mult),
                           reads=[t_s, t_const], writes=[t_s])
                    dve.op(lambda fx=fx, src=src, ch=ch: V.tensor_tensor(dT[:, ch, 0:NPH], fx[:, 0:NPH], src[:, NPH:2 * NPH], ALU.subtract),
                           reads=[t_s, t_hx], writes=[t_dT])
            dve.op(lambda ch=ch: V.memset(dT[:, ch, NOWN + NS:NT], 0.0), writes=[t_dT])
        for (src, dst) in ((h1l, pool_p), (h1s, pool_s)):
            po = stp_sb; t_po = t_stp
            for q4 in range(4):
                pe.group([(lambda k=k: PEn.transpose(PS[7][0:NPH, k * 128:(k + 1) * 128], src[:, q4 * 4 + k, :], ident[:, :])) for k in range(4)],
                         reads=[t_h1l, t_const], writes=[t_PS[7]])
                dve.op(lambda q4=q4, po=po: V.tensor_copy(po[:, q4 * 512:(q4 + 1) * 512], PS[7][0:NPH, :]), reads=[t_PS[7]], writes=[t_po])
            c.dma(sp, dst[:, :], po[:, :], reads=[t_po])
        for g in range(4):
            for oc4 in range(4):
                oc = g * 4 + oc4
                wt, t_w = wload(w_pool[g, :, oc4 * 128:(oc4 + 1) * 128], 4)
                banks = bank_sets[bsi_box[0]]; bsi_box[0] ^= 1
                proj(wt, t_w, 4, lambda ch, c0, n, g=g: dT[:, g * 4 + ch, c0:c0 + n], [t_dT], banks)
                resid_add(banks, oc, psc[:, oc:oc + 1])
        c.barrier()
        ffn(1, 3)

    A3.reset()
    stage2 = [A3.take([128, D], F32) for i in range(2)]; t_stage2 = [T("st0"), T("st1")]
    yT = A3.take([128, NCH, 128], F32); t_yT = T("yT")
    for (c0, n) in TOKT:
        rmsnorm_fm(xT, t_xT, (c0, n), 4, lambda ch, rs: None, sq2, t_sq2, rstd, t_rstd, 3)
        nsub = 4 if n == 512 else 1
        for sub in range(nsub):
            nr = 128 if n == 512 else NS
            k2 = sub % 2
            for ch in range(NCH):
                dve.op(lambda ch=ch, sub=sub, nr=nr, c0=c0: V.scalar_tensor_tensor(
                    yT[:, ch, 0:nr], xT[:, ch, c0 + sub * 128:c0 + sub * 128 + nr], gall[:, 4, ch:ch + 1],
                    rstd[:, sub * 128:sub * 128 + nr], ALU.mult, ALU.mult),
                    reads=[t_xT[ch], t_rstd, t_const], writes=[t_yT])
            for q4 in range(4):
                b = 4 + q4
                pe.group([(lambda k=k: PEn.transpose(PS[b][0:nr, k * 128:(k + 1) * 128], yT[:, q4 * 4 + k, 0:nr], ident[:, :]))
                          for k in range(4)], reads=[t_yT, t_const], writes=[t_PS[b]])
                evac(stage2[k2][0:nr, q4 * 512:(q4 + 1) * 512], PS[b][0:nr, :], [t_PS[b]], [t_stage2[k2]])
            if n == 512:
                c.dma(sp, y_own[c0 + sub * 128:c0 + (sub + 1) * 128, :], stage2[k2][:, :], reads=[t_stage2[k2]])
            else:
                c.dma(sp, y_smp[:, :], stage2[k2][0:NS, :], reads=[t_stage2[k2]])
    c.finish()
    return nc


_CACHE = {}


def _consts(j):
    bf = ml_dtypes.bfloat16
    half = 64
    inv = (np.float32(10000.0) ** (-np.arange(half, dtype=np.float32) / np.float32(half))).astype(np.float32)

    def cs(pos):
        ang = pos.astype(np.float32)[:, None] * inv[None, :]
        co, si = np.cos(ang).astype(np.float32), np.sin(ang).astype(np.float32)
        return np.stack([np.concatenate([co, co], 1).T, np.concatenate([si, si], 1).T], axis=1)
    T0 = j * 1024
    pos = np.zeros(NTP, np.float32)
    pos[0:NOWN] = T0 + np.arange(NOWN)
    pos[NOWN:NOWN + NS] = 16384 + np.arange(NS)
    pos[NOWN + NS:NT] = np.maximum(T0 - NH + np.arange(NH), 0)
    d = {}
    d["cs_own"] = np.ascontiguousarray(cs(pos))
    d["cs_past"] = np.ascontiguousarray(cs(np.arange(NPAST, dtype=np.float32)))
    d["c_ident"] = np.eye(128, dtype=np.float32)
    pr = np.zeros((128, 128), np.float32)
    for dd in range(64):
        pr[dd + 64, dd] = -1.0
        pr[dd, dd + 64] = 1.0
    d["c_prot"] = pr.astype(bf)
    ea = np.zeros((16, 16, 128), np.float32)
    for r in range(16):
        ea[r, r, :] = 1.0
    d["c_eall"] = ea.astype(bf)
    k = np.arange(128)[:, None]; q = np.arange(512)[None, :]
    cmm = np.stack([np.where(off * 128 + k <= q, 0.0, NEGB) for off in range(4)], axis=1)
    d["c_cm"] = cmm.astype(bf)
    hbm = np.zeros((128, 24, NH), np.float32)
    for kt in range(24):
        for qq in range(NH):
            p = T0 - NH + qq
            s = kt * 128 + np.arange(128)
            vis = (s <= p) if j > 0 else np.full(128, kt == 0)
            hbm[:, kt, qq] = np.where(vis, 0.0, NEGB)
    d["c_hb"] = hbm.astype(bf)
    adc = np.zeros((128, 8, 16), np.float32)
    for qt in range(8):
        ob = (qt * 128) // 256
        for sbk in range(16):
            if sbk < 12:
                v = 0.0 if sbk < 4 * j else -2e30
            else:
                o = sbk - 12
                v = 0.0 if o < ob else (1e30 if o == ob else -2e30)
            adc[:, qt, sbk] = v
    d["c_addc"] = adc
    adh = np.full((NH, 16), -2e30, np.float32)
    if j > 0:
        adh[:, :4 * j - 1] = 0.0
        adh[:, 4 * j - 1] = 1e30
    else:
        adh[:, 0] = 1e30
    d["c_addh"] = adh
    d["c_hv"] = np.full((128, NH), 1.0 if j > 0 else 0.0, np.float32)
    ic = np.zeros((128, 4, 16), np.float32)
    for g in range(4):
        w = 2 ** (g + 1)
        for i in range(16):
            ic[:, g, i] = 1.0 / min(T0 + i + 1, w)
    d["c_invc"] = ic
    sm = np.zeros((4, 4), np.float32)
    for kk in range(4):
        for qq in range(4):
            sm[kk, qq] = 0.0 if kk <= qq else NEGB
    d["c_smask"] = sm
    es = np.zeros((4, 4, 128), np.float32)
    for r in range(4):
        es[r, r, :] = 1.0
    d["c_esel"] = es
    d["c_pio"] = np.stack([np.arange(128, dtype=np.float32), 8.0 * np.arange(128, dtype=np.float32)], axis=1)
    d["c_iota64"] = np.ascontiguousarray(np.broadcast_to(np.arange(64, dtype=np.float32)[None, :], (128, 64)))
    return d


def kernel(x_prompt, x_sample, cache_k, cache_v, page_table, state_conv, state_pool, norm_mix, norm_ffn, norm_final,
           w_in, conv_w, w_o, w_pool, pool_scale, w_gate, w_up, w_down):
    f = lambda a: np.ascontiguousarray(np.asarray(a, dtype=np.float32))
    x_prompt, x_sample = f(x_prompt), f(x_sample)
    if "nc" not in _CACHE:
        _CACHE["nc"] = build_program()
    nc = _CACHE["nc"]
    fm = lambda v: np.asarray(v, np.float32).reshape(16, 128).T
    g_all = np.ascontiguousarray(np.stack([fm(norm_mix[0]), fm(norm_ffn[0]), fm(norm_mix[1]), fm(norm_ffn[1]), fm(norm_final)], axis=1))
    shared = {
        "g_all": g_all, "w_in": f(w_in)[0], "conv_w": np.ascontiguousarray(f(conv_w)[0].reshape(3, 8, 128).transpose(2, 1, 0)),
        "w_o": f(w_o)[0], "w_pool": f(w_pool)[0], "pscale": np.ascontiguousarray(fm(pool_scale[0])),
        "w_gate": f(w_gate), "w_up": f(w_up), "w_down": f(w_down),
    }
    if STAGE >= 5:
        shared["cache_k"] = f(cache_k)[0]; shared["cache_v"] = f(cache_v)[0]
    in_maps = []
    for c in range(8):
        b, j = c // 4, c % 4
        T0 = j * 1024
        m = dict(shared)
        m["x_own"] = np.ascontiguousarray(x_prompt[b, T0:T0 + 1024])
        halo = np.zeros((NH, D), np.float32)
        if j > 0:
            halo[:] = x_prompt[b, T0 - NH:T0]
        m["x_c"] = np.ascontiguousarray(np.concatenate([x_sample[c], halo], axis=0))
        m["x_past"] = np.ascontiguousarray(x_prompt[b, 0:NPAST])
        m["ptab"] = np.ascontiguousarray(np.asarray(page_table, np.int32)[c:c + 1])
        m["st_conv"] = np.ascontiguousarray(f(state_conv)[0, c])
        m["st_pool"] = np.ascontiguousarray(f(state_pool)[0, c])
        m.update(_consts(j))
        in_maps.append(m)
    res = run_bass_kernel_spmd(nc, in_maps, core_ids=list(range(8)))
    R = res.results
    y_prompt = np.stack([np.concatenate([R[b * 4 + j]["y_own"] for j in range(4)], 0) for b in range(2)], 0)
    y_sample = np.stack([R[c]["y_smp"] for c in range(8)], 0)
    k_prompt = np.stack([np.concatenate([R[b * 4 + j]["k_own"] for j in range(4)], 0) for b in range(2)], 0)[None]
    v_prompt = np.stack([np.concatenate([R[b * 4 + j]["v_own"] for j in range(4)], 0) for b in range(2)], 0)[None]
    k_sample = np.stack([R[c]["k_smp"] for c in range(8)], 0)[None]
    v_sample = np.stack([R[c]["v_smp"] for c in range(8)], 0)[None]
    conv_prompt = np.stack([R[3]["conv_p"], R[7]["conv_p"]], 0)[None]
    conv_sample = np.stack([R[c]["conv_s"] for c in range(8)], 0)[None]
    pool_prompt = np.stack([R[3]["pool_p"], R[7]["pool_p"]], 0)[None]
    pool_sample = np.stack([R[c]["pool_s"] for c in range(8)], 0)[None]
    outs = (y_prompt, y_sample, k_prompt, v_prompt, k_sample, v_sample, conv_prompt, conv_sample, pool_prompt, pool_sample)
    return tuple(np.ascontiguousarray(o.astype(np.float32)) for o in outs)
```
